# Optimizing a Trainium2 kernel written in Bass

```python
import jax, jax.numpy as jnp
from jax import lax
import numpy as np

D_MODEL = 1024
BATCH = 2
SEQ = 8192
DEPTH = 1
DEC_BATCH = 32
DEC_SEQ = 8
PAST_LEN = 16384
PAGE_SIZE = 128

FOX_HEADS = 8
FOX_HEAD_DIM = 64
GLA_HEADS = 4
GLA_KEY_DIM = 64
GLA_VAL_DIM = 128
GLA_GATE_RANK = 16
GLA_GATE_TAU = 16.0
GLA_CHUNK = 64
Q_BLOCK = 128
FFN_DIM = 2752
NORM_EPS = 1e-6
PROJ_SIZES = (
    FOX_HEADS * FOX_HEAD_DIM,
    FOX_HEADS * FOX_HEAD_DIM,
    FOX_HEADS * FOX_HEAD_DIM,
    FOX_HEADS,
    GLA_HEADS * GLA_KEY_DIM,
    GLA_HEADS * GLA_KEY_DIM,
    GLA_HEADS * GLA_VAL_DIM,
    GLA_HEADS * GLA_VAL_DIM,
    GLA_GATE_RANK,
)
PROJ_WIDTH = sum(PROJ_SIZES)
MIX_WIDTH = FOX_HEADS * FOX_HEAD_DIM + GLA_HEADS * GLA_VAL_DIM

kernel_name = 'fox_gla_parallel_heads_macaron_step'


def rmsnorm(x, g):
    x32 = x.astype(jnp.float32)
    y = x32 * lax.rsqrt(jnp.mean(x32 * x32, axis=-1, keepdims=True) + NORM_EPS)
    return (y * g.astype(jnp.float32)).astype(x.dtype)


def swiglu(x, w_gu, w_down):
    gate, up = jnp.split(x @ w_gu, 2, axis=-1)
    return (jax.nn.silu(gate) * up) @ w_down


def macaron_half(x, g_pre, w_gu, w_down, g_post):
    return x + 0.5 * rmsnorm(swiglu(rmsnorm(x, g_pre), w_gu, w_down), g_post)


def project(u, w_in, b_f, w_a2, b_a):
    B, T, _ = u.shape
    z = u @ w_in
    offs = np.cumsum(PROJ_SIZES)[:-1].tolist()
    fq, fk, fv, ff, gq, gk, gv, gg, ga = jnp.split(z, offs, axis=-1)
    fq = fq.reshape(B, T, FOX_HEADS, FOX_HEAD_DIM)
    fk = fk.reshape(B, T, FOX_HEADS, FOX_HEAD_DIM)
    fv = fv.reshape(B, T, FOX_HEADS, FOX_HEAD_DIM)
    logf = jax.nn.log_sigmoid((ff + b_f).astype(jnp.float32))
    gq = gq.astype(jnp.float32).reshape(B, T, GLA_HEADS, GLA_KEY_DIM) * (GLA_KEY_DIM ** -0.5)
    gk = gk.astype(jnp.float32).reshape(B, T, GLA_HEADS, GLA_KEY_DIM)
    gv = gv.astype(jnp.float32).reshape(B, T, GLA_HEADS, GLA_VAL_DIM)
    gg = gg.reshape(B, T, GLA_HEADS, GLA_VAL_DIM)
    log_a = jax.nn.log_sigmoid((ga @ w_a2 + b_a).astype(jnp.float32)) / GLA_GATE_TAU
    log_a = log_a.reshape(B, T, GLA_HEADS, GLA_KEY_DIM)
    return fq, fk, fv, logf, gq, gk, gv, gg, log_a


def merge(fox_o, gla_o, gla_g, g_gla, w_o):
    B, T = fox_o.shape[:2]
    gla_o = rmsnorm(gla_o, g_gla).astype(fox_o.dtype) * jax.nn.silu(gla_g)
    mix = jnp.concatenate([fox_o.reshape(B, T, -1), gla_o.reshape(B, T, -1)], axis=-1)
    return mix @ w_o


def fox_prompt(q, k, v, logf):
    B, T, H, D = q.shape
    c = jnp.swapaxes(jnp.cumsum(logf, axis=1), 1, 2)
    k_pos = jnp.arange(T)
    scale = D ** -0.5

    def block(i):
        start = i * Q_BLOCK
        qb = lax.dynamic_slice_in_dim(q, start, Q_BLOCK, axis=1)
        cb = lax.dynamic_slice_in_dim(c, start, Q_BLOCK, axis=2)
        s = (jnp.einsum('bqhd,bkhd->bhqk', qb, k).astype(jnp.float32) * scale
             + cb[..., :, None] - c[..., None, :])
        mask = (start + jnp.arange(Q_BLOCK))[:, None] >= k_pos[None, :]
        p = jax.nn.softmax(jnp.where(mask, s, -jnp.inf), axis=-1).astype(v.dtype)
        return jnp.einsum('bhqk,bkhd->bqhd', p, v)

    o = lax.map(block, jnp.arange(T // Q_BLOCK))
    return jnp.moveaxis(o, 0, 1).reshape(B, T, H, D)


def fox_sample(q, k, v, logf, pool_k, pool_v, pool_logf, page_table):
    DB, S, H, D = q.shape
    P = page_table.shape[1] * PAGE_SIZE
    pk = pool_k[page_table].reshape(DB, P, H, D)
    pv = pool_v[page_table].reshape(DB, P, H, D)
    plf = pool_logf[page_table].reshape(DB, P, H).astype(jnp.float32)
    rev = lax.cumsum(plf, axis=1, reverse=True)
    after = jnp.concatenate([rev[:, 1:], jnp.zeros_like(rev[:, :1])], axis=1)
    cn = jnp.swapaxes(jnp.cumsum(logf, axis=1), 1, 2)
    scale = D ** -0.5
    s_past = (jnp.einsum('bqhd,bkhd->bhqk', q, pk).astype(jnp.float32) * scale
              + cn[..., :, None] + jnp.swapaxes(after, 1, 2)[:, :, None, :])
    s_new = (jnp.einsum('bqhd,bkhd->bhqk', q, k).astype(jnp.float32) * scale
             + cn[..., :, None] - cn[..., None, :])
    causal = jnp.tril(jnp.ones((S, S), dtype=bool))
    s_new = jnp.where(causal, s_new, -jnp.inf)
    p = jax.nn.softmax(jnp.concatenate([s_past, s_new], axis=-1), axis=-1).astype(v.dtype)
    return (jnp.einsum('bhqk,bkhd->bqhd', p[..., :P], pv)
            + jnp.einsum('bhqk,bkhd->bqhd', p[..., P:], v))


def gla_chunk(q, k, v, log_a, s0):
    L = q.shape[1]
    b = jnp.cumsum(log_a, axis=1)
    o_inter = jnp.einsum('blhk,bhkv->blhv', q * jnp.exp(b), s0)
    causal = jnp.tril(jnp.ones((L, L), dtype=bool))
    diff = b[:, :, None] - b[:, None, :]
    decay = jnp.exp(jnp.where(causal[None, :, :, None, None], diff, -jnp.inf))
    attn = jnp.einsum('bthk,bshk,btshk->bhts', q, k, decay)
    o_intra = jnp.einsum('bhts,bshv->bthv', attn, v)
    b_last = b[:, -1]
    k_dec = k * jnp.exp(b_last[:, None] - b)
    s_new = jnp.exp(b_last)[..., None] * s0 + jnp.einsum('blhk,blhv->bhkv', k_dec, v)
    return o_inter + o_intra, s_new


def gla_prompt(q, k, v, log_a):
    B, T, H, K = q.shape
    nc = T // GLA_CHUNK

    def to_chunks(a):
        return jnp.moveaxis(a.reshape((B, nc, GLA_CHUNK) + a.shape[2:]), 1, 0)

    def step(s, xs):
        o, s_new = gla_chunk(xs[0], xs[1], xs[2], xs[3], s)
        return s_new, o

    s0 = jnp.zeros((B, H, K, GLA_VAL_DIM), jnp.float32)
    s_fin, o = lax.scan(step, s0, (to_chunks(q), to_chunks(k), to_chunks(v), to_chunks(log_a)))
    return jnp.moveaxis(o, 0, 1).reshape(B, T, H, GLA_VAL_DIM), s_fin


def setup_inputs(seed: int = 0) -> dict:
    key = jax.random.key(seed)
    ks = jax.random.split(key, 32)
    n_pages = PAST_LEN // PAGE_SIZE
    used = DEC_BATCH * n_pages
    n_pool = used + max(1, used // 4)

    def w(k, shape, fan_in):
        return jax.random.normal(k, (DEPTH,) + shape, jnp.float32) * (fan_in ** -0.5)

    def gain(k, n):
        return 1.0 + 0.05 * jax.random.normal(k, (DEPTH, n), jnp.float32)

    x_prompt = jax.random.normal(ks[0], (BATCH, SEQ, D_MODEL), jnp.float32)
    x_sample = jax.random.normal(ks[1], (DEC_BATCH, DEC_SEQ, D_MODEL), jnp.float32)
    cache_k = jax.random.normal(ks[2], (DEPTH, n_pool, PAGE_SIZE, FOX_HEADS, FOX_HEAD_DIM), jnp.float32)
    cache_v = jax.random.normal(ks[3], (DEPTH, n_pool, PAGE_SIZE, FOX_HEADS, FOX_HEAD_DIM), jnp.float32)
    logf_logit = float(np.log(PAST_LEN)) + 2.0 + 0.5 * jax.random.normal(
        ks[4], (DEPTH, n_pool, PAGE_SIZE, FOX_HEADS), jnp.float32)
    cache_logf = jax.nn.log_sigmoid(logf_logit)
    state_gla = 0.5 * jax.random.normal(ks[5], (DEPTH, DEC_BATCH, GLA_HEADS, GLA_KEY_DIM, GLA_VAL_DIM), jnp.float32)
    page_table = jax.random.permutation(ks[6], n_pool)[:used].reshape(DEC_BATCH, n_pages).astype(jnp.int32)
    return {
        'x_prompt': x_prompt,
        'x_sample': x_sample,
        'cache_k': cache_k,
        'cache_v': cache_v,
        'cache_logf': cache_logf,
        'state_gla': state_gla,
        'page_table': page_table,
        'ffn1_norm_pre': gain(ks[7], D_MODEL),
        'ffn1_w_gu': w(ks[8], (D_MODEL, 2 * FFN_DIM), D_MODEL),
        'ffn1_w_down': w(ks[9], (FFN_DIM, D_MODEL), FFN_DIM),
        'ffn1_norm_post': gain(ks[10], D_MODEL),
        'mix_norm_pre': gain(ks[11], D_MODEL),
        'w_in': w(ks[12], (D_MODEL, PROJ_WIDTH), D_MODEL),
        'b_forget': jax.random.uniform(ks[13], (DEPTH, FOX_HEADS), jnp.float32, 1.0, 5.0),
        'w_gate_up': w(ks[14], (GLA_GATE_RANK, GLA_HEADS * GLA_KEY_DIM), GLA_GATE_RANK),
        'b_gate': 0.1 * jax.random.normal(ks[15], (DEPTH, GLA_HEADS * GLA_KEY_DIM), jnp.float32),
        'gla_norm': gain(ks[16], GLA_VAL_DIM),
        'w_out': w(ks[17], (MIX_WIDTH, D_MODEL), MIX_WIDTH),
        'mix_norm_post': gain(ks[18], D_MODEL),
        'ffn2_norm_pre': gain(ks[19], D_MODEL),
        'ffn2_w_gu': w(ks[20], (D_MODEL, 2 * FFN_DIM), D_MODEL),
        'ffn2_w_down': w(ks[21], (FFN_DIM, D_MODEL), FFN_DIM),
        'ffn2_norm_post': gain(ks[22], D_MODEL),
    }


def reference(x_prompt, x_sample, cache_k, cache_v, cache_logf, state_gla, page_table,
              ffn1_norm_pre, ffn1_w_gu, ffn1_w_down, ffn1_norm_post,
              mix_norm_pre, w_in, b_forget, w_gate_up, b_gate, gla_norm, w_out, mix_norm_post,
              ffn2_norm_pre, ffn2_w_gu, ffn2_w_down, ffn2_norm_post):
    hp = x_prompt
    hs = x_sample
    kp_l, vp_l, fp_l, sp_l = [], [], [], []
    ks_l, vs_l, fs_l, ss_l = [], [], [], []
    for l in range(DEPTH):
        hp = macaron_half(hp, ffn1_norm_pre[l], ffn1_w_gu[l], ffn1_w_down[l], ffn1_norm_post[l])
        fq, fk, fv, lf, gq, gk, gv, gg, la = project(rmsnorm(hp, mix_norm_pre[l]), w_in[l],
                                                     b_forget[l], w_gate_up[l], b_gate[l])
        fo = fox_prompt(fq, fk, fv, lf)
        go, s_p = gla_prompt(gq, gk, gv, la)
        hp = hp + rmsnorm(merge(fo, go, gg, gla_norm[l], w_out[l]), mix_norm_post[l])
        hp = macaron_half(hp, ffn2_norm_pre[l], ffn2_w_gu[l], ffn2_w_down[l], ffn2_norm_post[l])
        kp_l.append(fk.astype(cache_k.dtype))
        vp_l.append(fv.astype(cache_v.dtype))
        fp_l.append(lf.astype(cache_logf.dtype))
        sp_l.append(s_p.astype(state_gla.dtype))
        hs = macaron_half(hs, ffn1_norm_pre[l], ffn1_w_gu[l], ffn1_w_down[l], ffn1_norm_post[l])
        fq, fk, fv, lf, gq, gk, gv, gg, la = project(rmsnorm(hs, mix_norm_pre[l]), w_in[l],
                                                     b_forget[l], w_gate_up[l], b_gate[l])
        fo = fox_sample(fq, fk, fv, lf, cache_k[l], cache_v[l], cache_logf[l], page_table)
        go, s_s = gla_chunk(gq, gk, gv, la, state_gla[l].astype(jnp.float32))
        hs = hs + rmsnorm(merge(fo, go, gg, gla_norm[l], w_out[l]), mix_norm_post[l])
        hs = macaron_half(hs, ffn2_norm_pre[l], ffn2_w_gu[l], ffn2_w_down[l], ffn2_norm_post[l])
        ks_l.append(fk.astype(cache_k.dtype))
        vs_l.append(fv.astype(cache_v.dtype))
        fs_l.append(lf.astype(cache_logf.dtype))
        ss_l.append(s_s.astype(state_gla.dtype))
    new_k_prompt = jnp.stack(kp_l, 0)
    new_v_prompt = jnp.stack(vp_l, 0)
    new_logf_prompt = jnp.stack(fp_l, 0)
    new_gla_prompt = jnp.stack(sp_l, 0)
    new_k_sample = jnp.stack(ks_l, 0)
    new_v_sample = jnp.stack(vs_l, 0)
    new_logf_sample = jnp.stack(fs_l, 0)
    new_gla_sample = jnp.stack(ss_l, 0)
    return (hp, hs, new_k_prompt, new_v_prompt, new_logf_prompt, new_gla_prompt,
            new_k_sample, new_v_sample, new_logf_sample, new_gla_sample)
```

```python
import os
from contextlib import ExitStack
import numpy as np
import concourse.bass as bass
import concourse.mybir as mybir
from concourse.bass_utils import run_bass_kernel_spmd

F32 = mybir.dt.float32
BF16 = mybir.dt.bfloat16
I32 = mybir.dt.int32
AF = mybir.ActivationFunctionType
ALU = mybir.AluOpType

D = 1024
FF = 2752
PW = 3096
NEG = -30000.0
STAGE = int(os.environ.get("MK_STAGE", "9"))
NPOOL = int(os.environ.get("MK_POOL", "5120"))

COMPUTE = ("act", "pool", "dve", "pe")
QUEUES = ("sync", "act", "pool", "dve", "pe")


class Op:
    __slots__ = ("eng", "fn", "reads", "writes", "dma", "deps", "sig", "idx", "inc")

    def __init__(self, eng, fn, reads, writes, dma, inc):
        self.eng, self.fn, self.reads, self.writes, self.dma = eng, fn, reads, writes, dma
        self.inc = inc
        self.deps = ()
        self.sig = None


class Sched:
    def __init__(self):
        self.ops = []

    def add(self, eng, fn, reads=(), writes=(), dma=None, inc=None):
        if inc is None:
            inc = 16 if dma is not None else 1
        self.ops.append(Op(eng, fn, tuple(reads), tuple(writes), dma, inc))

    def analyse(self):
        last_w, readers, last_dma = {}, {}, {}
        for i, op in enumerate(self.ops):
            op.idx = i
            deps = set()
            for k in op.reads:
                if k in last_w:
                    deps.add(last_w[k])
            for k in op.writes:
                if k in last_w:
                    deps.add(last_w[k])
                deps.update(readers.get(k, ()))
            if op.dma is not None and op.dma in last_dma:
                deps.add(last_dma[op.dma])
            deps.discard(i)
            for k in op.reads:
                readers.setdefault(k, []).append(i)
            for k in op.writes:
                last_w[k] = i
                readers[k] = []
            if op.dma is not None:
                last_dma[op.dma] = i
            op.deps = deps
        ops = self.ops
        waited = {q: {} for q in QUEUES}
        need = []
        for op in ops:
            best = {}
            for d in op.deps:
                p = ops[d]
                src = ("dma", p.dma) if p.dma is not None else ("eng", p.eng)
                if src == ("eng", "pe") and op.eng == "pe" and op.dma is None:
                    continue
                if d > best.get(src, -1):
                    best[src] = d
            w = waited[op.eng]
            lst = []
            for src, d in best.items():
                if w.get(src, -1) >= d:
                    continue
                w[src] = d
                lst.append((src, d))
            need.append(lst)
            for src, d in lst:
                ops[d].sig = True
        cnt = {}
        for op in ops:
            if op.sig or op.dma is not None:
                src = ("dma", op.dma) if op.dma is not None else ("eng", op.eng)
                cnt[src] = cnt.get(src, 0) + op.inc
                op.sig = cnt[src]
        self.need = need
        self.sources = list(cnt.keys())
        self.final = {}
        for op in ops:
            if op.dma is not None:
                self.final[("dma", op.dma)] = (op.eng, op.sig)

    def emit(self, nc):
        self.analyse()
        ops = self.ops
        with ExitStack() as es:
            sems = {}
            for n, src in enumerate(self.sources):
                sems[src] = es.enter_context(nc.semaphore("s%d" % n))
            block = es.enter_context(nc.Block())

            def section(q):
                def body(eng):
                    for op in ops:
                        if op.eng != q:
                            continue
                        for src, d in self.need[op.idx]:
                            eng.wait_ge(sems[src], ops[d].sig)
                        ins = op.fn(eng)
                        if op.sig:
                            src = ("dma", op.dma) if op.dma is not None else ("eng", op.eng)
                            ins.then_inc(sems[src], op.inc)
                    for src, (qq, val) in self.final.items():
                        if qq == q:
                            eng.wait_ge(sems[src], val)
                return body

            block.sync(section("sync"))
            block.scalar(section("act"))
            block.gpsimd(section("pool"))
            block.vector(section("dve"))
            block.tensor(section("pe"))


def build():
    nc = bass.Bass("TRN2", target_bir_lowering=False)
    S = Sched()
    es = ExitStack()

    def din(name, shape, dt=F32):
        return nc.dram_tensor(name, list(shape), dt, kind="ExternalInput").ap()

    def dout(name, shape, dt=F32):
        return nc.dram_tensor(name, list(shape), dt, kind="ExternalOutput").ap()

    def dscr(name, shape, dt=F32):
        return nc.dram_tensor(name, list(shape), dt)

    def sb(name, shape, dt=F32):
        return es.enter_context(nc.sbuf_tensor("S_" + name, list(shape), dt))

    xp = din("xp", [2048, D])
    xs = din("xs", [32, D])
    W = {}
    for nm, shp in (("w_gu1", [D, 2 * FF]), ("w_dn1", [FF, D]), ("w_in", [D, PW]), ("w_out", [D, D]),
                    ("w_gu2", [D, 2 * FF]), ("w_dn2", [FF, D]), ("w_a2", [16, 256])):
        W[nm] = din(nm, shp)
    G = {}
    for nm in ("g1pre", "g1post", "gmpre", "gmpost", "g2pre", "g2post"):
        G[nm] = din(nm, [128, D])
    bfor_d = din("bfor", [128, 8])
    bgate_d = din("bgate", [64, 4])
    ggla_d = din("ggla", [128, 512])
    ident_d = din("ident", [128, 128])
    tri_d = din("tri", [128, 128])
    ones_d = din("ones", [128, 128])
    rmask_d = din("rmask", [64, 544])
    gmask_d = din("gmask", [128, 512])
    gmask_s_d = din("gmask_s", [32, 128])
    cmask_d = din("cmask", [128, 4, 512])
    wsel_d = din("wsel", [128, 4])
    lmask_d = din("lmask", [16, 16])
    pcol_d = din("pcol", [128, 2])
    ptab_d = din("ptab", [128, 512], I32)
    sgla_d = din("sgla", [4, 4, 64, 128])
    cache_k = din("cache_k", [NPOOL * 128, 512])
    cache_v = din("cache_v", [NPOOL * 128, 512])
    cache_lf = din("cache_lf", [NPOOL * 128, 8])
    smask_new_d = din("smask_new", [64, 4, 32])
    esel_d = din("esel", [8, 64])
    qsel_d = din("qsel", [64, 8])
    bmask_d = din("bmask", [64, 512])
    colm_d = din("colm", [64, 4, 32])
    rowm_d = din("rowm", [32, 4])

    y_p = dout("y_p", [2048, D])
    y_s = dout("y_s", [32, D])
    nk_p = dout("nk_p", [2048, 512])
    nv_p = dout("nv_p", [2048, 512])
    nlf_p = dout("nlf_p", [2048, 8])
    gla_p = dout("gla_p", [64, 512])
    nk_s = dout("nk_s", [32, 512])
    nv_s = dout("nv_s", [32, 512])
    nlf_s = dout("nlf_s", [32, 8])
    gla_s = dout("gla_s", [4, 64, 512])


    NS = int(os.environ.get("MK_NG", "4"))
    kt_in = [dscr("kt_in%d" % i, [8 * 68, 512], BF16) for i in range(NS)]
    ktg = [dscr("ktg%d" % i, [4 * 8 * 68, 512], BF16) for i in range(NS)]
    v_in = [dscr("v_in%d" % i, [8 * 128 * 4, 65], BF16) for i in range(NS)]
    vgt = [dscr("vgt%d" % i, [4 * 8 * 128 * 4, 65], BF16) for i in range(NS)]
    f_in = [dscr("f_in%d" % i, [256, 512]) for i in range(NS)]
    fgt = [dscr("fgt%d" % i, [4 * 256, 512]) for i in range(NS)]
    mrow_d = din("mrow", [1, 2048])

    NW = 4
    wslot = [sb("wslot%d" % i, [128, 4224], BF16) for i in range(NW)]
    xnT = sb("xnT", [128, 8, 544], BF16)
    yacc = sb("yacc", [128, 5, 1024])
    hres = sb("hres", [128, 5, 1024])
    hT = sb("hT", [128, 4, 512], BF16)
    sg = sb("sg", [128, 512])
    tokC = sb("tokC", [128, 1024])
    tokb = sb("tokb", [128, 1024], BF16)
    ss = sb("ss", [128, 8])
    ident_f = sb("ident_f", [128, 128])
    ident_b = sb("ident_b", [128, 128], BF16)
    tri_f = sb("tri_f", [128, 128])
    ones_f = sb("ones_f", [128, 128])
    gains = {nm: sb("G_" + nm, [128, D]) for nm in G}
    bfor = sb("bfor", [128, 8])
    bgate = sb("bgate", [64, 4])
    nbgate = sb("nbgate", [64, 4])
    wa2 = sb("wa2", [16, 256], BF16)
    rmask = sb("rmask", [64, 544])
    ggla = sb("ggla", [128, 512])
    gmask = sb("gmask", [128, 512])
    cmask = sb("cmask", [128, 4, 512], BF16)
    Qext = sb("Qext", [68, 8, 512], BF16)
    KTst = sb("KTst", [68, 512], BF16)
    Vst = sb("Vst", [128, 4, 8, 65], BF16)
    lf = sb("lf", [128, 4, 8])
    lft = sb("lft", [128, 8])
    Cglob = sb("Cglob", [128, 64, 8])
    Crun = sb("Crun", [128, 8])
    Cst = sb("Cst", [128, 4, 8])
    Srun = sb("Srun", [64, 4, 128])
    wsel = sb("wsel", [128, 4])
    biasO = sb("biasO", [128, 4])
    biasT = sb("biasT", [128, 64])
    Pt = [sb("Pt%d" % i, [128, 512], BF16) for i in range(2)]
    Osb = sb("Osb", [65, 512])
    OnT = sb("OnT", [65, 8, 512], BF16)
    gate_tok = sb("gate", [128, 4, 512], BF16)
    gaT = sb("gaT", [16, 512], BF16)
    sp = sb("sp", [64, 512])
    cs = sb("cs", [64, 512])
    eb = sb("eb", [64, 512])
    enb = sb("enb", [64, 512])
    ebl = sb("ebl", [64, 4, 4])
    gqT = sb("gqT", [64, 4, 512], BF16)
    gkT = sb("gkT", [64, 4, 512], BF16)
    kdT = sb("kdT", [64, 4, 512], BF16)
    kd_tok = sb("kd_tok", [128, 4, 4, 64], BF16)
    vg_tok = sb("vg_tok", [128, 4, 512], BF16)
    Sg = sb("Sg", [64, 4, 128])
    Sgb = sb("Sgb", [64, 4, 128], BF16)
    Am = sb("Am", [128, 512], BF16)
    mix_tok = sb("mix_tok", [128, 512], BF16)
    mgT = sb("mgT", [128, 4, 512], BF16)
    otmp = sb("otmp", [128, 512])
    qTs = sb("qTs", [128, 4, 32], BF16)
    kTs = sb("kTs", [128, 4, 32], BF16)
    Qbd = sb("Qbd", [128, 4, 64], BF16)
    Vs_bf = sb("Vs_bf", [32, 512], BF16)
    gq_m = sb("gq_m", [64, 16, 32], BF16)
    OnT_s = sb("OnT_s", [65, 8, 32], BF16)
    mgT_s = sb("mgT_s", [128, 4, 32], BF16)
    PT = sb("PT", [128, 4, 64], BF16)
    dsum = sb("dsum", [64, 40])
    esel = sb("esel", [8, 64], BF16)
    qsel = sb("qsel", [64, 8], BF16)
    smask = sb("smask", [64, 128], BF16)
    colm = sb("colm", [64, 128])
    rowm = sb("rowm", [32, 4])
    gmask_s = sb("gmask_s", [32, 128])
    lfs = sb("lfs", [32, 8])
    ncnT = sb("ncnT", [8, 32], BF16)
    pcol = sb("pcol", [128, 2])

    psF = [es.enter_context(nc.psum_tensor("psF%d" % i, [128, 512], F32)) for i in range(4)]
    psA = [es.enter_context(nc.psum_tensor("psA%d" % i, [128, 512], F32)) for i in range(2)]
    psB = [es.enter_context(nc.psum_tensor("psB%d" % i, [128, 1024], BF16)) for i in range(2)]
    cnt = {"f": 0, "b": 0, "w": 0, "a": 0, "p": 0}

    def nextF():
        i = cnt["f"] % 4
        cnt["f"] += 1
        return psF[i], ("psF", i)

    def nextA():
        i = cnt["a"] % 2
        cnt["a"] += 1
        return psA[i], ("psA", i)

    def nextB():
        i = cnt["b"] % 2
        cnt["b"] += 1
        return psB[i], ("psB", i)

    def nextW():
        i = cnt["w"] % NW
        cnt["w"] += 1
        return wslot[i], ("w", i)

    def nextP():
        i = cnt["p"] % 2
        cnt["p"] += 1
        return Pt[i], ("Pt", i)
    def dma(q, out, in_, reads, writes, key):
        if isinstance(key, str) and key.startswith("c_"):
            key = "c_" + q
        S.add(q, lambda e: e.dma_start(out=out, in_=in_), reads, writes, dma=key)

    def mm(out, lhsT, rhs, st, sp, reads, writes):
        S.add("pe", lambda e: e.matmul(out, lhsT=lhsT, rhs=rhs, start=st, stop=sp), reads, writes)

    def tp(out, in_, idn, reads, writes):
        S.add("pe", lambda e: e.transpose(out=out, in_=in_, identity=idn), reads, writes)

    def act(out, in_, func, reads, writes, **kw):
        S.add("act", lambda e: e.activation(out=out, in_=in_, func=func, **kw), reads, writes)

    def vcopy(out, in_, reads, writes, eng="dve"):
        S.add(eng, lambda e: e.tensor_copy(out=out, in_=in_), reads, writes)

    def vtt(out, a, b, op, reads, writes, eng="dve"):
        S.add(eng, lambda e: e.tensor_tensor(out=out, in0=a, in1=b, op=op), reads, writes)

    def vts(out, a, s1, s2, op0, op1, reads, writes, eng="dve"):
        S.add(eng, lambda e: e.tensor_scalar(out=out, in0=a, scalar1=s1, scalar2=s2, op0=op0, op1=op1), reads, writes)

    def vstt(out, a, s, b, op0, op1, reads, writes):
        S.add("dve", lambda e: e.scalar_tensor_tensor(out=out, in0=a, scalar=s, in1=b, op0=op0, op1=op1), reads, writes)

    def vrecip(out, in_, reads, writes):
        S.add("dve", lambda e: e.reciprocal(out=out, in_=in_), reads, writes)

    def memset(ap, v, writes, eng="dve"):
        S.add(eng, lambda e: e.memset(ap, v), (), writes)


    dma("sync", ident_f[:], ident_d, (), ["ident_f"], "c_identf")
    dma("pool", ident_b[:], ident_d, (), ["ident_b"], "c_identb")
    dma("sync", tri_f[:], tri_d, (), ["tri_f"], "c_tri")
    dma("sync", ones_f[:], ones_d, (), ["ones_f"], "c_ones")
    for nm in G:
        dma("sync", gains[nm][:], G[nm], (), ["G_" + nm], "c_" + nm)
    dma("sync", bfor[:], bfor_d, (), ["bfor"], "c_bfor")
    dma("sync", bgate[:], bgate_d, (), ["bgate"], "c_bgate")
    dma("pool", wa2[:], W["w_a2"], (), ["wa2"], "c_wa2")
    dma("sync", rmask[:], rmask_d, (), ["rmask"], "c_rmask")
    dma("sync", ggla[:], ggla_d, (), ["ggla"], "c_ggla")
    dma("sync", gmask[:], gmask_d, (), ["gmask"], "c_gmask")
    dma("pool", cmask[:], cmask_d, (), ["cmask"], "c_cmask")
    dma("pool", esel[:], esel_d, (), ["esel"], "c_esel")
    dma("pool", qsel[:], qsel_d, (), ["qsel"], "c_qsel")
    dma("pool", smask[:], smask_new_d.rearrange("p b t -> p (b t)"), (), ["smask"], "c_smask")
    dma("sync", colm[:], colm_d.rearrange("p b t -> p (b t)"), (), ["colm"], "c_colm")
    dma("sync", rowm[:], rowm_d, (), ["rowm"], "c_rowm")
    dma("sync", gmask_s[:], gmask_s_d, (), ["gmask_s"], "c_gmask_s")
    dma("sync", pcol[:], pcol_d, (), ["pcol"], "c_pcol")
    memset(OnT_s[:], 1.0, [("OnT_s", b) for b in range(4)])
    vts(nbgate[:], bgate[:], -1.0, None, ALU.mult, ALU.bypass, ["bgate"], ["nbgate"])
    memset(KTst[:], 1.0, ["KTst"])
    memset(hT[0:1, 0, :], 0.0, [("hT", 0)])
    memset(hT[0:1, 1, :], NEG, [("hT", 1)])
    dma("sync", KTst[67:68, :], hT[0:1, 0, :], [("hT", 0)], ["KTst"], "c_k67")
    for h in range(8):
        dma("sync", Qext[67:68, h, :], hT[0:1, 1, :], [("hT", 1)], [("Qext", h)], "c_q67")
    memset(Srun[:], 0.0, ["Srun"])
    dma("sync", wsel[:], wsel_d, (), ["wsel"], "c_wsel")
    memset(Vst[:], 1.0, ["Vst"])
    memset(Crun[:], 0.0, ["Crun"])
    memset(Sg[:], 0.0, ["Sg"])
    memset(Sgb[:], 0.0, ["Sgb"])

    Wb = {}
    for nm, shp in (("w_gu1", [D, 2 * FF]), ("w_dn1", [FF, D]), ("w_in", [D, PW]), ("w_out", [D, D]),
                    ("w_gu2", [D, 2 * FF]), ("w_dn2", [FF, D])):
        Wb[nm] = dscr(nm + "_b", shp, BF16)
        for r0 in range(0, shp[0], 128):
            r1 = min(shp[0], r0 + 128)
            dma("pool", Wb[nm][r0:r1, :], W[nm][r0:r1, :], (), [("Wb", nm)], ("wcast", nm))
    WB = {nm: Wb[nm].ap() for nm in Wb}

    NG = NS
    groups = []
    for gi in range(NG):
        blks = []
        for j in range(4):
            blks.append(dict(kind="p", row0=gi * 512 + j * 128, n=128, yi=j, col0=j * 128, tile=gi, kb=j))
        if gi == NG - 1:
            blks.append(dict(kind="s", row0=0, n=32, yi=4, col0=512, tile=None, kb=0))
        groups.append(blks)

    def subtiles(gi):
        st = [dict(col0=0, tn=512, blks=groups[gi][0:4], kind="p", tile=gi, li=0)]
        if gi == NG - 1:
            st.append(dict(col0=512, tn=32, blks=groups[gi][4:5], kind="s", tile=None, li=1))
        return st

    def src_rows(b, prm, smp):
        return (prm if b["kind"] == "p" else smp)[b["row0"]:b["row0"] + b["n"], :]

    def xk(st):
        return [("xnT", st["col0"])] if st["tn"] == 32 else [("xnT", st["col0"] + j * 128) for j in range(4)]
    def rstd(src, n, srcks, col, dim=D, junk=None):
        act(tokC[0:n, 0:dim], src, AF.Square, list(srcks), ["tokC", ("ss", col)], accum_out=ss[0:n, col:col + 1])
        act(ss[0:n, col:col + 1], ss[0:n, col:col + 1], AF.Sqrt, [("ss", col)], [("ss", col)], scale=1.0 / dim, bias=1e-6)
        vrecip(ss[0:n, col:col + 1], ss[0:n, col:col + 1], [("ss", col)], [("ss", col)])

    def norm_to_T(src, n, srcks, gname, col0):
        rstd(src, n, srcks, 0)
        vstt(tokb[0:n, :], src, ss[0:n, 0:1], gains[gname][0:n, :], ALU.mult, ALU.mult,
             list(srcks) + [("ss", 0), "G_" + gname], ["tokb"])
        pb, pk = nextB()
        for c in range(8):
            tp(pb[:, c * 128:c * 128 + n], tokb[0:n, c * 128:(c + 1) * 128], ident_b[0:n, 0:n], ["tokb", "ident_b"], [pk])
        vcopy(xnT[:, :, col0:col0 + n], pb[:, :].rearrange("p (c t) -> p c t", c=8)[:, :, 0:n], [pk], [("xnT", col0)])

    def wload_cols(wname, lo, hi):
        ws, wk = nextW()
        n = hi - lo
        v = ws[:, 0:8 * n].rearrange("p (k n) -> p k n", k=8)
        dma("sync", v, WB[wname].rearrange("(k p) n -> p k n", p=128)[:, :, lo:hi], [("Wb", wname)], [wk], wk)
        return v, wk

    def ffn(gi, wgu, wdn):
        sts = subtiles(gi)
        for s in range(11):
            f0 = s * 256
            nf = min(256, FF - f0)
            chunks = [(c * 128, min(128, nf - c * 128)) for c in range((nf + 127) // 128)]
            ws, wk = nextW()
            wg = ws[:, 0:4096].rearrange("p (k t n) -> p k t n", k=8, t=2)
            for t in range(2):
                dma("sync", wg[:, :, t, 0:nf], WB[wgu].rearrange("(k p) n -> p k n", p=128)[:, :, t * FF + f0:t * FF + f0 + nf],
                    [("Wb", wgu)], [wk], wk)
            wd_s, wdk = nextW()
            wd = wd_s[:, 0:2048].rearrange("p (c n) -> p c n", c=2)
            for ci, (c0, cm) in enumerate(chunks):
                dma("sync", wd[0:cm, ci, :], WB[wdn][f0 + c0:f0 + c0 + cm, :], [("Wb", wdn)], [wdk], wdk)
            for st in sts:
                c0t, tn = st["col0"], st["tn"]
                for ci, (c0, cm) in enumerate(chunks):
                    pg, pgk = nextF()
                    pu, puk = nextF()
                    for k in range(8):
                        mm(pg[0:cm, 0:tn], wg[:, k, 0, c0:c0 + cm], xnT[:, k, c0t:c0t + tn], k == 0, k == 7, [wk] + xk(st), [pgk])
                    for k in range(8):
                        mm(pu[0:cm, 0:tn], wg[:, k, 1, c0:c0 + cm], xnT[:, k, c0t:c0t + tn], k == 0, k == 7, [wk] + xk(st), [puk])
                    act(sg[0:cm, 0:tn], pg[0:cm, 0:tn], AF.Silu, [pgk], ["sg"])
                    vtt(hT[0:cm, ci, 0:tn], sg[0:cm, 0:tn], pu[0:cm, 0:tn], ALU.mult, ["sg", puk], [("hT", ci)])
                for bi, b in enumerate(st["blks"]):
                    n = b["n"]
                    pd = [nextF(), nextF()]
                    for half in range(2):
                        for ci, (c0, cm) in enumerate(chunks):
                            mm(pd[half][0][0:n, :], hT[0:cm, ci, bi * 128:bi * 128 + n], wd[0:cm, ci, half * 512:(half + 1) * 512],
                               ci == 0, ci == len(chunks) - 1, [("hT", ci), wdk], [pd[half][1]])
                    for half in range(2):
                        dst = yacc[0:n, b["yi"], half * 512:(half + 1) * 512]
                        if s == 0:
                            vcopy(dst, pd[half][0][0:n, :], [pd[half][1]], [("yacc", b["yi"], half)])
                        else:
                            vtt(dst, dst, pd[half][0][0:n, :], ALU.add, [("yacc", b["yi"], half), pd[half][1]],
                                [("yacc", b["yi"], half)])

    def post_res(b, gname, scale, out_ap, outk):
        n = b["n"]
        ya = yacc[0:n, b["yi"], :]
        yk = [("yacc", b["yi"], 0), ("yacc", b["yi"], 1)]
        rstd(ya, n, yk, 1)
        vstt(tokC[0:n, :], ya, ss[0:n, 1:2], gains[gname][0:n, :], ALU.mult, ALU.mult, yk + [("ss", 1), "G_" + gname], ["tokC"])
        vstt(out_ap, tokC[0:n, :], scale, hres[0:n, b["yi"], :], ALU.mult, ALU.add, ["tokC", ("hres", b["yi"])], [outk])

    WIN = W["w_in"]
    for gi in range(NG):
        sts = subtiles(gi)
        T = gi
        nkb = 16 * T + 16
        for b in groups[gi]:
            n = b["n"]
            hk = ("hres", b["yi"])
            dma("sync", hres[0:n, b["yi"], :], src_rows(b, xp, xs), (), [hk], ("hres_ld", b["yi"]))
            norm_to_T(hres[0:n, b["yi"], :], n, [hk], "g1pre", b["col0"])
        ffn(gi, "w_gu1", "w_dn1")
        for b in groups[gi]:
            n = b["n"]
            hk = ("hres", b["yi"])
            post_res(b, "g1post", 0.5, hres[0:n, b["yi"], :], hk)
            norm_to_T(hres[0:n, b["yi"], :], n, [hk], "gmpre", b["col0"])
        if STAGE < 1:
            for b in groups[gi]:
                n = b["n"]
                dma("sync", src_rows(b, y_p, y_s), hres[0:n, b["yi"], :], [("hres", b["yi"])], (), ("y_st", b["yi"]))
            continue
        st = sts[0]
        w1, w1k = wload_cols("w_in", 0, 512)
        w1b, w1bk = wload_cols("w_in", 512, 1024)
        for h in range(8):
            pq, pqk = nextF()
            for k in range(8):
                mm(pq[0:64, :], w1[:, k, h * 64:(h + 1) * 64], xnT[:, k, 0:512], k == 0, k == 7, [w1k] + xk(st), [pqk])
            vts(Qext[0:64, h, :], pq[0:64, :], 0.125, None, ALU.mult, ALU.bypass, [pqk], [("Qext", h)])
            pk_, pkk = nextF()
            for k in range(8):
                mm(pk_[0:64, :], w1b[:, k, h * 64:(h + 1) * 64], xnT[:, k, 0:512], k == 0, k == 7, [w1bk] + xk(st), [pkk])
            vcopy(KTst[0:64, :], pk_[0:64, :], [pkk], ["KTst"])
            dma("sync", kt_in[T][h * 68:(h + 1) * 68, :], KTst[:, :], ["KTst"], [("kt_in", T)], "KTst_st")
        for b in st["blks"]:
            pk_, pkk = nextF()
            for k in range(8):
                mm(pk_[:, :], xnT[:, k, b["col0"]:b["col0"] + 128], w1b[:, k, :], k == 0, k == 7, [w1bk, ("xnT", b["col0"])], [pkk])
            vcopy(otmp[:, :], pk_[:, :], [pkk], ["otmp"])
            dma("sync", nk_p[b["row0"]:b["row0"] + 128, :], otmp[:, :], ["otmp"], (), "otmp_st")
        if len(sts) > 1:
            pk_, pkk = nextF()
            for k in range(8):
                mm(pk_[0:32, :], xnT[:, k, 512:544], w1b[:, k, :], k == 0, k == 7, [w1bk, ("xnT", 512)], [pkk])
            vcopy(otmp[0:32, :], pk_[0:32, :], [pkk], ["otmp"])
            dma("sync", nk_s[:, :], otmp[0:32, :], ["otmp"], (), "otmp_st")
        w2, w2k = wload_cols("w_in", 1024, 1536)
        w2f, w2fk = wload_cols("w_in", 1536, 1544)
        for b in st["blks"]:
            j = b["kb"]
            pv, pvk = nextF()
            for k in range(8):
                mm(pv[:, :], xnT[:, k, b["col0"]:b["col0"] + 128], w2[:, k, 0:512], k == 0, k == 7, [w2k, ("xnT", b["col0"])], [pvk])
            pf, pfk = nextF()
            for k in range(8):
                mm(pf[:, 0:8], xnT[:, k, b["col0"]:b["col0"] + 128], w2f[:, k, 0:8], k == 0, k == 7, [w2fk, ("xnT", b["col0"])], [pfk])
            vcopy(otmp[:, :], pv[:, :], [pvk], ["otmp"])
            dma("sync", nv_p[b["row0"]:b["row0"] + 128, :], otmp[:, :], ["otmp"], (), "otmp_st")
            vcopy(Vst[:, j, :, 0:64], pv[:, :].rearrange("p (h d) -> p h d", h=8), [pvk], [("Vst", j)])
            vtt(lft[:, :], pf[:, 0:8], bfor[:, :], ALU.add, [pfk, "bfor"], ["lft"])
            act(lft[:, :], lft[:, :], AF.Exp, ["lft"], ["lft"], scale=-1.0)
            act(lft[:, :], lft[:, :], AF.Ln, ["lft"], ["lft"], bias=1.0)
            vts(lf[:, j, :], lft[:, :], -1.0, None, ALU.mult, ALU.bypass, ["lft"], [("lf", j)])
            dma("sync", nlf_p[b["row0"]:b["row0"] + 128, :], lf[:, j, :], [("lf", j)], (), ("lf_st", j))
        if len(sts) > 1:
            pv, pvk = nextF()
            for k in range(8):
                mm(pv[0:32, :], xnT[:, k, 512:544], w2[:, k, 0:512], k == 0, k == 7, [w2k, ("xnT", 512)], [pvk])
            pf, pfk = nextF()
            for k in range(8):
                mm(pf[0:32, 0:8], xnT[:, k, 512:544], w2f[:, k, 0:8], k == 0, k == 7, [w2fk, ("xnT", 512)], [pfk])
            vcopy(otmp[0:32, :], pv[0:32, :], [pvk], ["otmp"])
            dma("sync", nv_s[:, :], otmp[0:32, :], ["otmp"], (), "otmp_st")
            vtt(lft[0:32, :], pf[0:32, 0:8], bfor[0:32, :], ALU.add, [pfk, "bfor"], ["lft"])
            act(lft[0:32, :], lft[0:32, :], AF.Exp, ["lft"], ["lft"], scale=-1.0)
            act(lft[0:32, :], lft[0:32, :], AF.Ln, ["lft"], ["lft"], bias=1.0)
            vts(lft[0:32, :], lft[0:32, :], -1.0, None, ALU.mult, ALU.bypass, ["lft"], ["lft"])
            dma("sync", nlf_s[:, :], lft[0:32, :], ["lft"], (), "lft_st")
        v_in_v = v_in[T].ap().rearrange("(h p b) e -> h p b e", h=8, p=128)
        for h in range(8):
            dma("sync", v_in_v[h], Vst[:, :, h, :], [("Vst", j) for j in range(4)], [("v_in", T)], ("Vst_st", h))
        lfk = [("lf", j) for j in range(4)]
        Cloc_t = yacc[:, 1, 512:544].rearrange("p (j h) -> p j h", h=8)
        for j in range(4):
            pc, pck = nextF()
            mm(pc[:, 0:8], tri_f[:, :], lf[:, j, :], True, j == 0, ["tri_f"] + lfk, [pck])
            for jj in range(j):
                mm(pc[:, 0:8], ones_f[:, :], lf[:, jj, :], False, jj == j - 1, ["ones_f"] + lfk, [pck])
            vcopy(Cloc_t[:, j, :], pc[:, 0:8], [pck], [("yacc", 1, 1)])
        dma("sync", f_in[T][128:256, 0:32], yacc[:, 1, 512:544], [("yacc", 1, 1)], [("f_in", T)], "cloc_st")
        pr, prk = nextF()
        for j in range(4):
            mm(pr[0:8, j * 128:(j + 1) * 128], lf[:, j, :], tri_f[:, :], True, j == 0, ["tri_f"] + lfk, [prk])
            for jj in range(j):
                mm(pr[0:8, j * 128:(j + 1) * 128], lf[:, jj, :], ones_f[:, :], False, jj == j - 1, ["ones_f"] + lfk, [prk])
        crl = hT[0:8, 0:3, :]
        CRK = [("hT", 0), ("hT", 1), ("hT", 2)]
        crf = otmp[0:8, :]
        crg = tokC[0:8, 0:512]
        vcopy(crl[:, 0, :], pr[0:8, :], [prk], CRK)
        vtt(crf, pr[0:8, :], crl[:, 0, :], ALU.subtract, [prk] + CRK, ["otmp"])
        vcopy(crl[:, 1, :], crf, ["otmp"], CRK)
        vtt(crg, crf, crl[:, 1, :], ALU.subtract, ["otmp"] + CRK, ["tokC"])
        vcopy(crl[:, 2, :], crg, ["tokC"], CRK)
        for h in range(8):
            for r in range(3):
                dma("sync", Qext[64 + r:65 + r, h, :], crl[h:h + 1, r, :], CRK, [("Qext", h)], ("Qx", h))
        w5, w5k = wload_cols("w_in", 2568, 3080)
        w5a, w5ak = wload_cols("w_in", 3080, 3096)
        for b in st["blks"]:
            j = b["kb"]
            pg, pgk = nextF()
            for k in range(8):
                mm(pg[:, :], xnT[:, k, b["col0"]:b["col0"] + 128], w5[:, k, 0:512], k == 0, k == 7, [w5k, ("xnT", b["col0"])], [pgk])
            act(gate_tok[:, j, :], pg[:, :], AF.Silu, [pgk], [("gate", j)])
        pa, pak = nextF()
        for k in range(8):
            mm(pa[0:16, :], w5a[:, k, 0:16], xnT[:, k, 0:512], k == 0, k == 7, [w5ak] + xk(st), [pak])
        vcopy(gaT[:, :], pa[0:16, :], [pak], ["gaT"])
        w3, w3k = wload_cols("w_in", 1544, 2056)
        for h in range(4):
            px, pxk = nextF()
            mm(px[0:64, :], wa2[0:16, h * 64:(h + 1) * 64], gaT[0:16, :], True, True, ["wa2", "gaT"], [pxk])
            act(sp[:, :], px[0:64, :], AF.Exp, [pxk, "nbgate"], ["sp"], scale=-1.0, bias=nbgate[:, h:h + 1])
            act(sp[:, :], sp[:, :], AF.Ln, ["sp"], ["sp"], bias=1.0)
            S.add("dve", (lambda e: e.tensor_tensor_scan(out=cs[:, :], data0=rmask[:, 0:512], data1=sp[:, :], initial=0.0,
                                                          op0=ALU.mult, op1=ALU.add)), ["rmask", "sp"], ["cs"])
            act(eb[:, :], cs[:, :], AF.Exp, ["cs"], ["eb"], scale=-1.0 / 16)
            act(enb[:, :], cs[:, :], AF.Exp, ["cs"], ["enb"], scale=1.0 / 16)
            vcopy(ebl[:, h, :], eb[:, :].rearrange("p (c t) -> p c t", c=4)[:, :, 127], ["eb"], [("ebl", h)])
            pq, pqk = nextF()
            for k in range(8):
                mm(pq[0:64, :], w3[:, k, h * 64:(h + 1) * 64], xnT[:, k, 0:512], k == 0, k == 7, [w3k] + xk(st), [pqk])
            vstt(gqT[:, h, :], pq[0:64, :], 0.125, eb[:, :], ALU.mult, ALU.mult, [pqk, "eb"], [("gqT", h)])
            pk_, pkk = nextF()
            for k in range(8):
                mm(pk_[0:64, :], w3[:, k, 256 + h * 64:256 + (h + 1) * 64], xnT[:, k, 0:512], k == 0, k == 7, [w3k] + xk(st), [pkk])
            vtt(gkT[:, h, :], pk_[0:64, :], enb[:, :], ALU.mult, [pkk, "enb"], [("gkT", h)])
            for c in range(4):
                vts(kdT[:, h, c * 128:(c + 1) * 128], gkT[:, h, c * 128:(c + 1) * 128], ebl[:, h, c:c + 1], None,
                    ALU.mult, ALU.bypass, [("gkT", h), ("ebl", h)], [("kdT", h)])
        for c in range(4):
            pb, pbk = nextB()
            for h in range(4):
                tp(pb[:, h * 64:(h + 1) * 64], kdT[:, h, c * 128:(c + 1) * 128], ident_b[0:64, 0:64], [("kdT", h), "ident_b"], [pbk])
            vcopy(kd_tok[:, c, :, :], pb[:, 0:256].rearrange("p (h k) -> p h k", h=4), [pbk], [("kd_tok", c)])
        w4, w4k = wload_cols("w_in", 2056, 2568)
        for b in st["blks"]:
            j = b["kb"]
            pv, pvk = nextF()
            for k in range(8):
                mm(pv[:, :], xnT[:, k, b["col0"]:b["col0"] + 128], w4[:, k, :], k == 0, k == 7, [w4k, ("xnT", b["col0"])], [pvk])
            vcopy(vg_tok[:, j, :], pv[:, :], [pvk], [("vg_tok", j)])
        if STAGE < 2:
            continue
        Sloc = yacc[0:64, 0, 512:1024].rearrange("p (h v) -> p h v", h=4)
        SLK = [("yacc", 0, 1)]
        for c in range(4):
            pS, pSk = nextF()
            for h in range(4):
                mm(pS[0:64, h * 128:(h + 1) * 128], kd_tok[:, c, h, :], vg_tok[:, c, h * 128:(h + 1) * 128], True, True,
                   [("kd_tok", c), ("vg_tok", c)], [pSk])
            if c == 0:
                vcopy(yacc[0:64, 0, 512:1024], pS[0:64, :], [pSk], SLK)
            else:
                for h in range(4):
                    vstt(Sloc[:, h, :], Sloc[:, h, :], ebl[:, h, c:c + 1], pS[0:64, h * 128:(h + 1) * 128], ALU.mult, ALU.add,
                         SLK + [("ebl", h), pSk], SLK)
        dma("sync", f_in[T][0:64, :], yacc[0:64, 0, 512:1024], SLK, [("f_in", T)], "sloc_st")
        eBt = yacc[0:64, 2, 512:516]
        EK = [("ebl", h) for h in range(4)]
        vtt(eBt, ebl[:, :, 0], ebl[:, :, 1], ALU.mult, EK, [("yacc", 2, 1)])
        vtt(eBt, eBt, ebl[:, :, 2], ALU.mult, EK + [("yacc", 2, 1)], [("yacc", 2, 1)])
        vtt(eBt, eBt, ebl[:, :, 3], ALU.mult, EK + [("yacc", 2, 1)], [("yacc", 2, 1)])
        dma("sync", f_in[T][64:128, 0:4], eBt, [("yacc", 2, 1)], [("f_in", T)], "ebt_st")
        RG = [[0, 1, 2, 3], [4, 5, 6, 7]]
        for nm, src, dst in (("kt", kt_in[T], ktg[T]), ("v", v_in[T], vgt[T]), ("f", f_in[T], fgt[T])):
            S.add("pool", (lambda e, src=src, dst=dst: e.collective_compute("AllGather", ALU.bypass, replica_groups=RG,
                                                                             ins=[src.ap().opt()], outs=[dst.ap().opt()])),
                  [(nm + "_in" if nm != "f" else "f_in", T)], [(nm + "g", T)], dma=("cc", nm), inc=1)
        Clg = yacc[:, 1, 0:128].rearrange("p (q h) -> p q h", h=8)
        CLK = [("yacc", 1, 0)]
        for gq in range(4):
            dma("sync", yacc[:, 1, gq * 32:(gq + 1) * 32], fgt[T][gq * 256 + 128:gq * 256 + 256, 0:32], [("fg", T)], CLK, "clg_ld")
        Coffs = yacc[:, 2, 0:40].rearrange("p (q h) -> p q h", h=8)
        CFK = [("yacc", 2, 0)]
        vcopy(Coffs[:, 0, :], Crun[:, :], ["Crun"], CFK)
        for gq in range(4):
            vts(otmp[:, 0:8], Clg[:, gq * 4 + 3, :], pcol[:, 1:2], None, ALU.mult, ALU.bypass, CLK + ["pcol"], ["otmp"])
            pt_, ptk = nextF()
            mm(pt_[:, 0:8], ones_f[:, :], otmp[:, 0:8], True, True, ["ones_f", "otmp"], [ptk])
            vtt(Coffs[:, gq + 1, :], Coffs[:, gq, :], pt_[:, 0:8], ALU.add, CFK + [ptk], CFK)
        for gq in range(4):
            for bl in range(4):
                qi = 16 * T + 4 * gq + bl
                vtt(Cglob[:, qi, :], Clg[:, gq * 4 + bl, :], Coffs[:, gq, :], ALU.add, CLK + CFK, [("Cglob", qi)])
        vts(Cst[:, T, :], Coffs[:, 0, :], wsel[:, 0:1], None, ALU.mult, ALU.bypass, CFK + ["wsel"], [("Cst", T)])
        for gq in range(1, 4):
            vstt(Cst[:, T, :], Coffs[:, gq, :], wsel[:, gq:gq + 1], Cst[:, T, :], ALU.mult, ALU.add, CFK + ["wsel", ("Cst", T)], [("Cst", T)])
        vcopy(Crun[:, :], Coffs[:, 4, :], CFK, ["Crun"])
        stg = yacc[0:64, 0, 0:512].rearrange("p (h v) -> p h v", h=4)
        STK = [("yacc", 0, 0)]
        ebs = yacc[0:64, 2, 516:520]
        for gq in range(4):
            dma("sync", yacc[0:64, 0, 0:512], fgt[T][gq * 256:gq * 256 + 64, :], [("fg", T)], STK, "stg_ld")
            dma("sync", ebs, fgt[T][gq * 256 + 64:gq * 256 + 128, 0:4], [("fg", T)], [("yacc", 2, 1)], "ebs_ld")
            if gq == 0:
                vts(Sg[:, :, :].rearrange("p h v -> p (h v)"), Srun[:, :, :].rearrange("p h v -> p (h v)"), wsel[0:64, 0:1], None,
                    ALU.mult, ALU.bypass, ["Srun", "wsel"], ["Sg"])
            else:
                vstt(Sg[:, :, :].rearrange("p h v -> p (h v)"), Srun[:, :, :].rearrange("p h v -> p (h v)"), wsel[0:64, gq:gq + 1],
                     Sg[:, :, :].rearrange("p h v -> p (h v)"), ALU.mult, ALU.add, ["Srun", "wsel", "Sg"], ["Sg"])
            for h in range(4):
                vstt(Srun[:, h, :], Srun[:, h, :], ebs[:, h:h + 1], stg[:, h, :], ALU.mult, ALU.add,
                     ["Srun", ("yacc", 2, 1)] + STK, ["Srun"])
        vcopy(Sgb[:, :, :], Sg[:, :, :], ["Sg"], ["Sgb"])
        for c in range(4):
            pA, pAk = nextF()
            for h in range(4):
                mm(pA[:, h * 128:(h + 1) * 128], gkT[:, h, c * 128:(c + 1) * 128], gqT[:, h, c * 128:(c + 1) * 128], True, True,
                   [("gkT", h), ("gqT", h)], [pAk])
            vtt(Am[:, :], pA[:, :], gmask[:, :], ALU.mult, [pAk, "gmask"], ["Am"])
            po, pok = nextF()
            for h in range(4):
                mm(po[:, h * 128:(h + 1) * 128], Am[:, h * 128:(h + 1) * 128], vg_tok[:, c, h * 128:(h + 1) * 128], True, False,
                   ["Am", ("vg_tok", c)], [pok])
                mm(po[:, h * 128:(h + 1) * 128], gqT[:, h, c * 128:(c + 1) * 128], Sgb[:, h, :], False, True,
                   [("gqT", h), "Sgb"], [pok])
            for h in range(4):
                act(tokC[:, h * 128:(h + 1) * 128], po[:, h * 128:(h + 1) * 128], AF.Square, [pok], ["tokC", ("ss", 4 + h)],
                    accum_out=ss[:, 4 + h:5 + h])
            act(ss[:, 4:8], ss[:, 4:8], AF.Sqrt, [("ss", 4 + h) for h in range(4)], [("ss", 4 + h) for h in range(4)],
                scale=1.0 / 128, bias=1e-6)
            vrecip(ss[:, 4:8], ss[:, 4:8], [("ss", 4 + h) for h in range(4)], [("ss", 4 + h) for h in range(4)])
            for h in range(4):
                vstt(otmp[:, h * 128:(h + 1) * 128], po[:, h * 128:(h + 1) * 128], ss[:, 4 + h:5 + h], ggla[:, h * 128:(h + 1) * 128],
                     ALU.mult, ALU.mult, [pok, ("ss", 4 + h), "ggla"], ["otmp"])
            vtt(mix_tok[:, :], otmp[:, :], gate_tok[:, c, :], ALU.mult, ["otmp", ("gate", c)], ["mix_tok"])
            pb, pbk = nextB()
            for q in range(4):
                tp(pb[:, q * 128:(q + 1) * 128], mix_tok[:, q * 128:(q + 1) * 128], ident_b[:, :], ["mix_tok", "ident_b"], [pbk])
            vcopy(mgT[:, :, c * 128:(c + 1) * 128], pb[:, 0:512].rearrange("p (q t) -> p q t", q=4), [pbk], [("mgT", c)])
            pS, pSk = nextF()
            for h in range(4):
                mm(pS[0:64, h * 128:(h + 1) * 128], kd_tok[:, c, h, :], vg_tok[:, c, h * 128:(h + 1) * 128], True, True,
                   [("kd_tok", c), ("vg_tok", c)], [pSk])
            for h in range(4):
                vstt(Sg[:, h, :], Sg[:, h, :], ebl[:, h, c:c + 1], pS[0:64, h * 128:(h + 1) * 128], ALU.mult, ALU.add,
                     ["Sg", ("ebl", h), pSk], ["Sg"])
            vcopy(Sgb[:, :, :], Sg[:, :, :], ["Sg"], ["Sgb"])
        if gi == NG - 1:
            dma("sync", gla_p, Srun[:, :, :].rearrange("p h v -> p (h v)"), ["Srun"], (), "gla_p_st")
        for h in range(8):
            vts(biasT[:, 0:nkb], Cglob[:, 0:nkb, h], Cst[:, T, h:h + 1], -1.0, ALU.subtract, ALU.mult,
                [("Cglob", q) for q in range(nkb)] + [("Cst", T)], ["biasT"])
            vts(biasO[:, :], yacc[:, 1, 512:544].rearrange("p (j h) -> p j h", h=8)[:, :, h], -1.0, None, ALU.mult, ALU.bypass,
                [("yacc", 1, 1)], ["biasO"])
            kts, ktk = nextW()
            kts2, ktk2 = nextW()
            for j in range(T + 1):
                dst_s, dk = (kts, ktk) if j < 2 else (kts2, ktk2)
                dcol = (j % 2) * 2048
                dma("sync", dst_s[0:68, dcol:dcol + 2048].rearrange("p (g t) -> p g t", g=4),
                    ktg[j].ap().rearrange("(g h r) t -> g h r t", g=4, h=8)[:, h, :, :].rearrange("g r t -> r g t"),
                    [("ktg", j)], [dk], dk)
            cur_s, ck_ = (kts, ktk) if T < 2 else (kts2, ktk2)
            ccol = (T % 2) * 2048
            dma("pool", cur_s[67:68, ccol:ccol + 2048], mrow_d, (), [ck_], ck_)
            if T < 2:
                dma("sync", kts2[0:1, 0:8], ktg[0][0:1, 0:8], [("ktg", 0)], [ktk2], ktk2)
            vs_, vk = nextW()
            vv = vs_[:, 0:nkb * 65].rearrange("p (b e) -> p b e", e=65)
            for j in range(T + 1):
                vsrc = vgt[j].ap().rearrange("(g h p b) e -> g h p b e", g=4, h=8, p=128)
                for gq in range(4):
                    dma("sync", vv[:, 16 * j + 4 * gq:16 * j + 4 * gq + 4, :], vsrc[gq, h], [("vg", j)], [vk], vk)
            os_, ok_ = nextW()
            ko = os_[0:68, 0:512]
            vo = os_[:, 512:772].rearrange("p (b e) -> p b e", e=65)
            dma("sync", ko, kt_in[T][h * 68:(h + 1) * 68, :], [("kt_in", T)], [ok_], ok_)
            dma("sync", vo, v_in[T].ap().rearrange("(h p b) e -> h p b e", h=8, p=128)[h], [("v_in", T)], [ok_], ok_)
            po, pok = nextA()
            for kb in range(nkb):
                pS, pSk = nextF()
                if kb < 32:
                    ksrc, kk_ = kts[0:68, kb * 128:(kb + 1) * 128], ktk
                else:
                    ksrc, kk_ = kts2[0:68, (kb - 32) * 128:(kb - 31) * 128], ktk2
                mm(pS[:, :], ksrc, Qext[0:68, h, :], True, True, [kk_, ("Qext", h)], [pSk])
                pt, ptk = nextP()
                act(pt[:, :], pS[:, :], AF.Exp, [pSk, "biasT"], [ptk], bias=biasT[:, kb:kb + 1])
                mm(po[0:65, :], vv[:, kb, :], pt[:, :], kb == 0, False, [vk, ptk], [pok])
            for kb in range(4):
                pS, pSk = nextF()
                mm(pS[:, :], ko[:, kb * 128:(kb + 1) * 128], Qext[0:68, h, :], True, False, [ok_, ("Qext", h)], [pSk])
                mm(pS[:, :], ident_b[:, :], cmask[:, kb, :], False, True, ["ident_b", "cmask"], [pSk])
                pt, ptk = nextP()
                act(pt[:, :], pS[:, :], AF.Exp, [pSk, "biasO"], [ptk], bias=biasO[:, kb:kb + 1])
                mm(po[0:65, :], vo[:, kb, :], pt[:, :], False, kb == 3, [ok_, ptk], [pok])
            vcopy(Osb[:, :], po[0:65, :], [pok], ["Osb"])
            vrecip(Osb[64:65, :], Osb[64:65, :], ["Osb"], ["Osb"])
            pbc, pbck = nextF()
            mm(pbc[0:65, :], ones_f[64:65, 0:65], Osb[64:65, :], True, True, ["ones_f", "Osb"], [pbck])
            vtt(OnT[:, h, :], Osb[:, :], pbc[0:65, :], ALU.mult, ["Osb", pbck], [("OnT", h)])
        if STAGE < 3:
            continue
        if len(sts) > 1 and not os.environ.get("MK_NOSAMPLE"):
            XS = [("xnT", 512)]
            xsT = lambda k: xnT[:, k, 512:544]
            w1, w1k = wload_cols("w_in", 0, 512)
            w1b, w1bk = wload_cols("w_in", 512, 1024)
            for j in range(4):
                pq, pqk = nextF()
                for k in range(8):
                    mm(pq[:, 0:32], w1[:, k, j * 128:(j + 1) * 128], xsT(k), k == 0, k == 7, [w1k] + XS, [pqk])
                vts(qTs[:, j, :], pq[:, 0:32], 0.125, None, ALU.mult, ALU.bypass, [pqk], ["qTs"])
                pk_, pkk = nextF()
                for k in range(8):
                    mm(pk_[:, 0:32], w1b[:, k, j * 128:(j + 1) * 128], xsT(k), k == 0, k == 7, [w1bk] + XS, [pkk])
                vcopy(kTs[:, j, :], pk_[:, 0:32], [pkk], ["kTs"])
            w2, w2k = wload_cols("w_in", 1024, 1536)
            w2f, w2fk = wload_cols("w_in", 1536, 1544)
            pv, pvk = nextF()
            for k in range(8):
                mm(pv[0:32, :], xsT(k), w2[:, k, 0:512], k == 0, k == 7, [w2k] + XS, [pvk])
            vcopy(Vs_bf[:, :], pv[0:32, :], [pvk], ["Vs_bf"])
            pf, pfk = nextF()
            for k in range(8):
                mm(pf[0:32, 0:8], xsT(k), w2f[:, k, 0:8], k == 0, k == 7, [w2fk] + XS, [pfk])
            vtt(lfs[:, :], pf[0:32, 0:8], bfor[0:32, :], ALU.add, [pfk, "bfor"], ["lfs"])
            act(lfs[:, :], lfs[:, :], AF.Exp, ["lfs"], ["lfs"], scale=-1.0)
            act(lfs[:, :], lfs[:, :], AF.Ln, ["lfs"], ["lfs"], bias=1.0)
            pc, pck = nextF()
            mm(pc[0:8, 0:32], lfs[:, :], gmask_s[0:32, 0:32], True, True, ["lfs", "gmask_s"], [pck])
            vcopy(ncnT[:, :], pc[0:8, 0:32], [pck], ["ncnT"])
            w5, w5k = wload_cols("w_in", 2568, 3080)
            w5a, w5ak = wload_cols("w_in", 3080, 3096)
            pg, pgk = nextF()
            for k in range(8):
                mm(pg[0:32, :], xsT(k), w5[:, k, 0:512], k == 0, k == 7, [w5k] + XS, [pgk])
            act(gate_tok[0:32, 0, :], pg[0:32, :], AF.Silu, [pgk], [("gate", 0)])
            pa, pak = nextF()
            for k in range(8):
                mm(pa[0:16, 0:32], w5a[:, k, 0:16], xsT(k), k == 0, k == 7, [w5ak] + XS, [pak])
            vcopy(gaT[:, 0:32], pa[0:16, 0:32], [pak], ["gaT"])
            w3, w3k = wload_cols("w_in", 1544, 2056)
            for h in range(4):
                px, pxk = nextF()
                mm(px[0:64, 0:32], wa2[0:16, h * 64:(h + 1) * 64], gaT[0:16, 0:32], True, True, ["wa2", "gaT"], [pxk])
                act(sp[:, 0:32], px[0:64, 0:32], AF.Exp, [pxk, "nbgate"], ["sp"], scale=-1.0, bias=nbgate[:, h:h + 1])
                act(sp[:, 0:32], sp[:, 0:32], AF.Ln, ["sp"], ["sp"], bias=1.0)
                S.add("dve", (lambda e: e.tensor_tensor_scan(out=cs[:, 0:32], data0=rmask[:, 512:544], data1=sp[:, 0:32], initial=0.0,
                                                              op0=ALU.mult, op1=ALU.add)), ["rmask", "sp"], ["cs"])
                act(eb[:, 0:32], cs[:, 0:32], AF.Exp, ["cs"], ["eb"], scale=-1.0 / 16)
                act(enb[:, 0:32], cs[:, 0:32], AF.Exp, ["cs"], ["enb"], scale=1.0 / 16)
                vcopy(ebl[:, h, :], eb[:, 0:32].rearrange("p (c t) -> p c t", c=4)[:, :, 7], ["eb"], [("ebl", h)])
                pq, pqk = nextF()
                for k in range(8):
                    mm(pq[0:64, 0:32], w3[:, k, h * 64:(h + 1) * 64], xsT(k), k == 0, k == 7, [w3k] + XS, [pqk])
                vstt(gqT[:, h, 0:32], pq[0:64, 0:32], 0.125, eb[:, 0:32], ALU.mult, ALU.mult, [pqk, "eb"], [("gqT", h)])
                pk_, pkk = nextF()
                for k in range(8):
                    mm(pk_[0:64, 0:32], w3[:, k, 256 + h * 64:256 + (h + 1) * 64], xsT(k), k == 0, k == 7, [w3k] + XS, [pkk])
                vtt(gkT[:, h, 0:32], pk_[0:64, 0:32], enb[:, 0:32], ALU.mult, [pkk, "enb"], [("gkT", h)])
                for c in range(4):
                    vts(kdT[:, h, c * 8:(c + 1) * 8], gkT[:, h, c * 8:(c + 1) * 8], ebl[:, h, c:c + 1], None,
                        ALU.mult, ALU.bypass, [("gkT", h), ("ebl", h)], [("kdT", h)])
            pb, pbk = nextB()
            for h in range(4):
                tp(pb[0:32, h * 64:(h + 1) * 64], kdT[:, h, 0:32], ident_b[0:64, 0:64], [("kdT", h), "ident_b"], [pbk])
            vcopy(kd_tok[0:32, 0, :, :], pb[0:32, 0:256].rearrange("p (h k) -> p h k", h=4), [pbk], [("kd_tok", 0)])
            w4, w4k = wload_cols("w_in", 2056, 2568)
            pv, pvk = nextF()
            for k in range(8):
                mm(pv[0:32, :], xsT(k), w4[:, k, :], k == 0, k == 7, [w4k] + XS, [pvk])
            vcopy(vg_tok[0:32, 0, :], pv[0:32, :], [pvk], [("vg_tok", 0)])

            idx_i = otmp[:, :].bitcast(I32)
            triS = sg[:, 0:128]
            bmask = tokC[0:64, 0:512]
            KTg = hT
            HK = [("hT", c) for c in range(4)]
            dma("sync", idx_i, ptab_d, (), ["otmp"], "idx_ld")
            vts(idx_i, idx_i, 128.0, pcol[:, 0:1], ALU.mult, ALU.add, ["otmp", "pcol"], ["otmp"])
            vtt(triS, ones_f[:, :], tri_f[:, :], ALU.subtract, ["ones_f", "tri_f"], ["sg"])
            dma("sync", bmask, bmask_d, (), ["tokC"], "bmask_ld")
            ck_rows, cv_rows, clf_rows = cache_k, cache_v, cache_lf

            def gather(out_ap, src, col, reads, writes, key):
                S.add("pool", lambda e: e.indirect_dma_start(out=out_ap, out_offset=None, in_=src,
                                                             in_offset=bass.IndirectOffsetOnAxis(ap=idx_i[:, col:col + 1], axis=0)),
                      ["otmp"] + list(reads), writes, dma=key)

            for bb in range(4):
                memset(Qbd[:], 0.0, ["Qbd"])
                for j in range(4):
                    vcopy(Qbd[0:64, j, (2 * j) * 8:(2 * j) * 8 + 8], qTs[0:64, j, bb * 8:(bb + 1) * 8], ["qTs"], ["Qbd"])
                    vcopy(Qbd[64:128, j, (2 * j + 1) * 8:(2 * j + 1) * 8 + 8], qTs[64:128, j, bb * 8:(bb + 1) * 8], ["qTs"], ["Qbd"])
                lb = 2 * (bb % 2)
                lfp = yacc[:, lb, :].rearrange("p (g h) -> p g h", h=8)
                prf = yacc[:, lb + 1, :].rearrange("p (g h) -> p g h", h=8)
                YK = lambda j, lb=lb: [("yacc", lb + j, 0), ("yacc", lb + j, 1)]
                nlb = 2 * ((bb + 1) % 2)
                lfp_n = yacc[:, nlb, :].rearrange("p (g h) -> p g h", h=8)
                YKn = [("yacc", nlb, 0), ("yacc", nlb, 1)]
                if bb == 0:
                    for pg_ in range(128):
                        gather(lfp[:, pg_, :], clf_rows, pg_, (), YK(0), ("lfp", pg_ % 8))
                for h in range(8):
                    S.add("dve", (lambda e, h=h, prf=prf, lfp=lfp: e.tensor_tensor_scan(out=prf[:, :, h], data0=ones_f[:, :], data1=lfp[:, :, h], initial=0.0,
                                                                       op0=ALU.mult, op1=ALU.add)), ["ones_f"] + YK(0), YK(1))
                for h in range(8):
                    vts(prf[:, :, h], prf[:, :, h], prf[:, 127, h:h + 1], -1.0, ALU.subtract, ALU.mult, YK(1), YK(1))
                pacc, pacck = nextA()
                for g in range(32):
                    ks_, kk = nextW()
                    kp = ks_[:, 0:2048].rearrange("p (a n) -> p a n", a=4)
                    vs2, vk2 = nextW()
                    vp = vs2[:, 0:2048].rearrange("p (a n) -> p a n", a=4)
                    for a in range(4):
                        gather(kp[:, a, :], ck_rows, bb * 128 + g * 4 + a, (), [kk], (kk, a))
                    for a in range(4):
                        gather(vp[:, a, :], cv_rows, bb * 128 + g * 4 + a, (), [vk2], (vk2, a))
                    if bb < 3:
                        for a in range(4):
                            gather(lfp_n[:, g * 4 + a, :], clf_rows, (bb + 1) * 128 + g * 4 + a, (), YKn, ("lfp", (g * 4 + a) % 8))
                    par = g % 2
                    KTg_, HK_ = (KTg, HK) if par == 0 else (xnT[:, 0:4, 0:512], [("xnT", 0), ("xnT", 128), ("xnT", 256), ("xnT", 384)] * 1)
                    afT_, afk = (gaT[0:8, :], "gaT") if par == 0 else (gate_tok[0:8, 1, :], ("gate", 1))
                    Pg_, pgk_ = (Am[0:64, :], "Am") if par == 0 else (tokb[0:64, 0:512], "tokb")
                    PT_, ptk_ = (PT[:, :, :], "PT") if par == 0 else (tokb[:, 512:768].rearrange("p (a n) -> p a n", a=4), "tokb")
                    pa, pak = nextF()
                    for a in range(4):
                        pg_ = g * 4 + a
                        mm(pa[0:8, a * 128:(a + 1) * 128], lfp[:, pg_, :], triS, True, False, YK(0) + ["sg"], [pak])
                        mm(pa[0:8, a * 128:(a + 1) * 128], prf[:, pg_, :], ones_f[:, :], False, True, YK(1) + ["ones_f"], [pak])
                    vcopy(afT_, pa[0:8, :], [pak], [afk])
                    for jj in range(2):
                        pb, pbk = nextB()
                        for j2 in range(2):
                            j = jj * 2 + j2
                            for a in range(4):
                                tp(pb[:, j2 * 512 + a * 128:j2 * 512 + (a + 1) * 128], kp[:, a, j * 128:(j + 1) * 128], ident_b[:, :],
                                   [kk, "ident_b"], [pbk])
                        vcopy(KTg_[:, jj * 2:jj * 2 + 2, :], pb[:, :].rearrange("p (j n) -> p j n", j=2), [pbk],
                              HK_[jj * 2:jj * 2 + 2] if par == 0 else HK_)
                    pS, pSk = nextF()
                    for j in range(4):
                        mm(pS[0:64, :], Qbd[:, j, :], KTg_[:, j, :], j == 0, False, ["Qbd"] + ([HK_[j]] if par == 0 else HK_), [pSk])
                    mm(pS[0:64, :], esel[0:8, :], afT_, False, True, ["esel", afk], [pSk])
                    act(Pg_, pS[0:64, :], AF.Exp, [pSk], [pgk_, ("dsum", g)], accum_out=dsum[:, g:g + 1])
                    pb, pbk = nextB()
                    for a in range(4):
                        tp(pb[:, a * 64:(a + 1) * 64], Pg_[:, a * 128:(a + 1) * 128], ident_b[0:64, 0:64], [pgk_, "ident_b"], [pbk])
                    vcopy(PT_, pb[:, 0:256].rearrange("p (a n) -> p a n", a=4), [pbk], [ptk_])
                    for a in range(4):
                        mm(pacc[0:64, :], PT_[:, a, :], vp[:, a, :], g == 0 and a == 0, False, [ptk_, vk2], [pacck])
                pS, pSk = nextF()
                for j in range(4):
                    mm(pS[0:64, 0:32], Qbd[:, j, :], kTs[:, j, :], j == 0, False, ["Qbd", "kTs"], [pSk])
                mm(pS[0:64, 0:32], esel[0:8, :], ncnT[0:8, :], False, False, ["esel", "ncnT"], [pSk])
                mm(pS[0:64, 0:32], ident_b[0:64, 0:64], smask[:, bb * 32:(bb + 1) * 32], False, True, ["ident_b", "smask"], [pSk])
                act(Am[0:64, 0:32], pS[0:64, 0:32], AF.Exp, [pSk], ["Am", ("dsum", 32)], accum_out=dsum[:, 32:33])
                pb, pbk = nextB()
                tp(pb[0:32, 0:64], Am[0:64, 0:32], ident_b[0:64, 0:64], ["Am", "ident_b"], [pbk])
                vcopy(PT[0:32, 0, :], pb[0:32, 0:64], [pbk], ["PT"])
                mm(pacc[0:64, :], PT[0:32, 0, :], Vs_bf[0:32, :], False, True, ["PT", "Vs_bf"], [pacck])
                DK = [("dsum", g) for g in range(33)]
                S.add("dve", lambda e: e.reduce_sum(out=dsum[:, 39:40], in_=dsum[:, 0:33], axis=mybir.AxisListType.X), DK, [("dsum", 39)])
                vrecip(dsum[:, 39:40], dsum[:, 39:40], [("dsum", 39)], [("dsum", 39)])
                vstt(mix_tok[0:64, :], pacc[0:64, :], dsum[:, 39:40], bmask, ALU.mult, ALU.mult, [pacck, ("dsum", 39), "tokC"], ["mix_tok"])
                po2, po2k = nextF()
                for h in range(8):
                    mm(po2[0:64, h * 8:(h + 1) * 8], mix_tok[0:64, h * 64:(h + 1) * 64], qsel[:, :], True, True, ["mix_tok", "qsel"], [po2k])
                vcopy(OnT_s[0:64, :, bb * 8:(bb + 1) * 8], po2[0:64, 0:64].rearrange("p (h q) -> p h q", h=8), [po2k], [("OnT_s", bb)])

            Vflat = Vst[:, :, :, :].rearrange("p a b c -> p (a b c)")
            Ssb = Vflat[0:64, 0:2048].rearrange("p (b h v) -> p b h v", b=4, h=4)
            VK = [("Vst", j) for j in range(4)]
            Ss1 = yacc[0:64, 2, 512:1024].rearrange("p (h v) -> p h v", h=4)
            for bb in range(4):
                dma("sync", Ss1, sgla_d[bb].rearrange("h k v -> k h v"), (), [("yacc", 2, 1)], "Ss1_ld")
                vcopy(Ssb[:, bb, :, :], Ss1, [("yacc", 2, 1)], VK)
                for h in range(4):
                    vtt(gq_m[:, bb * 4 + h, :], gqT[:, h, 0:32], colm[:, bb * 32:(bb + 1) * 32], ALU.mult, [("gqT", h), "colm"], ["gq_m"])
            pA, pAk = nextF()
            for h in range(4):
                mm(pA[0:32, h * 32:(h + 1) * 32], gkT[:, h, 0:32], gqT[:, h, 0:32], True, True, [("gkT", h), ("gqT", h)], [pAk])
            vtt(Am[0:32, 0:128], pA[0:32, 0:128], gmask_s[:, :], ALU.mult, [pAk, "gmask_s"], ["Am"])
            po, pok = nextF()
            for h in range(4):
                mm(po[0:32, h * 128:(h + 1) * 128], Am[0:32, h * 32:(h + 1) * 32], vg_tok[0:32, 0, h * 128:(h + 1) * 128], True, False,
                   ["Am", ("vg_tok", 0)], [pok])
                for bb in range(4):
                    mm(po[0:32, h * 128:(h + 1) * 128], gq_m[:, bb * 4 + h, :], Ssb[:, bb, h, :], False, bb == 3, ["gq_m"] + VK, [pok])
            for h in range(4):
                act(tokC[0:32, 512 + h * 128:512 + (h + 1) * 128], po[0:32, h * 128:(h + 1) * 128], AF.Square, [pok], ["tokC", ("ss", 4 + h)],
                    accum_out=ss[0:32, 4 + h:5 + h])
            SK = [("ss", 4 + h) for h in range(4)]
            act(ss[0:32, 4:8], ss[0:32, 4:8], AF.Sqrt, SK, SK, scale=1.0 / 128, bias=1e-6)
            vrecip(ss[0:32, 4:8], ss[0:32, 4:8], SK, SK)
            for h in range(4):
                vstt(otmp[0:32, h * 128:(h + 1) * 128], po[0:32, h * 128:(h + 1) * 128], ss[0:32, 4 + h:5 + h], ggla[0:32, h * 128:(h + 1) * 128],
                     ALU.mult, ALU.mult, [pok, ("ss", 4 + h), "ggla"], ["otmp"])
            vtt(mix_tok[0:32, :], otmp[0:32, :], gate_tok[0:32, 0, :], ALU.mult, ["otmp", ("gate", 0)], ["mix_tok"])
            pb, pbk = nextB()
            for q in range(4):
                tp(pb[:, q * 32:(q + 1) * 32], mix_tok[0:32, q * 128:(q + 1) * 128], ident_b[0:32, 0:32], ["mix_tok", "ident_b"], [pbk])
            vcopy(mgT_s[:, :, :], pb[:, 0:128].rearrange("p (q t) -> p q t", q=4), [pbk], ["mgT_s"])
            for bb in range(4):
                vts(kd_tok[0:32, 1, :, :], kd_tok[0:32, 0, :, :], rowm[:, bb:bb + 1], None, ALU.mult, ALU.bypass,
                    [("kd_tok", 0), "rowm"], [("kd_tok", 1)])
                pS, pSk = nextF()
                for h in range(4):
                    mm(pS[0:64, h * 128:(h + 1) * 128], kd_tok[0:32, 1, h, :], vg_tok[0:32, 0, h * 128:(h + 1) * 128], True, True,
                       [("kd_tok", 1), ("vg_tok", 0)], [pSk])
                dma("sync", Ss1, sgla_d[bb].rearrange("h k v -> k h v"), (), [("yacc", 2, 1)], "Ss1_ld")
                for h in range(4):
                    vstt(Ss1[:, h, :], Ss1[:, h, :], ebl[:, h, bb:bb + 1], pS[0:64, h * 128:(h + 1) * 128], ALU.mult, ALU.add,
                         [("yacc", 2, 1), ("ebl", h), pSk], [("yacc", 2, 1)])
                dma("sync", gla_s[bb].rearrange("k (h v) -> k h v", h=4), Ss1, [("yacc", 2, 1)], (), "Ss1_st")
        wo = []
        for half in range(2):
            ws, wk = nextW()
            v = ws[:, 0:4096].rearrange("p (h n) -> p h n", h=8)
            memset(v[64:65, :, :], 0.0, [wk])
            dma("sync", v[0:64, :, :], WB["w_out"][0:512, half * 512:(half + 1) * 512].rearrange("(h d) n -> d h n", d=64), [("Wb", "w_out")], [wk], wk)
            ws2, wk2 = nextW()
            v2 = ws2[:, 0:2048].rearrange("p (c n) -> p c n", c=4)
            dma("sync", v2, WB["w_out"][512:1024, half * 512:(half + 1) * 512].rearrange("(c p) n -> p c n", p=128), [("Wb", "w_out")], [wk2], wk2)
            wo.append((v, wk, v2, wk2))
        for b in st["blks"]:
            j = b["kb"]
            for half in range(2):
                v, wk, v2, wk2 = wo[half]
                pd, pdk = nextF()
                for h in range(8):
                    mm(pd[:, :], OnT[0:65, h, j * 128:(j + 1) * 128], v[0:65, h, :], h == 0, False, [("OnT", h), wk], [pdk])
                for c in range(4):
                    mm(pd[:, :], mgT[:, c, j * 128:(j + 1) * 128], v2[:, c, :], False, c == 3, [("mgT", j), wk2], [pdk])
                vcopy(yacc[:, j, half * 512:(half + 1) * 512], pd[:, :], [pdk], [("yacc", j, half)])
            post_res(b, "gmpost", 1.0, hres[:, j, :], ("hres", j))
            norm_to_T(hres[:, j, :], 128, [("hres", j)], "g2pre", b["col0"])
        for b in groups[gi][4:]:
            for half in range(2):
                v, wk, v2, wk2 = wo[half]
                pd, pdk = nextF()
                for h in range(8):
                    mm(pd[0:32, :], OnT_s[0:65, h, :], v[0:65, h, :], h == 0, False, [("OnT_s", q) for q in range(4)] + [wk], [pdk])
                for c in range(4):
                    mm(pd[0:32, :], mgT_s[:, c, :], v2[:, c, :], False, c == 3, ["mgT_s", wk2], [pdk])
                vcopy(yacc[0:32, 4, half * 512:(half + 1) * 512], pd[0:32, :], [pdk], [("yacc", 4, half)])
            post_res(b, "gmpost", 1.0, hres[0:32, 4, :], ("hres", 4))
            norm_to_T(hres[0:b["n"], b["yi"], :], b["n"], [("hres", b["yi"])], "g2pre", b["col0"])
        ffn(gi, "w_gu2", "w_dn2")
        for b in groups[gi]:
            n = b["n"]
            post_res(b, "g2post", 0.5, hres[0:n, b["yi"], :], ("hres", b["yi"]))
            dma("sync", src_rows(b, y_p, y_s), hres[0:n, b["yi"], :], [("hres", b["yi"])], (), ("y_st", b["yi"]))
    S.emit(nc)
    es.close()
    return nc


_NC = None


def _consts():
    c = {}
    c["ident"] = np.eye(128, dtype=np.float32)
    s = np.arange(128)
    c["tri"] = (s[:, None] <= s[None, :]).astype(np.float32)
    c["ones"] = np.ones((128, 128), np.float32)
    rm = np.ones((64, 544), np.float32)
    rm[:, 0:512:128] = 0
    rm[:, 512::8] = 0
    c["rmask"] = rm
    c["gmask"] = np.tile((s[:, None] <= s[None, :]).astype(np.float32), (1, 4))
    t = np.arange(32)
    c["gmask_s"] = np.tile(((t[:, None] <= t[None, :]) & (t[:, None] // 8 == t[None, :] // 8)).astype(np.float32), (1, 4))
    q = np.arange(512)
    cm = np.zeros((128, 4, 512), np.float32)
    for kb in range(4):
        cm[:, kb, :] = np.where(q[None, :] >= kb * 128 + s[:, None], 0.0, NEG)
    c["cmask"] = cm
    c["wsel"] = np.zeros((128, 4), np.float32)
    c["lmask"] = np.zeros((16, 16), np.float32)
    c["pcol"] = np.stack([np.arange(128, dtype=np.float32), (np.arange(128) == 127).astype(np.float32)], 1)
    hq = np.arange(64)
    c["esel"] = (np.arange(8)[:, None] == hq[None, :] // 8).astype(np.float32)
    c["qsel"] = (hq[:, None] % 8 == np.arange(8)[None, :]).astype(np.float32)
    c["bmask"] = (hq[:, None] // 8 == np.arange(512)[None, :] // 64).astype(np.float32)
    c["colm"] = np.broadcast_to((np.arange(4)[:, None] == t[None, :] // 8).astype(np.float32)[None], (64, 4, 32)).copy()
    c["rowm"] = (t[:, None] // 8 == np.arange(4)[None, :]).astype(np.float32)
    sm = np.full((64, 4, 32), NEG, np.float32)
    for b in range(4):
        for qq in range(8):
            sm[qq::8, b, b * 8:b * 8 + qq + 1] = 0.0
    c["smask_new"] = sm
    return c


def kernel(**inp):
    global _NC
    if _NC is None:
        _NC = build()
    f = lambda a: np.ascontiguousarray(np.asarray(a), dtype=np.float32)
    bc = lambda v, n=128: np.ascontiguousarray(np.broadcast_to(f(v).reshape(1, -1), (n, f(v).size)))
    xp_full = f(inp["x_prompt"])
    xs_full = f(inp["x_sample"]).reshape(256, D)
    ck = f(inp["cache_k"]).reshape(-1, 512)
    cv = f(inp["cache_v"]).reshape(-1, 512)
    clf = f(inp["cache_logf"]).reshape(-1, 8)
    pt = np.asarray(inp["page_table"]).astype(np.int32)
    sg = f(inp["state_gla"])[0]
    shared = dict(
        w_gu1=f(inp["ffn1_w_gu"])[0], w_dn1=f(inp["ffn1_w_down"])[0], w_in=f(inp["w_in"])[0], w_out=f(inp["w_out"])[0],
        w_gu2=f(inp["ffn2_w_gu"])[0], w_dn2=f(inp["ffn2_w_down"])[0], w_a2=f(inp["w_gate_up"])[0],
        g1pre=bc(inp["ffn1_norm_pre"]), g1post=bc(inp["ffn1_norm_post"]), gmpre=bc(inp["mix_norm_pre"]),
        gmpost=bc(inp["mix_norm_post"]), g2pre=bc(inp["ffn2_norm_pre"]), g2post=bc(inp["ffn2_norm_post"]),
        bfor=bc(inp["b_forget"]), bgate=np.ascontiguousarray(f(inp["b_gate"]).reshape(4, 64).T),
        ggla=np.ascontiguousarray(np.tile(bc(inp["gla_norm"]), (1, 4))),
        cache_k=ck, cache_v=cv, cache_lf=clf)
    shared.update(_consts())
    in_maps = []
    for c in range(8):
        m = dict(shared)
        g = c % 4
        m["xp"] = np.ascontiguousarray(xp_full[c // 4].reshape(4, 4, 512, D)[:, g].reshape(2048, D))
        ws = np.zeros((128, 4), np.float32); ws[:, g] = 1.0
        m["wsel"] = ws
        m["mrow"] = np.repeat((np.arange(4) >= g).astype(np.float32), 512)[None, :].copy()
        m["xs"] = np.ascontiguousarray(xs_full[c * 32:(c + 1) * 32])
        m["ptab"] = np.ascontiguousarray(np.broadcast_to(pt[c * 4:(c + 1) * 4].reshape(1, 512), (128, 512)))
        m["sgla"] = np.ascontiguousarray(sg[c * 4:(c + 1) * 4])
        in_maps.append(m)
    res = run_bass_kernel_spmd(_NC, in_maps, core_ids=list(range(8))).results
    r = lambda c, k: np.asarray(res[c][k], dtype=np.float32)
    def gath(key, w):
        o = np.zeros((2, 4, 4, 512, w), np.float32)
        for c in range(8):
            o[c // 4, :, c % 4] = r(c, key).reshape(4, 512, w)
        return o.reshape(2, 8192, w)
    y_prompt = gath("y_p", D)
    y_sample = np.concatenate([r(c, "y_s") for c in range(8)]).reshape(32, 8, D)
    nk_p = gath("nk_p", 512).reshape(1, 2, 8192, 8, 64)
    nv_p = gath("nv_p", 512).reshape(1, 2, 8192, 8, 64)
    nlf_p = gath("nlf_p", 8).reshape(1, 2, 8192, 8)
    gla_p = np.stack([r(0, "gla_p"), r(4, "gla_p")]).reshape(2, 64, 4, 128).transpose(0, 2, 1, 3).reshape(1, 2, 4, 64, 128)
    nk_s = np.concatenate([r(c, "nk_s") for c in range(8)]).reshape(1, 32, 8, 8, 64)
    nv_s = np.concatenate([r(c, "nv_s") for c in range(8)]).reshape(1, 32, 8, 8, 64)
    nlf_s = np.concatenate([r(c, "nlf_s") for c in range(8)]).reshape(1, 32, 8, 8)
    gla_s = np.concatenate([r(c, "gla_s") for c in range(8)]).reshape(32, 64, 4, 128).transpose(0, 2, 1, 3).reshape(1, 32, 4, 64, 128)
    return (y_prompt, y_sample, nk_p, nv_p, nlf_p, np.ascontiguousarray(gla_p), nk_s, nv_s, nlf_s, np.ascontiguousarray(gla_s))
```

```python
import os
from contextlib import ExitStack
import numpy as np
import concourse.bass as bass
import concourse.mybir as mybir
from concourse.bass_utils import run_bass_kernel_spmd

F32 = mybir.dt.float32
BF16 = mybir.dt.bfloat16
I32 = mybir.dt.int32
AF = mybir.ActivationFunctionType
ALU = mybir.AluOpType

D = 1024
FF = 2752
PW = 3096
NEG = -30000.0
STAGE = int(os.environ.get("MK_STAGE", "9"))
NPOOL = int(os.environ.get("MK_POOL", "5120"))

COMPUTE = ("act", "pool", "dve", "pe")
QUEUES = ("sync", "act", "pool", "dve", "pe")


class Op:
    __slots__ = ("eng", "fn", "reads", "writes", "dma", "deps", "sig", "idx", "inc")

    def __init__(self, eng, fn, reads, writes, dma, inc):
        self.eng, self.fn, self.reads, self.writes, self.dma = eng, fn, reads, writes, dma
        self.inc = inc
        self.deps = ()
        self.sig = None


class Sched:
    def __init__(self):
        self.ops = []

    def add(self, eng, fn, reads=(), writes=(), dma=None, inc=None):
        if inc is None:
            inc = 16 if dma is not None else 1
        self.ops.append(Op(eng, fn, tuple(reads), tuple(writes), dma, inc))

    def analyse(self):
        last_w, readers, last_dma = {}, {}, {}
        for i, op in enumerate(self.ops):
            op.idx = i
            deps = set()
            for k in op.reads:
                if k in last_w:
                    deps.add(last_w[k])
            for k in op.writes:
                if k in last_w:
                    deps.add(last_w[k])
                deps.update(readers.get(k, ()))
            if op.dma is not None and op.dma in last_dma:
                deps.add(last_dma[op.dma])
            deps.discard(i)
            for k in op.reads:
                readers.setdefault(k, []).append(i)
            for k in op.writes:
                last_w[k] = i
                readers[k] = []
            if op.dma is not None:
                last_dma[op.dma] = i
            op.deps = deps
        ops = self.ops
        waited = {q: {} for q in QUEUES}
        need = []
        for op in ops:
            best = {}
            for d in op.deps:
                p = ops[d]
                src = ("dma", p.dma) if p.dma is not None else ("eng", p.eng)
                if src == ("eng", "pe") and op.eng == "pe" and op.dma is None:
                    continue
                if d > best.get(src, -1):
                    best[src] = d
            w = waited[op.eng]
            lst = []
            for src, d in best.items():
                if w.get(src, -1) >= d:
                    continue
                w[src] = d
                lst.append((src, d))
            need.append(lst)
            for src, d in lst:
                ops[d].sig = True
        cnt = {}
        for op in ops:
            if op.sig or op.dma is not None:
                src = ("dma", op.dma) if op.dma is not None else ("eng", op.eng)
                cnt[src] = cnt.get(src, 0) + op.inc
                op.sig = cnt[src]
        self.need = need
        self.sources = list(cnt.keys())
        self.final = {}
        for op in ops:
            if op.dma is not None:
                self.final[("dma", op.dma)] = (op.eng, op.sig)

    def emit(self, nc):
        self.analyse()
        ops = self.ops
        with ExitStack() as es:
            sems = {}
            for n, src in enumerate(self.sources):
                sems[src] = es.enter_context(nc.semaphore("s%d" % n))
            block = es.enter_context(nc.Block())

            def section(q):
                def body(eng):
                    for op in ops:
                        if op.eng != q:
                            continue
                        for src, d in self.need[op.idx]:
                            eng.wait_ge(sems[src], ops[d].sig)
                        ins = op.fn(eng)
                        if op.sig:
                            src = ("dma", op.dma) if op.dma is not None else ("eng", op.eng)
                            ins.then_inc(sems[src], op.inc)
                    for src, (qq, val) in self.final.items():
                        if qq == q:
                            eng.wait_ge(sems[src], val)
                return body

            block.sync(section("sync"))
            block.scalar(section("act"))
            block.gpsimd(section("pool"))
            block.vector(section("dve"))
            block.tensor(section("pe"))


def build():
    nc = bass.Bass("TRN2", target_bir_lowering=False)
    S = Sched()
    es = ExitStack()

    def din(name, shape, dt=F32):
        return nc.dram_tensor(name, list(shape), dt, kind="ExternalInput").ap()

    def dout(name, shape, dt=F32):
        return nc.dram_tensor(name, list(shape), dt, kind="ExternalOutput").ap()

    def dscr(name, shape, dt=F32):
        return nc.dram_tensor(name, list(shape), dt)

    def sb(name, shape, dt=F32):
        return es.enter_context(nc.sbuf_tensor("S_" + name, list(shape), dt))

    xp = din("xp", [2048, D])
    xs = din("xs", [32, D])
    W = {}
    for nm, shp in (("w_gu1", [D, 2 * FF]), ("w_dn1", [FF, D]), ("w_in", [D, PW]), ("w_out", [D, D]),
                    ("w_gu2", [D, 2 * FF]), ("w_dn2", [FF, D]), ("w_a2", [16, 256])):
        W[nm] = din(nm, shp)
    G = {}
    for nm in ("g1pre", "g1post", "gmpre", "gmpost", "g2pre", "g2post"):
        G[nm] = din(nm, [128, D])
    bfor_d = din("bfor", [128, 8])
    bgate_d = din("bgate", [64, 4])
    ggla_d = din("ggla", [128, 512])
    ident_d = din("ident", [128, 128])
    tri_d = din("tri", [128, 128])
    ones_d = din("ones", [128, 128])
    rmask_d = din("rmask", [64, 544])
    gmask_d = din("gmask", [128, 512])
    gmask_s_d = din("gmask_s", [32, 128])
    cmask_d = din("cmask", [128, 4, 512])
    wsel_d = din("wsel", [128, 4])
    lmask_d = din("lmask", [16, 16])
    pcol_d = din("pcol", [128, 3])
    ptab4_d = din("ptab4", [128, 128], I32)
    triS4_d = din("triS4", [128, 128])
    ptab_d = din("ptab", [128, 512], I32)
    sgla_d = din("sgla", [4, 4, 64, 128])
    cache_k = din("cache_k", [NPOOL * 128, 512])
    cache_v = din("cache_v", [NPOOL * 128, 512])
    cache_lf = din("cache_lf", [NPOOL * 128, 8])
    smask_new_d = din("smask_new", [64, 4, 32])
    esel_d = din("esel", [8, 64])
    qsel_d = din("qsel", [64, 8])
    bmask_d = din("bmask", [64, 512])
    colm_d = din("colm", [64, 4, 32])
    rowm_d = din("rowm", [32, 4])

    y_p = dout("y_p", [2048, D])
    y_s = dout("y_s", [32, D])
    nk_p = dout("nk_p", [2048, 512])
    nv_p = dout("nv_p", [2048, 512])
    nlf_p = dout("nlf_p", [2048, 8])
    gla_p = dout("gla_p", [64, 512])
    nk_s = dout("nk_s", [32, 512])
    nv_s = dout("nv_s", [32, 512])
    nlf_s = dout("nlf_s", [32, 8])
    gla_s = dout("gla_s", [4, 64, 512])


    NS = int(os.environ.get("MK_NG", "4"))
    kt_in = [dscr("kt_in%d" % i, [8 * 68, 512], BF16) for i in range(NS)]
    ktg = [dscr("ktg%d" % i, [4 * 8 * 68, 512], BF16) for i in range(NS)]
    v_in = [dscr("v_in%d" % i, [8 * 128 * 4, 65], BF16) for i in range(NS)]
    vgt = [dscr("vgt%d" % i, [4 * 8 * 128 * 4, 65], BF16) for i in range(NS)]
    f_in = [dscr("f_in%d" % i, [256, 512]) for i in range(NS)]
    fgt = [dscr("fgt%d" % i, [4 * 256, 512]) for i in range(NS)]
    mrow_d = din("mrow", [1, 2048])

    NW = 4
    wslot = [sb("wslot%d" % i, [128, 4224], BF16) for i in range(NW)]
    xnT = sb("xnT", [128, 8, 544], BF16)
    yacc = sb("yacc", [128, 5, 1024])
    hres = sb("hres", [128, 5, 1024])
    hT = sb("hT", [128, 4, 512], BF16)
    sg = sb("sg", [128, 512])
    tokC = sb("tokC", [128, 1024])
    tokb = sb("tokb", [128, 1024], BF16)
    ss = sb("ss", [128, 8])
    ident_f = sb("ident_f", [128, 128])
    ident_b = sb("ident_b", [128, 128], BF16)
    tri_f = sb("tri_f", [128, 128])
    ones_f = sb("ones_f", [128, 128])
    gains = {nm: sb("G_" + nm, [128, D]) for nm in G}
    bfor = sb("bfor", [128, 8])
    bgate = sb("bgate", [64, 4])
    nbgate = sb("nbgate", [64, 4])
    wa2 = sb("wa2", [16, 256], BF16)
    rmask = sb("rmask", [64, 544])
    ggla = sb("ggla", [128, 512])
    gmask = sb("gmask", [128, 512])
    cmask = sb("cmask", [128, 4, 512], BF16)
    Qext = sb("Qext", [68, 8, 512], BF16)
    KTst = sb("KTst", [68, 512], BF16)
    Vst = sb("Vst", [128, 4, 8, 65], BF16)
    lf = sb("lf", [128, 4, 8])
    lft = sb("lft", [128, 8])
    Cglob = sb("Cglob", [128, 64, 8])
    Crun = sb("Crun", [128, 8])
    Cst = sb("Cst", [128, 4, 8])
    Srun = sb("Srun", [64, 4, 128])
    wsel = sb("wsel", [128, 4])
    biasO = sb("biasO", [128, 4])
    biasT = sb("biasT", [128, 64])
    Pt = [sb("Pt%d" % i, [128, 512], BF16) for i in range(2)]
    Osb = sb("Osb", [65, 512])
    OnT = sb("OnT", [65, 8, 512], BF16)
    gate_tok = sb("gate", [128, 4, 512], BF16)
    gaT = sb("gaT", [16, 512], BF16)
    sp = sb("sp", [64, 512])
    cs = sb("cs", [64, 512])
    eb = sb("eb", [64, 512])
    enb = sb("enb", [64, 512])
    ebl = sb("ebl", [64, 4, 4])
    gqT = sb("gqT", [64, 4, 512], BF16)
    gkT = sb("gkT", [64, 4, 512], BF16)
    kdT = sb("kdT", [64, 4, 512], BF16)
    kd_tok = sb("kd_tok", [128, 4, 4, 64], BF16)
    vg_tok = sb("vg_tok", [128, 4, 512], BF16)
    Sg = sb("Sg", [64, 4, 128])
    Sgb = sb("Sgb", [64, 4, 128], BF16)
    Am = sb("Am", [128, 512], BF16)
    mix_tok = sb("mix_tok", [128, 512], BF16)
    mgT = sb("mgT", [128, 4, 512], BF16)
    otmp = sb("otmp", [128, 512])
    qTs = sb("qTs", [128, 4, 32], BF16)
    kTs = sb("kTs", [128, 4, 32], BF16)
    Qbd = sb("Qbd", [128, 4, 64], BF16)
    Vs_bf = sb("Vs_bf", [32, 512], BF16)
    gq_m = sb("gq_m", [64, 16, 32], BF16)
    OnT_s = sb("OnT_s", [65, 8, 32], BF16)
    mgT_s = sb("mgT_s", [128, 4, 32], BF16)
    PT = sb("PT", [128, 4, 64], BF16)
    dsum = sb("dsum", [64, 40])
    esel = sb("esel", [8, 64], BF16)
    qsel = sb("qsel", [64, 8], BF16)
    smask = sb("smask", [64, 128], BF16)
    colm = sb("colm", [64, 128])
    rowm = sb("rowm", [32, 4])
    gmask_s = sb("gmask_s", [32, 128])
    lfs = sb("lfs", [32, 8])
    ncnT = sb("ncnT", [8, 32], BF16)
    pcol = sb("pcol", [128, 3])

    psF = [es.enter_context(nc.psum_tensor("psF%d" % i, [128, 512], F32)) for i in range(4)]
    psA = [es.enter_context(nc.psum_tensor("psA%d" % i, [128, 512], F32)) for i in range(2)]
    psB = [es.enter_context(nc.psum_tensor("psB%d" % i, [128, 1024], BF16)) for i in range(2)]
    cnt = {"f": 0, "b": 0, "w": 0, "a": 0, "p": 0}

    def nextF():
        i = cnt["f"] % 4
        cnt["f"] += 1
        return psF[i], ("psF", i)

    def nextA():
        i = cnt["a"] % 2
        cnt["a"] += 1
        return psA[i], ("psA", i)

    def nextB():
        i = cnt["b"] % 2
        cnt["b"] += 1
        return psB[i], ("psB", i)

    def nextW():
        i = cnt["w"] % NW
        cnt["w"] += 1
        return wslot[i], ("w", i)

    def nextP():
        i = cnt["p"] % 2
        cnt["p"] += 1
        return Pt[i], ("Pt", i)
    def dma(q, out, in_, reads, writes, key):
        if isinstance(key, str) and key.startswith("c_"):
            key = "c_" + q
        S.add(q, lambda e: e.dma_start(out=out, in_=in_), reads, writes, dma=key)

    def mm(out, lhsT, rhs, st, sp, reads, writes):
        S.add("pe", lambda e: e.matmul(out, lhsT=lhsT, rhs=rhs, start=st, stop=sp), reads, writes)

    def tp(out, in_, idn, reads, writes):
        S.add("pe", lambda e: e.transpose(out=out, in_=in_, identity=idn), reads, writes)

    def act(out, in_, func, reads, writes, **kw):
        S.add("act", lambda e: e.activation(out=out, in_=in_, func=func, **kw), reads, writes)

    def vcopy(out, in_, reads, writes, eng="dve"):
        S.add(eng, lambda e: e.tensor_copy(out=out, in_=in_), reads, writes)

    def vtt(out, a, b, op, reads, writes, eng="dve"):
        S.add(eng, lambda e: e.tensor_tensor(out=out, in0=a, in1=b, op=op), reads, writes)

    def vts(out, a, s1, s2, op0, op1, reads, writes, eng="dve"):
        S.add(eng, lambda e: e.tensor_scalar(out=out, in0=a, scalar1=s1, scalar2=s2, op0=op0, op1=op1), reads, writes)

    def vstt(out, a, s, b, op0, op1, reads, writes):
        S.add("dve", lambda e: e.scalar_tensor_tensor(out=out, in0=a, scalar=s, in1=b, op0=op0, op1=op1), reads, writes)

    def vrecip(out, in_, reads, writes):
        S.add("dve", lambda e: e.reciprocal(out=out, in_=in_), reads, writes)

    def memset(ap, v, writes, eng="dve"):
        S.add(eng, lambda e: e.memset(ap, v), (), writes)


    dma("sync", ident_f[:], ident_d, (), ["ident_f"], "c_identf")
    dma("pool", ident_b[:], ident_d, (), ["ident_b"], "c_identb")
    dma("sync", tri_f[:], tri_d, (), ["tri_f"], "c_tri")
    dma("sync", ones_f[:], ones_d, (), ["ones_f"], "c_ones")
    for nm in G:
        dma("sync", gains[nm][:], G[nm], (), ["G_" + nm], "c_" + nm)
    dma("sync", bfor[:], bfor_d, (), ["bfor"], "c_bfor")
    dma("sync", bgate[:], bgate_d, (), ["bgate"], "c_bgate")
    dma("pool", wa2[:], W["w_a2"], (), ["wa2"], "c_wa2")
    dma("sync", rmask[:], rmask_d, (), ["rmask"], "c_rmask")
    dma("sync", ggla[:], ggla_d, (), ["ggla"], "c_ggla")
    dma("sync", gmask[:], gmask_d, (), ["gmask"], "c_gmask")
    dma("pool", cmask[:], cmask_d, (), ["cmask"], "c_cmask")
    dma("pool", esel[:], esel_d, (), ["esel"], "c_esel")
    dma("pool", qsel[:], qsel_d, (), ["qsel"], "c_qsel")
    dma("pool", smask[:], smask_new_d.rearrange("p b t -> p (b t)"), (), ["smask"], "c_smask")
    dma("sync", colm[:], colm_d.rearrange("p b t -> p (b t)"), (), ["colm"], "c_colm")
    dma("sync", rowm[:], rowm_d, (), ["rowm"], "c_rowm")
    dma("sync", gmask_s[:], gmask_s_d, (), ["gmask_s"], "c_gmask_s")
    dma("sync", pcol[:], pcol_d, (), ["pcol"], "c_pcol")
    memset(OnT_s[:], 1.0, [("OnT_s", b) for b in range(4)])
    vts(nbgate[:], bgate[:], -1.0, None, ALU.mult, ALU.bypass, ["bgate"], ["nbgate"])
    memset(KTst[:], 1.0, ["KTst"])
    memset(hT[0:1, 0, :], 0.0, [("hT", 0)])
    memset(hT[0:1, 1, :], NEG, [("hT", 1)])
    dma("sync", KTst[67:68, :], hT[0:1, 0, :], [("hT", 0)], ["KTst"], "c_k67")
    for h in range(8):
        dma("sync", Qext[67:68, h, :], hT[0:1, 1, :], [("hT", 1)], [("Qext", h)], "c_q67")
    memset(Srun[:], 0.0, ["Srun"])
    dma("sync", wsel[:], wsel_d, (), ["wsel"], "c_wsel")
    memset(Vst[:], 1.0, ["Vst"])
    memset(Crun[:], 0.0, ["Crun"])
    memset(Sg[:], 0.0, ["Sg"])
    memset(Sgb[:], 0.0, ["Sgb"])

    Wb = {}
    for nm, shp in (("w_gu1", [D, 2 * FF]), ("w_dn1", [FF, D]), ("w_in", [D, PW]), ("w_out", [D, D]),
                    ("w_gu2", [D, 2 * FF]), ("w_dn2", [FF, D])):
        Wb[nm] = dscr(nm + "_b", shp, BF16)
        for r0 in range(0, shp[0], 128):
            r1 = min(shp[0], r0 + 128)
            dma("pool", Wb[nm][r0:r1, :], W[nm][r0:r1, :], (), [("Wb", nm)], ("wcast", nm))
    WB = {nm: Wb[nm].ap() for nm in Wb}

    NG = NS
    groups = []
    for gi in range(NG):
        blks = []
        for j in range(4):
            blks.append(dict(kind="p", row0=gi * 512 + j * 128, n=128, yi=j, col0=j * 128, tile=gi, kb=j))
        if gi == NG - 1:
            blks.append(dict(kind="s", row0=0, n=32, yi=4, col0=512, tile=None, kb=0))
        groups.append(blks)

    def subtiles(gi):
        st = [dict(col0=0, tn=512, blks=groups[gi][0:4], kind="p", tile=gi, li=0)]
        if gi == NG - 1:
            st.append(dict(col0=512, tn=32, blks=groups[gi][4:5], kind="s", tile=None, li=1))
        return st

    def src_rows(b, prm, smp):
        return (prm if b["kind"] == "p" else smp)[b["row0"]:b["row0"] + b["n"], :]

    def xk(st):
        return [("xnT", st["col0"])] if st["tn"] == 32 else [("xnT", st["col0"] + j * 128) for j in range(4)]
    def rstd(src, n, srcks, col, dim=D, junk=None):
        act(tokC[0:n, 0:dim], src, AF.Square, list(srcks), ["tokC", ("ss", col)], accum_out=ss[0:n, col:col + 1])
        act(ss[0:n, col:col + 1], ss[0:n, col:col + 1], AF.Sqrt, [("ss", col)], [("ss", col)], scale=1.0 / dim, bias=1e-6)
        vrecip(ss[0:n, col:col + 1], ss[0:n, col:col + 1], [("ss", col)], [("ss", col)])

    def norm_to_T(src, n, srcks, gname, col0):
        rstd(src, n, srcks, 0)
        vstt(tokb[0:n, :], src, ss[0:n, 0:1], gains[gname][0:n, :], ALU.mult, ALU.mult,
             list(srcks) + [("ss", 0), "G_" + gname], ["tokb"])
        pb, pk = nextB()
        for c in range(8):
            tp(pb[:, c * 128:c * 128 + n], tokb[0:n, c * 128:(c + 1) * 128], ident_b[0:n, 0:n], ["tokb", "ident_b"], [pk])
        vcopy(xnT[:, :, col0:col0 + n], pb[:, :].rearrange("p (c t) -> p c t", c=8)[:, :, 0:n], [pk], [("xnT", col0)])

    def wload_cols(wname, lo, hi):
        ws, wk = nextW()
        n = hi - lo
        v = ws[:, 0:8 * n].rearrange("p (k n) -> p k n", k=8)
        dma("sync", v, WB[wname].rearrange("(k p) n -> p k n", p=128)[:, :, lo:hi], [("Wb", wname)], [wk], wk)
        return v, wk

    def ffn(gi, wgu, wdn):
        sts = subtiles(gi)
        for s in range(11):
            f0 = s * 256
            nf = min(256, FF - f0)
            chunks = [(c * 128, min(128, nf - c * 128)) for c in range((nf + 127) // 128)]
            ws, wk = nextW()
            wg = ws[:, 0:4096].rearrange("p (k t n) -> p k t n", k=8, t=2)
            for t in range(2):
                dma("sync", wg[:, :, t, 0:nf], WB[wgu].rearrange("(k p) n -> p k n", p=128)[:, :, t * FF + f0:t * FF + f0 + nf],
                    [("Wb", wgu)], [wk], wk)
            wd_s, wdk = nextW()
            wd = wd_s[:, 0:2048].rearrange("p (c n) -> p c n", c=2)
            for ci, (c0, cm) in enumerate(chunks):
                dma("sync", wd[0:cm, ci, :], WB[wdn][f0 + c0:f0 + c0 + cm, :], [("Wb", wdn)], [wdk], wdk)
            for st in sts:
                c0t, tn = st["col0"], st["tn"]
                for ci, (c0, cm) in enumerate(chunks):
                    pg, pgk = nextF()
                    pu, puk = nextF()
                    for k in range(8):
                        mm(pg[0:cm, 0:tn], wg[:, k, 0, c0:c0 + cm], xnT[:, k, c0t:c0t + tn], k == 0, k == 7, [wk] + xk(st), [pgk])
                    for k in range(8):
                        mm(pu[0:cm, 0:tn], wg[:, k, 1, c0:c0 + cm], xnT[:, k, c0t:c0t + tn], k == 0, k == 7, [wk] + xk(st), [puk])
                    act(sg[0:cm, 0:tn], pg[0:cm, 0:tn], AF.Silu, [pgk], ["sg"])
                    vtt(hT[0:cm, ci, 0:tn], sg[0:cm, 0:tn], pu[0:cm, 0:tn], ALU.mult, ["sg", puk], [("hT", ci)])
                for bi, b in enumerate(st["blks"]):
                    n = b["n"]
                    pd = [nextF(), nextF()]
                    for half in range(2):
                        for ci, (c0, cm) in enumerate(chunks):
                            mm(pd[half][0][0:n, :], hT[0:cm, ci, bi * 128:bi * 128 + n], wd[0:cm, ci, half * 512:(half + 1) * 512],
                               ci == 0, ci == len(chunks) - 1, [("hT", ci), wdk], [pd[half][1]])
                    for half in range(2):
                        dst = yacc[0:n, b["yi"], half * 512:(half + 1) * 512]
                        if s == 0:
                            vcopy(dst, pd[half][0][0:n, :], [pd[half][1]], [("yacc", b["yi"], half)])
                        else:
                            vtt(dst, dst, pd[half][0][0:n, :], ALU.add, [("yacc", b["yi"], half), pd[half][1]],
                                [("yacc", b["yi"], half)])

    def post_res(b, gname, scale, out_ap, outk):
        n = b["n"]
        ya = yacc[0:n, b["yi"], :]
        yk = [("yacc", b["yi"], 0), ("yacc", b["yi"], 1)]
        rstd(ya, n, yk, 1)
        vstt(tokC[0:n, :], ya, ss[0:n, 1:2], gains[gname][0:n, :], ALU.mult, ALU.mult, yk + [("ss", 1), "G_" + gname], ["tokC"])
        vstt(out_ap, tokC[0:n, :], scale, hres[0:n, b["yi"], :], ALU.mult, ALU.add, ["tokC", ("hres", b["yi"])], [outk])

    WIN = W["w_in"]
    for gi in range(NG):
        sts = subtiles(gi)
        T = gi
        nkb = 16 * T + 16
        for b in groups[gi]:
            n = b["n"]
            hk = ("hres", b["yi"])
            dma("sync", hres[0:n, b["yi"], :], src_rows(b, xp, xs), (), [hk], ("hres_ld", b["yi"]))
            norm_to_T(hres[0:n, b["yi"], :], n, [hk], "g1pre", b["col0"])
        ffn(gi, "w_gu1", "w_dn1")
        for b in groups[gi]:
            n = b["n"]
            hk = ("hres", b["yi"])
            post_res(b, "g1post", 0.5, hres[0:n, b["yi"], :], hk)
            norm_to_T(hres[0:n, b["yi"], :], n, [hk], "gmpre", b["col0"])
        if STAGE < 1:
            for b in groups[gi]:
                n = b["n"]
                dma("sync", src_rows(b, y_p, y_s), hres[0:n, b["yi"], :], [("hres", b["yi"])], (), ("y_st", b["yi"]))
            continue
        st = sts[0]
        w1, w1k = wload_cols("w_in", 0, 512)
        w1b, w1bk = wload_cols("w_in", 512, 1024)
        for h in range(8):
            pq, pqk = nextF()
            for k in range(8):
                mm(pq[0:64, :], w1[:, k, h * 64:(h + 1) * 64], xnT[:, k, 0:512], k == 0, k == 7, [w1k] + xk(st), [pqk])
            vts(Qext[0:64, h, :], pq[0:64, :], 0.125, None, ALU.mult, ALU.bypass, [pqk], [("Qext", h)])
            pk_, pkk = nextF()
            for k in range(8):
                mm(pk_[0:64, :], w1b[:, k, h * 64:(h + 1) * 64], xnT[:, k, 0:512], k == 0, k == 7, [w1bk] + xk(st), [pkk])
            vcopy(KTst[0:64, :], pk_[0:64, :], [pkk], ["KTst"])
            dma("sync", kt_in[T][h * 68:(h + 1) * 68, :], KTst[:, :], ["KTst"], [("kt_in", T)], "KTst_st")
        for b in st["blks"]:
            pk_, pkk = nextF()
            for k in range(8):
                mm(pk_[:, :], xnT[:, k, b["col0"]:b["col0"] + 128], w1b[:, k, :], k == 0, k == 7, [w1bk, ("xnT", b["col0"])], [pkk])
            vcopy(otmp[:, :], pk_[:, :], [pkk], ["otmp"])
            dma("sync", nk_p[b["row0"]:b["row0"] + 128, :], otmp[:, :], ["otmp"], (), "otmp_st")
        if len(sts) > 1:
            pk_, pkk = nextF()
            for k in range(8):
                mm(pk_[0:32, :], xnT[:, k, 512:544], w1b[:, k, :], k == 0, k == 7, [w1bk, ("xnT", 512)], [pkk])
            vcopy(otmp[0:32, :], pk_[0:32, :], [pkk], ["otmp"])
            dma("sync", nk_s[:, :], otmp[0:32, :], ["otmp"], (), "otmp_st")
        w2, w2k = wload_cols("w_in", 1024, 1536)
        w2f, w2fk = wload_cols("w_in", 1536, 1544)
        for b in st["blks"]:
            j = b["kb"]
            pv, pvk = nextF()
            for k in range(8):
                mm(pv[:, :], xnT[:, k, b["col0"]:b["col0"] + 128], w2[:, k, 0:512], k == 0, k == 7, [w2k, ("xnT", b["col0"])], [pvk])
            pf, pfk = nextF()
            for k in range(8):
                mm(pf[:, 0:8], xnT[:, k, b["col0"]:b["col0"] + 128], w2f[:, k, 0:8], k == 0, k == 7, [w2fk, ("xnT", b["col0"])], [pfk])
            vcopy(otmp[:, :], pv[:, :], [pvk], ["otmp"])
            dma("sync", nv_p[b["row0"]:b["row0"] + 128, :], otmp[:, :], ["otmp"], (), "otmp_st")
            vcopy(Vst[:, j, :, 0:64], pv[:, :].rearrange("p (h d) -> p h d", h=8), [pvk], [("Vst", j)])
            vtt(lft[:, :], pf[:, 0:8], bfor[:, :], ALU.add, [pfk, "bfor"], ["lft"])
            act(lft[:, :], lft[:, :], AF.Exp, ["lft"], ["lft"], scale=-1.0)
            act(lft[:, :], lft[:, :], AF.Ln, ["lft"], ["lft"], bias=1.0)
            vts(lf[:, j, :], lft[:, :], -1.0, None, ALU.mult, ALU.bypass, ["lft"], [("lf", j)])
            dma("sync", nlf_p[b["row0"]:b["row0"] + 128, :], lf[:, j, :], [("lf", j)], (), ("lf_st", j))
        if len(sts) > 1:
            pv, pvk = nextF()
            for k in range(8):
                mm(pv[0:32, :], xnT[:, k, 512:544], w2[:, k, 0:512], k == 0, k == 7, [w2k, ("xnT", 512)], [pvk])
            pf, pfk = nextF()
            for k in range(8):
                mm(pf[0:32, 0:8], xnT[:, k, 512:544], w2f[:, k, 0:8], k == 0, k == 7, [w2fk, ("xnT", 512)], [pfk])
            vcopy(otmp[0:32, :], pv[0:32, :], [pvk], ["otmp"])
            dma("sync", nv_s[:, :], otmp[0:32, :], ["otmp"], (), "otmp_st")
            vtt(lft[0:32, :], pf[0:32, 0:8], bfor[0:32, :], ALU.add, [pfk, "bfor"], ["lft"])
            act(lft[0:32, :], lft[0:32, :], AF.Exp, ["lft"], ["lft"], scale=-1.0)
            act(lft[0:32, :], lft[0:32, :], AF.Ln, ["lft"], ["lft"], bias=1.0)
            vts(lft[0:32, :], lft[0:32, :], -1.0, None, ALU.mult, ALU.bypass, ["lft"], ["lft"])
            dma("sync", nlf_s[:, :], lft[0:32, :], ["lft"], (), "lft_st")
        v_in_v = v_in[T].ap().rearrange("(h p b) e -> h p b e", h=8, p=128)
        for h in range(8):
            dma("sync", v_in_v[h], Vst[:, :, h, :], [("Vst", j) for j in range(4)], [("v_in", T)], ("Vst_st", h))
        lfk = [("lf", j) for j in range(4)]
        Cloc_t = yacc[:, 1, 512:544].rearrange("p (j h) -> p j h", h=8)
        for j in range(4):
            pc, pck = nextF()
            mm(pc[:, 0:8], tri_f[:, :], lf[:, j, :], True, j == 0, ["tri_f"] + lfk, [pck])
            for jj in range(j):
                mm(pc[:, 0:8], ones_f[:, :], lf[:, jj, :], False, jj == j - 1, ["ones_f"] + lfk, [pck])
            vcopy(Cloc_t[:, j, :], pc[:, 0:8], [pck], [("yacc", 1, 1)])
        dma("sync", f_in[T][128:256, 0:32], yacc[:, 1, 512:544], [("yacc", 1, 1)], [("f_in", T)], "cloc_st")
        pr, prk = nextF()
        for j in range(4):
            mm(pr[0:8, j * 128:(j + 1) * 128], lf[:, j, :], tri_f[:, :], True, j == 0, ["tri_f"] + lfk, [prk])
            for jj in range(j):
                mm(pr[0:8, j * 128:(j + 1) * 128], lf[:, jj, :], ones_f[:, :], False, jj == j - 1, ["ones_f"] + lfk, [prk])
        crl = hT[0:8, 0:3, :]
        CRK = [("hT", 0), ("hT", 1), ("hT", 2)]
        crf = otmp[0:8, :]
        crg = tokC[0:8, 0:512]
        vcopy(crl[:, 0, :], pr[0:8, :], [prk], CRK)
        vtt(crf, pr[0:8, :], crl[:, 0, :], ALU.subtract, [prk] + CRK, ["otmp"])
        vcopy(crl[:, 1, :], crf, ["otmp"], CRK)
        vtt(crg, crf, crl[:, 1, :], ALU.subtract, ["otmp"] + CRK, ["tokC"])
        vcopy(crl[:, 2, :], crg, ["tokC"], CRK)
        for h in range(8):
            for r in range(3):
                dma("sync", Qext[64 + r:65 + r, h, :], crl[h:h + 1, r, :], CRK, [("Qext", h)], ("Qx", h))
        w5, w5k = wload_cols("w_in", 2568, 3080)
        w5a, w5ak = wload_cols("w_in", 3080, 3096)
        for b in st["blks"]:
            j = b["kb"]
            pg, pgk = nextF()
            for k in range(8):
                mm(pg[:, :], xnT[:, k, b["col0"]:b["col0"] + 128], w5[:, k, 0:512], k == 0, k == 7, [w5k, ("xnT", b["col0"])], [pgk])
            act(gate_tok[:, j, :], pg[:, :], AF.Silu, [pgk], [("gate", j)])
        pa, pak = nextF()
        for k in range(8):
            mm(pa[0:16, :], w5a[:, k, 0:16], xnT[:, k, 0:512], k == 0, k == 7, [w5ak] + xk(st), [pak])
        vcopy(gaT[:, :], pa[0:16, :], [pak], ["gaT"])
        w3, w3k = wload_cols("w_in", 1544, 2056)
        for h in range(4):
            px, pxk = nextF()
            mm(px[0:64, :], wa2[0:16, h * 64:(h + 1) * 64], gaT[0:16, :], True, True, ["wa2", "gaT"], [pxk])
            act(sp[:, :], px[0:64, :], AF.Exp, [pxk, "nbgate"], ["sp"], scale=-1.0, bias=nbgate[:, h:h + 1])
            act(sp[:, :], sp[:, :], AF.Ln, ["sp"], ["sp"], bias=1.0)
            S.add("dve", (lambda e: e.tensor_tensor_scan(out=cs[:, :], data0=rmask[:, 0:512], data1=sp[:, :], initial=0.0,
                                                          op0=ALU.mult, op1=ALU.add)), ["rmask", "sp"], ["cs"])
            act(eb[:, :], cs[:, :], AF.Exp, ["cs"], ["eb"], scale=-1.0 / 16)
            act(enb[:, :], cs[:, :], AF.Exp, ["cs"], ["enb"], scale=1.0 / 16)
            vcopy(ebl[:, h, :], eb[:, :].rearrange("p (c t) -> p c t", c=4)[:, :, 127], ["eb"], [("ebl", h)])
            pq, pqk = nextF()
            for k in range(8):
                mm(pq[0:64, :], w3[:, k, h * 64:(h + 1) * 64], xnT[:, k, 0:512], k == 0, k == 7, [w3k] + xk(st), [pqk])
            vstt(gqT[:, h, :], pq[0:64, :], 0.125, eb[:, :], ALU.mult, ALU.mult, [pqk, "eb"], [("gqT", h)])
            pk_, pkk = nextF()
            for k in range(8):
                mm(pk_[0:64, :], w3[:, k, 256 + h * 64:256 + (h + 1) * 64], xnT[:, k, 0:512], k == 0, k == 7, [w3k] + xk(st), [pkk])
            vtt(gkT[:, h, :], pk_[0:64, :], enb[:, :], ALU.mult, [pkk, "enb"], [("gkT", h)])
            for c in range(4):
                vts(kdT[:, h, c * 128:(c + 1) * 128], gkT[:, h, c * 128:(c + 1) * 128], ebl[:, h, c:c + 1], None,
                    ALU.mult, ALU.bypass, [("gkT", h), ("ebl", h)], [("kdT", h)])
        for c in range(4):
            pb, pbk = nextB()
            for h in range(4):
                tp(pb[:, h * 64:(h + 1) * 64], kdT[:, h, c * 128:(c + 1) * 128], ident_b[0:64, 0:64], [("kdT", h), "ident_b"], [pbk])
            vcopy(kd_tok[:, c, :, :], pb[:, 0:256].rearrange("p (h k) -> p h k", h=4), [pbk], [("kd_tok", c)])
        w4, w4k = wload_cols("w_in", 2056, 2568)
        for b in st["blks"]:
            j = b["kb"]
            pv, pvk = nextF()
            for k in range(8):
                mm(pv[:, :], xnT[:, k, b["col0"]:b["col0"] + 128], w4[:, k, :], k == 0, k == 7, [w4k, ("xnT", b["col0"])], [pvk])
            vcopy(vg_tok[:, j, :], pv[:, :], [pvk], [("vg_tok", j)])
        if STAGE < 2:
            continue
        Sloc = yacc[0:64, 0, 512:1024].rearrange("p (h v) -> p h v", h=4)
        SLK = [("yacc", 0, 1)]
        for c in range(4):
            pS, pSk = nextF()
            for h in range(4):
                mm(pS[0:64, h * 128:(h + 1) * 128], kd_tok[:, c, h, :], vg_tok[:, c, h * 128:(h + 1) * 128], True, True,
                   [("kd_tok", c), ("vg_tok", c)], [pSk])
            if c == 0:
                vcopy(yacc[0:64, 0, 512:1024], pS[0:64, :], [pSk], SLK)
            else:
                for h in range(4):
                    vstt(Sloc[:, h, :], Sloc[:, h, :], ebl[:, h, c:c + 1], pS[0:64, h * 128:(h + 1) * 128], ALU.mult, ALU.add,
                         SLK + [("ebl", h), pSk], SLK)
        dma("sync", f_in[T][0:64, :], yacc[0:64, 0, 512:1024], SLK, [("f_in", T)], "sloc_st")
        eBt = yacc[0:64, 2, 512:516]
        EK = [("ebl", h) for h in range(4)]
        vtt(eBt, ebl[:, :, 0], ebl[:, :, 1], ALU.mult, EK, [("yacc", 2, 1)])
        vtt(eBt, eBt, ebl[:, :, 2], ALU.mult, EK + [("yacc", 2, 1)], [("yacc", 2, 1)])
        vtt(eBt, eBt, ebl[:, :, 3], ALU.mult, EK + [("yacc", 2, 1)], [("yacc", 2, 1)])
        dma("sync", f_in[T][64:128, 0:4], eBt, [("yacc", 2, 1)], [("f_in", T)], "ebt_st")
        RG = [[0, 1, 2, 3], [4, 5, 6, 7]]
        for nm, src, dst in (("kt", kt_in[T], ktg[T]), ("v", v_in[T], vgt[T]), ("f", f_in[T], fgt[T])):
            S.add("pool", (lambda e, src=src, dst=dst: e.collective_compute("AllGather", ALU.bypass, replica_groups=RG,
                                                                             ins=[src.ap().opt()], outs=[dst.ap().opt()])),
                  [(nm + "_in" if nm != "f" else "f_in", T)], [(nm + "g", T)], dma=("cc", nm), inc=1)
        Clg = yacc[:, 1, 0:128].rearrange("p (q h) -> p q h", h=8)
        CLK = [("yacc", 1, 0)]
        for gq in range(4):
            dma("sync", yacc[:, 1, gq * 32:(gq + 1) * 32], fgt[T][gq * 256 + 128:gq * 256 + 256, 0:32], [("fg", T)], CLK, "clg_ld")
        Coffs = yacc[:, 2, 0:40].rearrange("p (q h) -> p q h", h=8)
        CFK = [("yacc", 2, 0)]
        vcopy(Coffs[:, 0, :], Crun[:, :], ["Crun"], CFK)
        for gq in range(4):
            vts(otmp[:, 0:8], Clg[:, gq * 4 + 3, :], pcol[:, 1:2], None, ALU.mult, ALU.bypass, CLK + ["pcol"], ["otmp"])
            pt_, ptk = nextF()
            mm(pt_[:, 0:8], ones_f[:, :], otmp[:, 0:8], True, True, ["ones_f", "otmp"], [ptk])
            vtt(Coffs[:, gq + 1, :], Coffs[:, gq, :], pt_[:, 0:8], ALU.add, CFK + [ptk], CFK)
        for gq in range(4):
            for bl in range(4):
                qi = 16 * T + 4 * gq + bl
                vtt(Cglob[:, qi, :], Clg[:, gq * 4 + bl, :], Coffs[:, gq, :], ALU.add, CLK + CFK, [("Cglob", qi)])
        vts(Cst[:, T, :], Coffs[:, 0, :], wsel[:, 0:1], None, ALU.mult, ALU.bypass, CFK + ["wsel"], [("Cst", T)])
        for gq in range(1, 4):
            vstt(Cst[:, T, :], Coffs[:, gq, :], wsel[:, gq:gq + 1], Cst[:, T, :], ALU.mult, ALU.add, CFK + ["wsel", ("Cst", T)], [("Cst", T)])
        vcopy(Crun[:, :], Coffs[:, 4, :], CFK, ["Crun"])
        stg = yacc[0:64, 0, 0:512].rearrange("p (h v) -> p h v", h=4)
        STK = [("yacc", 0, 0)]
        ebs = yacc[0:64, 2, 516:520]
        for gq in range(4):
            dma("sync", yacc[0:64, 0, 0:512], fgt[T][gq * 256:gq * 256 + 64, :], [("fg", T)], STK, "stg_ld")
            dma("sync", ebs, fgt[T][gq * 256 + 64:gq * 256 + 128, 0:4], [("fg", T)], [("yacc", 2, 1)], "ebs_ld")
            if gq == 0:
                vts(Sg[:, :, :].rearrange("p h v -> p (h v)"), Srun[:, :, :].rearrange("p h v -> p (h v)"), wsel[0:64, 0:1], None,
                    ALU.mult, ALU.bypass, ["Srun", "wsel"], ["Sg"])
            else:
                vstt(Sg[:, :, :].rearrange("p h v -> p (h v)"), Srun[:, :, :].rearrange("p h v -> p (h v)"), wsel[0:64, gq:gq + 1],
                     Sg[:, :, :].rearrange("p h v -> p (h v)"), ALU.mult, ALU.add, ["Srun", "wsel", "Sg"], ["Sg"])
            for h in range(4):
                vstt(Srun[:, h, :], Srun[:, h, :], ebs[:, h:h + 1], stg[:, h, :], ALU.mult, ALU.add,
                     ["Srun", ("yacc", 2, 1)] + STK, ["Srun"])
        vcopy(Sgb[:, :, :], Sg[:, :, :], ["Sg"], ["Sgb"])
        for c in range(4):
            pA, pAk = nextF()
            for h in range(4):
                mm(pA[:, h * 128:(h + 1) * 128], gkT[:, h, c * 128:(c + 1) * 128], gqT[:, h, c * 128:(c + 1) * 128], True, True,
                   [("gkT", h), ("gqT", h)], [pAk])
            vtt(Am[:, :], pA[:, :], gmask[:, :], ALU.mult, [pAk, "gmask"], ["Am"])
            po, pok = nextF()
            for h in range(4):
                mm(po[:, h * 128:(h + 1) * 128], Am[:, h * 128:(h + 1) * 128], vg_tok[:, c, h * 128:(h + 1) * 128], True, False,
                   ["Am", ("vg_tok", c)], [pok])
                mm(po[:, h * 128:(h + 1) * 128], gqT[:, h, c * 128:(c + 1) * 128], Sgb[:, h, :], False, True,
                   [("gqT", h), "Sgb"], [pok])
            for h in range(4):
                act(tokC[:, h * 128:(h + 1) * 128], po[:, h * 128:(h + 1) * 128], AF.Square, [pok], ["tokC", ("ss", 4 + h)],
                    accum_out=ss[:, 4 + h:5 + h])
            act(ss[:, 4:8], ss[:, 4:8], AF.Sqrt, [("ss", 4 + h) for h in range(4)], [("ss", 4 + h) for h in range(4)],
                scale=1.0 / 128, bias=1e-6)
            vrecip(ss[:, 4:8], ss[:, 4:8], [("ss", 4 + h) for h in range(4)], [("ss", 4 + h) for h in range(4)])
            for h in range(4):
                vstt(otmp[:, h * 128:(h + 1) * 128], po[:, h * 128:(h + 1) * 128], ss[:, 4 + h:5 + h], ggla[:, h * 128:(h + 1) * 128],
                     ALU.mult, ALU.mult, [pok, ("ss", 4 + h), "ggla"], ["otmp"])
            vtt(mix_tok[:, :], otmp[:, :], gate_tok[:, c, :], ALU.mult, ["otmp", ("gate", c)], ["mix_tok"])
            pb, pbk = nextB()
            for q in range(4):
                tp(pb[:, q * 128:(q + 1) * 128], mix_tok[:, q * 128:(q + 1) * 128], ident_b[:, :], ["mix_tok", "ident_b"], [pbk])
            vcopy(mgT[:, :, c * 128:(c + 1) * 128], pb[:, 0:512].rearrange("p (q t) -> p q t", q=4), [pbk], [("mgT", c)])
            pS, pSk = nextF()
            for h in range(4):
                mm(pS[0:64, h * 128:(h + 1) * 128], kd_tok[:, c, h, :], vg_tok[:, c, h * 128:(h + 1) * 128], True, True,
                   [("kd_tok", c), ("vg_tok", c)], [pSk])
            for h in range(4):
                vstt(Sg[:, h, :], Sg[:, h, :], ebl[:, h, c:c + 1], pS[0:64, h * 128:(h + 1) * 128], ALU.mult, ALU.add,
                     ["Sg", ("ebl", h), pSk], ["Sg"])
            vcopy(Sgb[:, :, :], Sg[:, :, :], ["Sg"], ["Sgb"])
        if gi == NG - 1:
            dma("sync", gla_p, Srun[:, :, :].rearrange("p h v -> p (h v)"), ["Srun"], (), "gla_p_st")
        for h in range(8):
            vts(biasT[:, 0:nkb], Cglob[:, 0:nkb, h], Cst[:, T, h:h + 1], -1.0, ALU.subtract, ALU.mult,
                [("Cglob", q) for q in range(nkb)] + [("Cst", T)], ["biasT"])
            vts(biasO[:, :], yacc[:, 1, 512:544].rearrange("p (j h) -> p j h", h=8)[:, :, h], -1.0, None, ALU.mult, ALU.bypass,
                [("yacc", 1, 1)], ["biasO"])
            kts, ktk = nextW()
            kts2, ktk2 = nextW()
            for j in range(T + 1):
                dst_s, dk = (kts, ktk) if j < 2 else (kts2, ktk2)
                dcol = (j % 2) * 2048
                dma("sync", dst_s[0:68, dcol:dcol + 2048].rearrange("p (g t) -> p g t", g=4),
                    ktg[j].ap().rearrange("(g h r) t -> g h r t", g=4, h=8)[:, h, :, :].rearrange("g r t -> r g t"),
                    [("ktg", j)], [dk], dk)
            cur_s, ck_ = (kts, ktk) if T < 2 else (kts2, ktk2)
            ccol = (T % 2) * 2048
            dma("pool", cur_s[67:68, ccol:ccol + 2048], mrow_d, (), [ck_], ck_)
            if T < 2:
                dma("sync", kts2[0:1, 0:8], ktg[0][0:1, 0:8], [("ktg", 0)], [ktk2], ktk2)
            vs_, vk = nextW()
            vv = vs_[:, 0:nkb * 65].rearrange("p (b e) -> p b e", e=65)
            for j in range(T + 1):
                vsrc = vgt[j].ap().rearrange("(g h p b) e -> g h p b e", g=4, h=8, p=128)
                for gq in range(4):
                    dma("sync", vv[:, 16 * j + 4 * gq:16 * j + 4 * gq + 4, :], vsrc[gq, h], [("vg", j)], [vk], vk)
            os_, ok_ = nextW()
            ko = os_[0:68, 0:512]
            vo = os_[:, 512:772].rearrange("p (b e) -> p b e", e=65)
            dma("sync", ko, kt_in[T][h * 68:(h + 1) * 68, :], [("kt_in", T)], [ok_], ok_)
            dma("sync", vo, v_in[T].ap().rearrange("(h p b) e -> h p b e", h=8, p=128)[h], [("v_in", T)], [ok_], ok_)
            po, pok = nextA()
            for kb in range(nkb):
                pS, pSk = nextF()
                if kb < 32:
                    ksrc, kk_ = kts[0:68, kb * 128:(kb + 1) * 128], ktk
                else:
                    ksrc, kk_ = kts2[0:68, (kb - 32) * 128:(kb - 31) * 128], ktk2
                mm(pS[:, :], ksrc, Qext[0:68, h, :], True, True, [kk_, ("Qext", h)], [pSk])
                pt, ptk = nextP()
                act(pt[:, :], pS[:, :], AF.Exp, [pSk, "biasT"], [ptk], bias=biasT[:, kb:kb + 1])
                mm(po[0:65, :], vv[:, kb, :], pt[:, :], kb == 0, False, [vk, ptk], [pok])
            for kb in range(4):
                pS, pSk = nextF()
                mm(pS[:, :], ko[:, kb * 128:(kb + 1) * 128], Qext[0:68, h, :], True, False, [ok_, ("Qext", h)], [pSk])
                mm(pS[:, :], ident_b[:, :], cmask[:, kb, :], False, True, ["ident_b", "cmask"], [pSk])
                pt, ptk = nextP()
                act(pt[:, :], pS[:, :], AF.Exp, [pSk, "biasO"], [ptk], bias=biasO[:, kb:kb + 1])
                mm(po[0:65, :], vo[:, kb, :], pt[:, :], False, kb == 3, [ok_, ptk], [pok])
            vcopy(Osb[:, :], po[0:65, :], [pok], ["Osb"])
            vrecip(Osb[64:65, :], Osb[64:65, :], ["Osb"], ["Osb"])
            pbc, pbck = nextF()
            mm(pbc[0:65, :], ones_f[64:65, 0:65], Osb[64:65, :], True, True, ["ones_f", "Osb"], [pbck])
            vtt(OnT[:, h, :], Osb[:, :], pbc[0:65, :], ALU.mult, ["Osb", pbck], [("OnT", h)])
        if STAGE < 3:
            continue
        if len(sts) > 1 and not os.environ.get("MK_NOSAMPLE"):
            XS = [("xnT", 512)]
            xsT = lambda k: xnT[:, k, 512:544]
            w1, w1k = wload_cols("w_in", 0, 512)
            w1b, w1bk = wload_cols("w_in", 512, 1024)
            for j in range(4):
                pq, pqk = nextF()
                for k in range(8):
                    mm(pq[:, 0:32], w1[:, k, j * 128:(j + 1) * 128], xsT(k), k == 0, k == 7, [w1k] + XS, [pqk])
                vts(qTs[:, j, :], pq[:, 0:32], 0.125, None, ALU.mult, ALU.bypass, [pqk], ["qTs"])
                pk_, pkk = nextF()
                for k in range(8):
                    mm(pk_[:, 0:32], w1b[:, k, j * 128:(j + 1) * 128], xsT(k), k == 0, k == 7, [w1bk] + XS, [pkk])
                vcopy(kTs[:, j, :], pk_[:, 0:32], [pkk], ["kTs"])
            w2, w2k = wload_cols("w_in", 1024, 1536)
            w2f, w2fk = wload_cols("w_in", 1536, 1544)
            pv, pvk = nextF()
            for k in range(8):
                mm(pv[0:32, :], xsT(k), w2[:, k, 0:512], k == 0, k == 7, [w2k] + XS, [pvk])
            vcopy(Vs_bf[:, :], pv[0:32, :], [pvk], ["Vs_bf"])
            pf, pfk = nextF()
            for k in range(8):
                mm(pf[0:32, 0:8], xsT(k), w2f[:, k, 0:8], k == 0, k == 7, [w2fk] + XS, [pfk])
            vtt(lfs[:, :], pf[0:32, 0:8], bfor[0:32, :], ALU.add, [pfk, "bfor"], ["lfs"])
            act(lfs[:, :], lfs[:, :], AF.Exp, ["lfs"], ["lfs"], scale=-1.0)
            act(lfs[:, :], lfs[:, :], AF.Ln, ["lfs"], ["lfs"], bias=1.0)
            pc, pck = nextF()
            mm(pc[0:8, 0:32], lfs[:, :], gmask_s[0:32, 0:32], True, True, ["lfs", "gmask_s"], [pck])
            vcopy(ncnT[:, :], pc[0:8, 0:32], [pck], ["ncnT"])
            w5, w5k = wload_cols("w_in", 2568, 3080)
            w5a, w5ak = wload_cols("w_in", 3080, 3096)
            pg, pgk = nextF()
            for k in range(8):
                mm(pg[0:32, :], xsT(k), w5[:, k, 0:512], k == 0, k == 7, [w5k] + XS, [pgk])
            act(gate_tok[0:32, 0, :], pg[0:32, :], AF.Silu, [pgk], [("gate", 0)])
            pa, pak = nextF()
            for k in range(8):
                mm(pa[0:16, 0:32], w5a[:, k, 0:16], xsT(k), k == 0, k == 7, [w5ak] + XS, [pak])
            vcopy(gaT[:, 0:32], pa[0:16, 0:32], [pak], ["gaT"])
            w3, w3k = wload_cols("w_in", 1544, 2056)
            for h in range(4):
                px, pxk = nextF()
                mm(px[0:64, 0:32], wa2[0:16, h * 64:(h + 1) * 64], gaT[0:16, 0:32], True, True, ["wa2", "gaT"], [pxk])
                act(sp[:, 0:32], px[0:64, 0:32], AF.Exp, [pxk, "nbgate"], ["sp"], scale=-1.0, bias=nbgate[:, h:h + 1])
                act(sp[:, 0:32], sp[:, 0:32], AF.Ln, ["sp"], ["sp"], bias=1.0)
                S.add("dve", (lambda e: e.tensor_tensor_scan(out=cs[:, 0:32], data0=rmask[:, 512:544], data1=sp[:, 0:32], initial=0.0,
                                                              op0=ALU.mult, op1=ALU.add)), ["rmask", "sp"], ["cs"])
                act(eb[:, 0:32], cs[:, 0:32], AF.Exp, ["cs"], ["eb"], scale=-1.0 / 16)
                act(enb[:, 0:32], cs[:, 0:32], AF.Exp, ["cs"], ["enb"], scale=1.0 / 16)
                vcopy(ebl[:, h, :], eb[:, 0:32].rearrange("p (c t) -> p c t", c=4)[:, :, 7], ["eb"], [("ebl", h)])
                pq, pqk = nextF()
                for k in range(8):
                    mm(pq[0:64, 0:32], w3[:, k, h * 64:(h + 1) * 64], xsT(k), k == 0, k == 7, [w3k] + XS, [pqk])
                vstt(gqT[:, h, 0:32], pq[0:64, 0:32], 0.125, eb[:, 0:32], ALU.mult, ALU.mult, [pqk, "eb"], [("gqT", h)])
                pk_, pkk = nextF()
                for k in range(8):
                    mm(pk_[0:64, 0:32], w3[:, k, 256 + h * 64:256 + (h + 1) * 64], xsT(k), k == 0, k == 7, [w3k] + XS, [pkk])
                vtt(gkT[:, h, 0:32], pk_[0:64, 0:32], enb[:, 0:32], ALU.mult, [pkk, "enb"], [("gkT", h)])
                for c in range(4):
                    vts(kdT[:, h, c * 8:(c + 1) * 8], gkT[:, h, c * 8:(c + 1) * 8], ebl[:, h, c:c + 1], None,
                        ALU.mult, ALU.bypass, [("gkT", h), ("ebl", h)], [("kdT", h)])
            pb, pbk = nextB()
            for h in range(4):
                tp(pb[0:32, h * 64:(h + 1) * 64], kdT[:, h, 0:32], ident_b[0:64, 0:64], [("kdT", h), "ident_b"], [pbk])
            vcopy(kd_tok[0:32, 0, :, :], pb[0:32, 0:256].rearrange("p (h k) -> p h k", h=4), [pbk], [("kd_tok", 0)])
            w4, w4k = wload_cols("w_in", 2056, 2568)
            pv, pvk = nextF()
            for k in range(8):
                mm(pv[0:32, :], xsT(k), w4[:, k, :], k == 0, k == 7, [w4k] + XS, [pvk])
            vcopy(vg_tok[0:32, 0, :], pv[0:32, :], [pvk], [("vg_tok", 0)])

            idx_i = otmp[:, :].bitcast(I32)
            triS = sg[:, 0:128]
            bmask = tokC[0:64, 0:512]
            KTg = hT
            HK = [("hT", c) for c in range(4)]
            dma("sync", idx_i, ptab_d, (), ["otmp"], "idx_ld")
            vts(idx_i, idx_i, 128.0, pcol[:, 0:1], ALU.mult, ALU.add, ["otmp", "pcol"], ["otmp"])
            vtt(triS, ones_f[:, :], tri_f[:, :], ALU.subtract, ["ones_f", "tri_f"], ["sg"])
            dma("sync", bmask, bmask_d, (), ["tokC"], "bmask_ld")
            idx4 = sg[:, 256:384].bitcast(I32)
            triS4 = sg[:, 384:512].rearrange("p (r n) -> p r n", r=4)
            dma("sync", idx4, ptab4_d, (), ["sg"], "idx4_ld")
            dma("sync", sg[:, 384:512], triS4_d, (), ["sg"], "tri4_ld")
            vts(idx4, idx4, 32.0, pcol[:, 2:3], ALU.mult, ALU.add, ["sg", "pcol"], ["sg"])
            ck4 = cache_k.rearrange("(r f) n -> r (f n)", f=4)
            cv4 = cache_v.rearrange("(r f) n -> r (f n)", f=4)

            def gather4(out_ap, src, col, writes, key):
                S.add("pool", lambda e: e.indirect_dma_start(out=out_ap, out_offset=None, in_=src,
                                                             in_offset=bass.IndirectOffsetOnAxis(ap=idx4[:, col:col + 1], axis=0)),
                      ["sg"], writes, dma=key)
            ck_rows, cv_rows, clf_rows = cache_k, cache_v, cache_lf

            def gather(out_ap, src, col, reads, writes, key):
                S.add("pool", lambda e: e.indirect_dma_start(out=out_ap, out_offset=None, in_=src,
                                                             in_offset=bass.IndirectOffsetOnAxis(ap=idx_i[:, col:col + 1], axis=0)),
                      ["otmp"] + list(reads), writes, dma=key)

            for bb in range(4):
                memset(Qbd[:], 0.0, ["Qbd"])
                for j in range(4):
                    vcopy(Qbd[0:64, j, (2 * j) * 8:(2 * j) * 8 + 8], qTs[0:64, j, bb * 8:(bb + 1) * 8], ["qTs"], ["Qbd"])
                    vcopy(Qbd[64:128, j, (2 * j + 1) * 8:(2 * j + 1) * 8 + 8], qTs[64:128, j, bb * 8:(bb + 1) * 8], ["qTs"], ["Qbd"])
                lb = 2 * (bb % 2)
                lfp = yacc[:, lb, :].rearrange("p (g h) -> p g h", h=8)
                prf = yacc[:, lb + 1, :].rearrange("p (g h) -> p g h", h=8)
                YK = lambda j, lb=lb: [("yacc", lb + j, 0), ("yacc", lb + j, 1)]
                nlb = 2 * ((bb + 1) % 2)
                lfp_n = yacc[:, nlb, :].rearrange("p (g h) -> p g h", h=8)
                YKn = [("yacc", nlb, 0), ("yacc", nlb, 1)]
                if bb == 0:
                    for pg_ in range(128):
                        gather(lfp[:, pg_, :], clf_rows, pg_, (), YK(0), ("lfp", pg_ % 8))
                for h in range(8):
                    S.add("dve", (lambda e, h=h, prf=prf, lfp=lfp: e.tensor_tensor_scan(out=prf[:, :, h], data0=ones_f[:, :], data1=lfp[:, :, h], initial=0.0,
                                                                       op0=ALU.mult, op1=ALU.add)), ["ones_f"] + YK(0), YK(1))
                for h in range(8):
                    vts(prf[:, :, h], prf[:, :, h], prf[:, 127, h:h + 1], -1.0, ALU.subtract, ALU.mult, YK(1), YK(1))
                pacc, pacck = nextA()
                for g in range(32):
                    ks_, kk = nextW()
                    kp = ks_[:, 0:2048].rearrange("p (a n) -> p a n", a=4)
                    vs2, vk2 = nextW()
                    vp = vs2[:, 0:2048].rearrange("p (a n) -> p a n", a=4)
                    gather4(ks_[:, 0:2048], ck4, bb * 32 + g, [kk], (kk, 0))
                    gather4(vs2[:, 0:2048], cv4, bb * 32 + g, [vk2], (vk2, 0))
                    if bb < 3:
                        for a in range(4):
                            gather(lfp_n[:, g * 4 + a, :], clf_rows, (bb + 1) * 128 + g * 4 + a, (), YKn, ("lfp", (g * 4 + a) % 8))
                    par = g % 2
                    KTg_, HK_ = (KTg, HK) if par == 0 else (xnT[:, 0:4, 0:512], [("xnT", 0), ("xnT", 128), ("xnT", 256), ("xnT", 384)] * 1)
                    afT_, afk = (gaT[0:8, :], "gaT") if par == 0 else (gate_tok[0:8, 1, :], ("gate", 1))
                    Pg_, pgk_ = (Am[0:64, :], "Am") if par == 0 else (tokb[0:64, 0:512], "tokb")
                    PT_, ptk_ = (PT[:, :, :], "PT") if par == 0 else (tokb[:, 512:768].rearrange("p (a n) -> p a n", a=4), "tokb")
                    pa, pak = nextF()
                    for r4 in range(4):
                        for pq in range(4):
                            pg_ = g * 4 + pq
                            oc = r4 * 128 + pq * 32
                            mm(pa[0:8, oc:oc + 32], lfp[:, pg_, :], triS4[:, r4, :], True, False, YK(0) + ["sg"], [pak])
                            mm(pa[0:8, oc:oc + 32], prf[:, pg_, :], ones_f[:, 0:32], False, True, YK(1) + ["ones_f"], [pak])
                    vcopy(afT_, pa[0:8, :], [pak], [afk])
                    for jj in range(2):
                        pb, pbk = nextB()
                        for j2 in range(2):
                            j = jj * 2 + j2
                            for a in range(4):
                                tp(pb[:, j2 * 512 + a * 128:j2 * 512 + (a + 1) * 128], kp[:, a, j * 128:(j + 1) * 128], ident_b[:, :],
                                   [kk, "ident_b"], [pbk])
                        vcopy(KTg_[:, jj * 2:jj * 2 + 2, :], pb[:, :].rearrange("p (j n) -> p j n", j=2), [pbk],
                              HK_[jj * 2:jj * 2 + 2] if par == 0 else HK_)
                    pS, pSk = nextF()
                    for j in range(4):
                        mm(pS[0:64, :], Qbd[:, j, :], KTg_[:, j, :], j == 0, False, ["Qbd"] + ([HK_[j]] if par == 0 else HK_), [pSk])
                    mm(pS[0:64, :], esel[0:8, :], afT_, False, True, ["esel", afk], [pSk])
                    act(Pg_, pS[0:64, :], AF.Exp, [pSk], [pgk_, ("dsum", g)], accum_out=dsum[:, g:g + 1])
                    pb, pbk = nextB()
                    for a in range(4):
                        tp(pb[:, a * 64:(a + 1) * 64], Pg_[:, a * 128:(a + 1) * 128], ident_b[0:64, 0:64], [pgk_, "ident_b"], [pbk])
                    vcopy(PT_, pb[:, 0:256].rearrange("p (a n) -> p a n", a=4), [pbk], [ptk_])
                    for a in range(4):
                        mm(pacc[0:64, :], PT_[:, a, :], vp[:, a, :], g == 0 and a == 0, False, [ptk_, vk2], [pacck])
                pS, pSk = nextF()
                for j in range(4):
                    mm(pS[0:64, 0:32], Qbd[:, j, :], kTs[:, j, :], j == 0, False, ["Qbd", "kTs"], [pSk])
                mm(pS[0:64, 0:32], esel[0:8, :], ncnT[0:8, :], False, False, ["esel", "ncnT"], [pSk])
                mm(pS[0:64, 0:32], ident_b[0:64, 0:64], smask[:, bb * 32:(bb + 1) * 32], False, True, ["ident_b", "smask"], [pSk])
                act(Am[0:64, 0:32], pS[0:64, 0:32], AF.Exp, [pSk], ["Am", ("dsum", 32)], accum_out=dsum[:, 32:33])
                pb, pbk = nextB()
                tp(pb[0:32, 0:64], Am[0:64, 0:32], ident_b[0:64, 0:64], ["Am", "ident_b"], [pbk])
                vcopy(PT[0:32, 0, :], pb[0:32, 0:64], [pbk], ["PT"])
                mm(pacc[0:64, :], PT[0:32, 0, :], Vs_bf[0:32, :], False, True, ["PT", "Vs_bf"], [pacck])
                DK = [("dsum", g) for g in range(33)]
                S.add("dve", lambda e: e.reduce_sum(out=dsum[:, 39:40], in_=dsum[:, 0:33], axis=mybir.AxisListType.X), DK, [("dsum", 39)])
                vrecip(dsum[:, 39:40], dsum[:, 39:40], [("dsum", 39)], [("dsum", 39)])
                vstt(mix_tok[0:64, :], pacc[0:64, :], dsum[:, 39:40], bmask, ALU.mult, ALU.mult, [pacck, ("dsum", 39), "tokC"], ["mix_tok"])
                po2, po2k = nextF()
                for h in range(8):
                    mm(po2[0:64, h * 8:(h + 1) * 8], mix_tok[0:64, h * 64:(h + 1) * 64], qsel[:, :], True, True, ["mix_tok", "qsel"], [po2k])
                vcopy(OnT_s[0:64, :, bb * 8:(bb + 1) * 8], po2[0:64, 0:64].rearrange("p (h q) -> p h q", h=8), [po2k], [("OnT_s", bb)])

            Vflat = Vst[:, :, :, :].rearrange("p a b c -> p (a b c)")
            Ssb = Vflat[0:64, 0:2048].rearrange("p (b h v) -> p b h v", b=4, h=4)
            VK = [("Vst", j) for j in range(4)]
            Ss1 = yacc[0:64, 2, 512:1024].rearrange("p (h v) -> p h v", h=4)
            for bb in range(4):
                dma("sync", Ss1, sgla_d[bb].rearrange("h k v -> k h v"), (), [("yacc", 2, 1)], "Ss1_ld")
                vcopy(Ssb[:, bb, :, :], Ss1, [("yacc", 2, 1)], VK)
                for h in range(4):
                    vtt(gq_m[:, bb * 4 + h, :], gqT[:, h, 0:32], colm[:, bb * 32:(bb + 1) * 32], ALU.mult, [("gqT", h), "colm"], ["gq_m"])
            pA, pAk = nextF()
            for h in range(4):
                mm(pA[0:32, h * 32:(h + 1) * 32], gkT[:, h, 0:32], gqT[:, h, 0:32], True, True, [("gkT", h), ("gqT", h)], [pAk])
            vtt(Am[0:32, 0:128], pA[0:32, 0:128], gmask_s[:, :], ALU.mult, [pAk, "gmask_s"], ["Am"])
            po, pok = nextF()
            for h in range(4):
                mm(po[0:32, h * 128:(h + 1) * 128], Am[0:32, h * 32:(h + 1) * 32], vg_tok[0:32, 0, h * 128:(h + 1) * 128], True, False,
                   ["Am", ("vg_tok", 0)], [pok])
                for bb in range(4):
                    mm(po[0:32, h * 128:(h + 1) * 128], gq_m[:, bb * 4 + h, :], Ssb[:, bb, h, :], False, bb == 3, ["gq_m"] + VK, [pok])
            for h in range(4):
                act(tokC[0:32, 512 + h * 128:512 + (h + 1) * 128], po[0:32, h * 128:(h + 1) * 128], AF.Square, [pok], ["tokC", ("ss", 4 + h)],
                    accum_out=ss[0:32, 4 + h:5 + h])
            SK = [("ss", 4 + h) for h in range(4)]
            act(ss[0:32, 4:8], ss[0:32, 4:8], AF.Sqrt, SK, SK, scale=1.0 / 128, bias=1e-6)
            vrecip(ss[0:32, 4:8], ss[0:32, 4:8], SK, SK)
            for h in range(4):
                vstt(otmp[0:32, h * 128:(h + 1) * 128], po[0:32, h * 128:(h + 1) * 128], ss[0:32, 4 + h:5 + h], ggla[0:32, h * 128:(h + 1) * 128],
                     ALU.mult, ALU.mult, [pok, ("ss", 4 + h), "ggla"], ["otmp"])
            vtt(mix_tok[0:32, :], otmp[0:32, :], gate_tok[0:32, 0, :], ALU.mult, ["otmp", ("gate", 0)], ["mix_tok"])
            pb, pbk = nextB()
            for q in range(4):
                tp(pb[:, q * 32:(q + 1) * 32], mix_tok[0:32, q * 128:(q + 1) * 128], ident_b[0:32, 0:32], ["mix_tok", "ident_b"], [pbk])
            vcopy(mgT_s[:, :, :], pb[:, 0:128].rearrange("p (q t) -> p q t", q=4), [pbk], ["mgT_s"])
            for bb in range(4):
                vts(kd_tok[0:32, 1, :, :], kd_tok[0:32, 0, :, :], rowm[:, bb:bb + 1], None, ALU.mult, ALU.bypass,
                    [("kd_tok", 0), "rowm"], [("kd_tok", 1)])
                pS, pSk = nextF()
                for h in range(4):
                    mm(pS[0:64, h * 128:(h + 1) * 128], kd_tok[0:32, 1, h, :], vg_tok[0:32, 0, h * 128:(h + 1) * 128], True, True,
                       [("kd_tok", 1), ("vg_tok", 0)], [pSk])
                dma("sync", Ss1, sgla_d[bb].rearrange("h k v -> k h v"), (), [("yacc", 2, 1)], "Ss1_ld")
                for h in range(4):
                    vstt(Ss1[:, h, :], Ss1[:, h, :], ebl[:, h, bb:bb + 1], pS[0:64, h * 128:(h + 1) * 128], ALU.mult, ALU.add,
                         [("yacc", 2, 1), ("ebl", h), pSk], [("yacc", 2, 1)])
                dma("sync", gla_s[bb].rearrange("k (h v) -> k h v", h=4), Ss1, [("yacc", 2, 1)], (), "Ss1_st")
        wo = []
        for half in range(2):
            ws, wk = nextW()
            v = ws[:, 0:4096].rearrange("p (h n) -> p h n", h=8)
            memset(v[64:65, :, :], 0.0, [wk])
            dma("sync", v[0:64, :, :], WB["w_out"][0:512, half * 512:(half + 1) * 512].rearrange("(h d) n -> d h n", d=64), [("Wb", "w_out")], [wk], wk)
            ws2, wk2 = nextW()
            v2 = ws2[:, 0:2048].rearrange("p (c n) -> p c n", c=4)
            dma("sync", v2, WB["w_out"][512:1024, half * 512:(half + 1) * 512].rearrange("(c p) n -> p c n", p=128), [("Wb", "w_out")], [wk2], wk2)
            wo.append((v, wk, v2, wk2))
        for b in st["blks"]:
            j = b["kb"]
            for half in range(2):
                v, wk, v2, wk2 = wo[half]
                pd, pdk = nextF()
                for h in range(8):
                    mm(pd[:, :], OnT[0:65, h, j * 128:(j + 1) * 128], v[0:65, h, :], h == 0, False, [("OnT", h), wk], [pdk])
                for c in range(4):
                    mm(pd[:, :], mgT[:, c, j * 128:(j + 1) * 128], v2[:, c, :], False, c == 3, [("mgT", j), wk2], [pdk])
                vcopy(yacc[:, j, half * 512:(half + 1) * 512], pd[:, :], [pdk], [("yacc", j, half)])
            post_res(b, "gmpost", 1.0, hres[:, j, :], ("hres", j))
            norm_to_T(hres[:, j, :], 128, [("hres", j)], "g2pre", b["col0"])
        for b in groups[gi][4:]:
            for half in range(2):
                v, wk, v2, wk2 = wo[half]
                pd, pdk = nextF()
                for h in range(8):
                    mm(pd[0:32, :], OnT_s[0:65, h, :], v[0:65, h, :], h == 0, False, [("OnT_s", q) for q in range(4)] + [wk], [pdk])
                for c in range(4):
                    mm(pd[0:32, :], mgT_s[:, c, :], v2[:, c, :], False, c == 3, ["mgT_s", wk2], [pdk])
                vcopy(yacc[0:32, 4, half * 512:(half + 1) * 512], pd[0:32, :], [pdk], [("yacc", 4, half)])
            post_res(b, "gmpost", 1.0, hres[0:32, 4, :], ("hres", 4))
            norm_to_T(hres[0:b["n"], b["yi"], :], b["n"], [("hres", b["yi"])], "g2pre", b["col0"])
        ffn(gi, "w_gu2", "w_dn2")
        for b in groups[gi]:
            n = b["n"]
            post_res(b, "g2post", 0.5, hres[0:n, b["yi"], :], ("hres", b["yi"]))
            dma("sync", src_rows(b, y_p, y_s), hres[0:n, b["yi"], :], [("hres", b["yi"])], (), ("y_st", b["yi"]))
    S.emit(nc)
    es.close()
    return nc


_NC = None


def _consts():
    c = {}
    c["ident"] = np.eye(128, dtype=np.float32)
    s = np.arange(128)
    c["tri"] = (s[:, None] <= s[None, :]).astype(np.float32)
    c["ones"] = np.ones((128, 128), np.float32)
    rm = np.ones((64, 544), np.float32)
    rm[:, 0:512:128] = 0
    rm[:, 512::8] = 0
    c["rmask"] = rm
    c["gmask"] = np.tile((s[:, None] <= s[None, :]).astype(np.float32), (1, 4))
    t = np.arange(32)
    c["gmask_s"] = np.tile(((t[:, None] <= t[None, :]) & (t[:, None] // 8 == t[None, :] // 8)).astype(np.float32), (1, 4))
    q = np.arange(512)
    cm = np.zeros((128, 4, 512), np.float32)
    for kb in range(4):
        cm[:, kb, :] = np.where(q[None, :] >= kb * 128 + s[:, None], 0.0, NEG)
    c["cmask"] = cm
    c["wsel"] = np.zeros((128, 4), np.float32)
    c["lmask"] = np.zeros((16, 16), np.float32)
    c["pcol"] = np.stack([np.arange(128, dtype=np.float32), (np.arange(128) == 127).astype(np.float32),
                          (np.arange(128) % 32).astype(np.float32)], 1)
    kk_ = np.arange(128)[:, None, None]
    c["triS4"] = (kk_ > 4 * np.arange(32)[None, None, :] + np.arange(4)[None, :, None]).astype(np.float32).reshape(128, 128)
    hq = np.arange(64)
    c["esel"] = (np.arange(8)[:, None] == hq[None, :] // 8).astype(np.float32)
    c["qsel"] = (hq[:, None] % 8 == np.arange(8)[None, :]).astype(np.float32)
    c["bmask"] = (hq[:, None] // 8 == np.arange(512)[None, :] // 64).astype(np.float32)
    c["colm"] = np.broadcast_to((np.arange(4)[:, None] == t[None, :] // 8).astype(np.float32)[None], (64, 4, 32)).copy()
    c["rowm"] = (t[:, None] // 8 == np.arange(4)[None, :]).astype(np.float32)
    sm = np.full((64, 4, 32), NEG, np.float32)
    for b in range(4):
        for qq in range(8):
            sm[qq::8, b, b * 8:b * 8 + qq + 1] = 0.0
    c["smask_new"] = sm
    return c


def kernel(**inp):
    global _NC
    if _NC is None:
        _NC = build()
    f = lambda a: np.ascontiguousarray(np.asarray(a), dtype=np.float32)
    bc = lambda v, n=128: np.ascontiguousarray(np.broadcast_to(f(v).reshape(1, -1), (n, f(v).size)))
    xp_full = f(inp["x_prompt"])
    xs_full = f(inp["x_sample"]).reshape(256, D)
    ck = f(inp["cache_k"]).reshape(-1, 512)
    cv = f(inp["cache_v"]).reshape(-1, 512)
    clf = f(inp["cache_logf"]).reshape(-1, 8)
    pt = np.asarray(inp["page_table"]).astype(np.int32)
    sg = f(inp["state_gla"])[0]
    shared = dict(
        w_gu1=f(inp["ffn1_w_gu"])[0], w_dn1=f(inp["ffn1_w_down"])[0], w_in=f(inp["w_in"])[0], w_out=f(inp["w_out"])[0],
        w_gu2=f(inp["ffn2_w_gu"])[0], w_dn2=f(inp["ffn2_w_down"])[0], w_a2=f(inp["w_gate_up"])[0],
        g1pre=bc(inp["ffn1_norm_pre"]), g1post=bc(inp["ffn1_norm_post"]), gmpre=bc(inp["mix_norm_pre"]),
        gmpost=bc(inp["mix_norm_post"]), g2pre=bc(inp["ffn2_norm_pre"]), g2post=bc(inp["ffn2_norm_post"]),
        bfor=bc(inp["b_forget"]), bgate=np.ascontiguousarray(f(inp["b_gate"]).reshape(4, 64).T),
        ggla=np.ascontiguousarray(np.tile(bc(inp["gla_norm"]), (1, 4))),
        cache_k=ck, cache_v=cv, cache_lf=clf)
    shared.update(_consts())
    in_maps = []
    for c in range(8):
        m = dict(shared)
        g = c % 4
        m["xp"] = np.ascontiguousarray(xp_full[c // 4].reshape(4, 4, 512, D)[:, g].reshape(2048, D))
        ws = np.zeros((128, 4), np.float32); ws[:, g] = 1.0
        m["wsel"] = ws
        m["mrow"] = np.repeat((np.arange(4) >= g).astype(np.float32), 512)[None, :].copy()
        m["xs"] = np.ascontiguousarray(xs_full[c * 32:(c + 1) * 32])
        m["ptab"] = np.ascontiguousarray(np.broadcast_to(pt[c * 4:(c + 1) * 4].reshape(1, 512), (128, 512)))
        m["sgla"] = np.ascontiguousarray(sg[c * 4:(c + 1) * 4])
        p4 = pt[c * 4:(c + 1) * 4].reshape(4, 32, 4)
        m["ptab4"] = np.ascontiguousarray(np.transpose(p4, (2, 0, 1))[np.arange(128) // 32].reshape(128, 128))
        in_maps.append(m)
    res = run_bass_kernel_spmd(_NC, in_maps, core_ids=list(range(8))).results
    r = lambda c, k: np.asarray(res[c][k], dtype=np.float32)
    def gath(key, w):
        o = np.zeros((2, 4, 4, 512, w), np.float32)
        for c in range(8):
            o[c // 4, :, c % 4] = r(c, key).reshape(4, 512, w)
        return o.reshape(2, 8192, w)
    y_prompt = gath("y_p", D)
    y_sample = np.concatenate([r(c, "y_s") for c in range(8)]).reshape(32, 8, D)
    nk_p = gath("nk_p", 512).reshape(1, 2, 8192, 8, 64)
    nv_p = gath("nv_p", 512).reshape(1, 2, 8192, 8, 64)
    nlf_p = gath("nlf_p", 8).reshape(1, 2, 8192, 8)
    gla_p = np.stack([r(0, "gla_p"), r(4, "gla_p")]).reshape(2, 64, 4, 128).transpose(0, 2, 1, 3).reshape(1, 2, 4, 64, 128)
    nk_s = np.concatenate([r(c, "nk_s") for c in range(8)]).reshape(1, 32, 8, 8, 64)
    nv_s = np.concatenate([r(c, "nv_s") for c in range(8)]).reshape(1, 32, 8, 8, 64)
    nlf_s = np.concatenate([r(c, "nlf_s") for c in range(8)]).reshape(1, 32, 8, 8)
    gla_s = np.concatenate([r(c, "gla_s") for c in range(8)]).reshape(32, 64, 4, 128).transpose(0, 2, 1, 3).reshape(1, 32, 4, 64, 128)
    return (y_prompt, y_sample, nk_p, nv_p, nlf_p, np.ascontiguousarray(gla_p), nk_s, nv_s, nlf_s, np.ascontiguousarray(gla_s))
```

```python
import os
from contextlib import ExitStack
import numpy as np
import concourse.bass as bass
import concourse.mybir as mybir
from concourse.bass_utils import run_bass_kernel_spmd

F32 = mybir.dt.float32
BF16 = mybir.dt.bfloat16
I32 = mybir.dt.int32
AF = mybir.ActivationFunctionType
ALU = mybir.AluOpType

D = 1024
FF = 2752
PW = 3096
NEG = -30000.0
STAGE = int(os.environ.get("MK_STAGE", "9"))
NPOOL = int(os.environ.get("MK_POOL", "5120"))

COMPUTE = ("act", "pool", "dve", "pe")
QUEUES = ("sync", "act", "pool", "dve", "pe")


class Op:
    __slots__ = ("eng", "fn", "reads", "writes", "dma", "deps", "sig", "idx", "inc")

    def __init__(self, eng, fn, reads, writes, dma, inc):
        self.eng, self.fn, self.reads, self.writes, self.dma = eng, fn, reads, writes, dma
        self.inc = inc
        self.deps = ()
        self.sig = None


class Sched:
    def __init__(self):
        self.ops = []

    def add(self, eng, fn, reads=(), writes=(), dma=None, inc=None):
        if inc is None:
            inc = 16 if dma is not None else 1
        self.ops.append(Op(eng, fn, tuple(reads), tuple(writes), dma, inc))

    def analyse(self):
        last_w, readers, last_dma = {}, {}, {}
        for i, op in enumerate(self.ops):
            op.idx = i
            deps = set()
            for k in op.reads:
                if k in last_w:
                    deps.add(last_w[k])
            for k in op.writes:
                if k in last_w:
                    deps.add(last_w[k])
                deps.update(readers.get(k, ()))
            if op.dma is not None and op.dma in last_dma:
                deps.add(last_dma[op.dma])
            deps.discard(i)
            for k in op.reads:
                readers.setdefault(k, []).append(i)
            for k in op.writes:
                last_w[k] = i
                readers[k] = []
            if op.dma is not None:
                last_dma[op.dma] = i
            op.deps = deps
        ops = self.ops
        waited = {q: {} for q in QUEUES}
        need = []
        for op in ops:
            best = {}
            for d in op.deps:
                p = ops[d]
                src = ("dma", p.dma) if p.dma is not None else ("eng", p.eng)
                if src == ("eng", "pe") and op.eng == "pe" and op.dma is None:
                    continue
                if d > best.get(src, -1):
                    best[src] = d
            w = waited[op.eng]
            lst = []
            for src, d in best.items():
                if w.get(src, -1) >= d:
                    continue
                w[src] = d
                lst.append((src, d))
            need.append(lst)
            for src, d in lst:
                ops[d].sig = True
        cnt = {}
        for op in ops:
            if op.sig or op.dma is not None:
                src = ("dma", op.dma) if op.dma is not None else ("eng", op.eng)
                cnt[src] = cnt.get(src, 0) + op.inc
                op.sig = cnt[src]
        self.need = need
        self.sources = list(cnt.keys())
        self.final = {}
        for op in ops:
            if op.dma is not None:
                self.final[("dma", op.dma)] = (op.eng, op.sig)

    def emit(self, nc):
        self.analyse()
        ops = self.ops
        with ExitStack() as es:
            sems = {}
            for n, src in enumerate(self.sources):
                sems[src] = es.enter_context(nc.semaphore("s%d" % n))
            block = es.enter_context(nc.Block())

            def section(q):
                def body(eng):
                    for op in ops:
                        if op.eng != q:
                            continue
                        for src, d in self.need[op.idx]:
                            eng.wait_ge(sems[src], ops[d].sig)
                        ins = op.fn(eng)
                        if op.sig:
                            src = ("dma", op.dma) if op.dma is not None else ("eng", op.eng)
                            ins.then_inc(sems[src], op.inc)
                    for src, (qq, val) in self.final.items():
                        if qq == q:
                            eng.wait_ge(sems[src], val)
                return body

            block.sync(section("sync"))
            block.scalar(section("act"))
            block.gpsimd(section("pool"))
            block.vector(section("dve"))
            block.tensor(section("pe"))


def build():
    nc = bass.Bass("TRN2", target_bir_lowering=False)
    S = Sched()
    es = ExitStack()

    def din(name, shape, dt=F32):
        return nc.dram_tensor(name, list(shape), dt, kind="ExternalInput").ap()

    def dout(name, shape, dt=F32):
        return nc.dram_tensor(name, list(shape), dt, kind="ExternalOutput").ap()

    def dscr(name, shape, dt=F32):
        return nc.dram_tensor(name, list(shape), dt)

    def sb(name, shape, dt=F32):
        return es.enter_context(nc.sbuf_tensor("S_" + name, list(shape), dt))

    xp = din("xp", [2048, D])
    xs = din("xs", [32, D])
    W = {}
    for nm, shp in (("w_gu1", [D, 2 * FF]), ("w_dn1", [FF, D]), ("w_in", [D, PW]), ("w_out", [D, D]),
                    ("w_gu2", [D, 2 * FF]), ("w_dn2", [FF, D]), ("w_a2", [16, 256])):
        W[nm] = din(nm, shp)
    G = {}
    for nm in ("g1pre", "g1post", "gmpre", "gmpost", "g2pre", "g2post"):
        G[nm] = din(nm, [128, D])
    bfor_d = din("bfor", [128, 8])
    bgate_d = din("bgate", [64, 4])
    ggla_d = din("ggla", [128, 512])
    ident_d = din("ident", [128, 128])
    tri_d = din("tri", [128, 128])
    ones_d = din("ones", [128, 128])
    rmask_d = din("rmask", [64, 544])
    gmask_d = din("gmask", [128, 512])
    gmask_s_d = din("gmask_s", [32, 128])
    cmask_d = din("cmask", [128, 4, 512])
    wsel_d = din("wsel", [128, 4])
    lmask_d = din("lmask", [16, 16])
    pcol_d = din("pcol", [128, 3])
    ptab4_d = din("ptab4", [128, 128], I32)
    triS4_d = din("triS4", [128, 128])
    ptab_d = din("ptab", [128, 512], I32)
    sgla_d = din("sgla", [4, 4, 64, 128])
    cache_k = din("cache_k", [NPOOL * 128, 512])
    cache_v = din("cache_v", [NPOOL * 128, 512])
    cache_lf = din("cache_lf", [NPOOL * 128, 8])
    smask_new_d = din("smask_new", [64, 4, 32])
    esel_d = din("esel", [8, 64])
    qsel_d = din("qsel", [64, 8])
    bmask_d = din("bmask", [64, 512])
    colm_d = din("colm", [64, 4, 32])
    rowm_d = din("rowm", [32, 4])

    y_p = dout("y_p", [2048, D])
    y_s = dout("y_s", [32, D])
    nk_p = dout("nk_p", [2048, 512])
    nv_p = dout("nv_p", [2048, 512])
    nlf_p = dout("nlf_p", [2048, 8])
    gla_p = dout("gla_p", [64, 512])
    nk_s = dout("nk_s", [32, 512])
    nv_s = dout("nv_s", [32, 512])
    nlf_s = dout("nlf_s", [32, 8])
    gla_s = dout("gla_s", [4, 64, 512])


    NS = int(os.environ.get("MK_NG", "4"))
    kt_in = [dscr("kt_in%d" % i, [8 * 68, 512], BF16) for i in range(NS)]
    ktg = [dscr("ktg%d" % i, [4 * 8 * 68, 512], BF16) for i in range(NS)]
    v_in = [dscr("v_in%d" % i, [8 * 128 * 4, 65], BF16) for i in range(NS)]
    vgt = [dscr("vgt%d" % i, [4 * 8 * 128 * 4, 65], BF16) for i in range(NS)]
    f_in = [dscr("f_in%d" % i, [256, 512]) for i in range(NS)]
    fgt = [dscr("fgt%d" % i, [4 * 256, 512]) for i in range(NS)]
    mrow_d = din("mrow", [1, 2048])

    NW = 4
    wslot = [sb("wslot%d" % i, [128, 4224], BF16) for i in range(NW)]
    xnT = sb("xnT", [128, 8, 544], BF16)
    yacc = sb("yacc", [128, 5, 1024])
    hres = sb("hres", [128, 5, 1024])
    hT = sb("hT", [128, 4, 512], BF16)
    sg = sb("sg", [128, 512])
    tokC = sb("tokC", [128, 1024])
    tokb = sb("tokb", [128, 1024], BF16)
    ss = sb("ss", [128, 8])
    ident_f = sb("ident_f", [128, 128])
    ident_b = sb("ident_b", [128, 128], BF16)
    tri_f = sb("tri_f", [128, 128])
    ones_f = sb("ones_f", [128, 128])
    gains = {nm: sb("G_" + nm, [128, D]) for nm in G}
    bfor = sb("bfor", [128, 8])
    bgate = sb("bgate", [64, 4])
    nbgate = sb("nbgate", [64, 4])
    wa2 = sb("wa2", [16, 256], BF16)
    rmask = sb("rmask", [64, 544])
    ggla = sb("ggla", [128, 512])
    gmask = sb("gmask", [128, 512])
    cmask = sb("cmask", [128, 4, 512], BF16)
    Qext = sb("Qext", [68, 8, 512], BF16)
    KTst = sb("KTst", [68, 512], BF16)
    Vst = sb("Vst", [128, 4, 8, 65], BF16)
    lf = sb("lf", [128, 4, 8])
    lft = sb("lft", [128, 8])
    Cglob = sb("Cglob", [128, 64, 8])
    Crun = sb("Crun", [128, 8])
    Cst = sb("Cst", [128, 4, 8])
    Srun = sb("Srun", [64, 4, 128])
    wsel = sb("wsel", [128, 4])
    biasO = sb("biasO", [128, 4])
    biasT = sb("biasT", [128, 64])
    Pt = [sb("Pt%d" % i, [128, 512], BF16) for i in range(2)]
    Osb = sb("Osb", [65, 512])
    OnT = sb("OnT", [65, 8, 512], BF16)
    gate_tok = sb("gate", [128, 4, 512], BF16)
    gaT = sb("gaT", [16, 512], BF16)
    sp = sb("sp", [64, 512])
    cs = sb("cs", [64, 512])
    eb = sb("eb", [64, 512])
    enb = sb("enb", [64, 512])
    ebl = sb("ebl", [64, 4, 4])
    gqT = sb("gqT", [64, 4, 512], BF16)
    gkT = sb("gkT", [64, 4, 512], BF16)
    kdT = sb("kdT", [64, 4, 512], BF16)
    kd_tok = sb("kd_tok", [128, 4, 4, 64], BF16)
    vg_tok = sb("vg_tok", [128, 4, 512], BF16)
    Sg = sb("Sg", [64, 4, 128])
    Sgb = sb("Sgb", [64, 4, 128], BF16)
    Am = sb("Am", [128, 512], BF16)
    mix_tok = sb("mix_tok", [128, 512], BF16)
    mgT = sb("mgT", [128, 4, 512], BF16)
    otmp = sb("otmp", [128, 512])
    qTs = sb("qTs", [128, 4, 32], BF16)
    kTs = sb("kTs", [128, 4, 32], BF16)
    Qbd = sb("Qbd", [128, 4, 64], BF16)
    Vs_bf = sb("Vs_bf", [32, 512], BF16)
    gq_m = sb("gq_m", [64, 16, 32], BF16)
    OnT_s = sb("OnT_s", [65, 8, 32], BF16)
    mgT_s = sb("mgT_s", [128, 4, 32], BF16)
    PT = sb("PT", [128, 4, 64], BF16)
    dsum = sb("dsum", [64, 40])
    esel = sb("esel", [8, 64], BF16)
    qsel = sb("qsel", [64, 8], BF16)
    smask = sb("smask", [64, 128], BF16)
    colm = sb("colm", [64, 128])
    rowm = sb("rowm", [32, 4])
    gmask_s = sb("gmask_s", [32, 128])
    lfs = sb("lfs", [32, 8])
    ncnT = sb("ncnT", [8, 32], BF16)
    pcol = sb("pcol", [128, 3])

    psF = [es.enter_context(nc.psum_tensor("psF%d" % i, [128, 512], F32)) for i in range(4)]
    psA = [es.enter_context(nc.psum_tensor("psA%d" % i, [128, 512], F32)) for i in range(2)]
    psB = [es.enter_context(nc.psum_tensor("psB%d" % i, [128, 1024], BF16)) for i in range(2)]
    cnt = {"f": 0, "b": 0, "w": 0, "a": 0, "p": 0}

    def nextF():
        i = cnt["f"] % 4
        cnt["f"] += 1
        return psF[i], ("psF", i)

    def nextA():
        i = cnt["a"] % 2
        cnt["a"] += 1
        return psA[i], ("psA", i)

    def nextB():
        i = cnt["b"] % 2
        cnt["b"] += 1
        return psB[i], ("psB", i)

    def nextW():
        i = cnt["w"] % NW
        cnt["w"] += 1
        return wslot[i], ("w", i)

    def nextP():
        i = cnt["p"] % 2
        cnt["p"] += 1
        return Pt[i], ("Pt", i)
    def dma(q, out, in_, reads, writes, key):
        if isinstance(key, str) and key.startswith("c_"):
            key = "c_" + q
        S.add(q, lambda e: e.dma_start(out=out, in_=in_), reads, writes, dma=key)

    def mm(out, lhsT, rhs, st, sp, reads, writes):
        S.add("pe", lambda e: e.matmul(out, lhsT=lhsT, rhs=rhs, start=st, stop=sp), reads, writes)

    def tp(out, in_, idn, reads, writes):
        S.add("pe", lambda e: e.transpose(out=out, in_=in_, identity=idn), reads, writes)

    def act(out, in_, func, reads, writes, **kw):
        S.add("act", lambda e: e.activation(out=out, in_=in_, func=func, **kw), reads, writes)

    def vcopy(out, in_, reads, writes, eng="dve"):
        S.add(eng, lambda e: e.tensor_copy(out=out, in_=in_), reads, writes)

    def vtt(out, a, b, op, reads, writes, eng="dve"):
        S.add(eng, lambda e: e.tensor_tensor(out=out, in0=a, in1=b, op=op), reads, writes)

    def vts(out, a, s1, s2, op0, op1, reads, writes, eng="dve"):
        S.add(eng, lambda e: e.tensor_scalar(out=out, in0=a, scalar1=s1, scalar2=s2, op0=op0, op1=op1), reads, writes)

    def vstt(out, a, s, b, op0, op1, reads, writes):
        S.add("dve", lambda e: e.scalar_tensor_tensor(out=out, in0=a, scalar=s, in1=b, op0=op0, op1=op1), reads, writes)

    def vrecip(out, in_, reads, writes):
        S.add("dve", lambda e: e.reciprocal(out=out, in_=in_), reads, writes)

    def memset(ap, v, writes, eng="dve"):
        S.add(eng, lambda e: e.memset(ap, v), (), writes)


    dma("sync", ident_f[:], ident_d, (), ["ident_f"], "c_identf")
    dma("pool", ident_b[:], ident_d, (), ["ident_b"], "c_identb")
    dma("sync", tri_f[:], tri_d, (), ["tri_f"], "c_tri")
    dma("sync", ones_f[:], ones_d, (), ["ones_f"], "c_ones")
    for nm in G:
        dma("sync", gains[nm][:], G[nm], (), ["G_" + nm], "c_" + nm)
    dma("sync", bfor[:], bfor_d, (), ["bfor"], "c_bfor")
    dma("sync", bgate[:], bgate_d, (), ["bgate"], "c_bgate")
    dma("pool", wa2[:], W["w_a2"], (), ["wa2"], "c_wa2")
    dma("sync", rmask[:], rmask_d, (), ["rmask"], "c_rmask")
    dma("sync", ggla[:], ggla_d, (), ["ggla"], "c_ggla")
    dma("sync", gmask[:], gmask_d, (), ["gmask"], "c_gmask")
    dma("pool", cmask[:], cmask_d, (), ["cmask"], "c_cmask")
    dma("pool", esel[:], esel_d, (), ["esel"], "c_esel")
    dma("pool", qsel[:], qsel_d, (), ["qsel"], "c_qsel")
    dma("pool", smask[:], smask_new_d.rearrange("p b t -> p (b t)"), (), ["smask"], "c_smask")
    dma("sync", colm[:], colm_d.rearrange("p b t -> p (b t)"), (), ["colm"], "c_colm")
    dma("sync", rowm[:], rowm_d, (), ["rowm"], "c_rowm")
    dma("sync", gmask_s[:], gmask_s_d, (), ["gmask_s"], "c_gmask_s")
    dma("sync", pcol[:], pcol_d, (), ["pcol"], "c_pcol")
    memset(OnT_s[:], 1.0, [("OnT_s", b) for b in range(4)])
    vts(nbgate[:], bgate[:], -1.0, None, ALU.mult, ALU.bypass, ["bgate"], ["nbgate"])
    memset(KTst[:], 1.0, ["KTst"])
    memset(hT[0:1, 0, :], 0.0, [("hT", 0)])
    memset(hT[0:1, 1, :], NEG, [("hT", 1)])
    dma("sync", KTst[67:68, :], hT[0:1, 0, :], [("hT", 0)], ["KTst"], "c_k67")
    for h in range(8):
        dma("sync", Qext[67:68, h, :], hT[0:1, 1, :], [("hT", 1)], [("Qext", h)], "c_q67")
    memset(Srun[:], 0.0, ["Srun"])
    dma("sync", wsel[:], wsel_d, (), ["wsel"], "c_wsel")
    memset(Vst[:], 1.0, ["Vst"])
    memset(Crun[:], 0.0, ["Crun"])
    memset(Sg[:], 0.0, ["Sg"])
    memset(Sgb[:], 0.0, ["Sgb"])

    Wb = {}
    for nm, shp in (("w_gu1", [D, 2 * FF]), ("w_dn1", [FF, D]), ("w_in", [D, PW]), ("w_out", [D, D]),
                    ("w_gu2", [D, 2 * FF]), ("w_dn2", [FF, D])):
        Wb[nm] = dscr(nm + "_b", shp, BF16)
        for r0 in range(0, shp[0], 128):
            r1 = min(shp[0], r0 + 128)
            dma("pool", Wb[nm][r0:r1, :], W[nm][r0:r1, :], (), [("Wb", nm)], ("wcast", nm))
    WB = {nm: Wb[nm].ap() for nm in Wb}

    NG = NS
    groups = []
    for gi in range(NG):
        blks = []
        for j in range(4):
            blks.append(dict(kind="p", row0=gi * 512 + j * 128, n=128, yi=j, col0=j * 128, tile=gi, kb=j))
        if gi == NG - 1:
            blks.append(dict(kind="s", row0=0, n=32, yi=4, col0=512, tile=None, kb=0))
        groups.append(blks)

    def subtiles(gi):
        st = [dict(col0=0, tn=512, blks=groups[gi][0:4], kind="p", tile=gi, li=0)]
        if gi == NG - 1:
            st.append(dict(col0=512, tn=32, blks=groups[gi][4:5], kind="s", tile=None, li=1))
        return st

    def src_rows(b, prm, smp):
        return (prm if b["kind"] == "p" else smp)[b["row0"]:b["row0"] + b["n"], :]

    def xk(st):
        return [("xnT", st["col0"])] if st["tn"] == 32 else [("xnT", st["col0"] + j * 128) for j in range(4)]
    def rstd(src, n, srcks, col, dim=D, junk=None):
        act(tokC[0:n, 0:dim], src, AF.Square, list(srcks), ["tokC", ("ss", col)], accum_out=ss[0:n, col:col + 1])
        act(ss[0:n, col:col + 1], ss[0:n, col:col + 1], AF.Sqrt, [("ss", col)], [("ss", col)], scale=1.0 / dim, bias=1e-6)
        vrecip(ss[0:n, col:col + 1], ss[0:n, col:col + 1], [("ss", col)], [("ss", col)])

    def norm_to_T(src, n, srcks, gname, col0):
        rstd(src, n, srcks, 0)
        vstt(tokb[0:n, :], src, ss[0:n, 0:1], gains[gname][0:n, :], ALU.mult, ALU.mult,
             list(srcks) + [("ss", 0), "G_" + gname], ["tokb"])
        pb, pk = nextB()
        for c in range(8):
            tp(pb[:, c * 128:c * 128 + n], tokb[0:n, c * 128:(c + 1) * 128], ident_b[0:n, 0:n], ["tokb", "ident_b"], [pk])
        vcopy(xnT[:, :, col0:col0 + n], pb[:, :].rearrange("p (c t) -> p c t", c=8)[:, :, 0:n], [pk], [("xnT", col0)])

    def wload_cols(wname, lo, hi):
        ws, wk = nextW()
        n = hi - lo
        v = ws[:, 0:8 * n].rearrange("p (k n) -> p k n", k=8)
        dma("sync", v, WB[wname].rearrange("(k p) n -> p k n", p=128)[:, :, lo:hi], [("Wb", wname)], [wk], wk)
        return v, wk

    def ffn(gi, wgu, wdn):
        sts = subtiles(gi)
        for s in range(11):
            f0 = s * 256
            nf = min(256, FF - f0)
            chunks = [(c * 128, min(128, nf - c * 128)) for c in range((nf + 127) // 128)]
            ws, wk = nextW()
            wg = ws[:, 0:4096].rearrange("p (k t n) -> p k t n", k=8, t=2)
            for t in range(2):
                dma("sync", wg[:, :, t, 0:nf], WB[wgu].rearrange("(k p) n -> p k n", p=128)[:, :, t * FF + f0:t * FF + f0 + nf],
                    [("Wb", wgu)], [wk], wk)
            wd_s, wdk = nextW()
            wd = wd_s[:, 0:2048].rearrange("p (c n) -> p c n", c=2)
            for ci, (c0, cm) in enumerate(chunks):
                dma("sync", wd[0:cm, ci, :], WB[wdn][f0 + c0:f0 + c0 + cm, :], [("Wb", wdn)], [wdk], wdk)
            for st in sts:
                c0t, tn = st["col0"], st["tn"]
                for ci, (c0, cm) in enumerate(chunks):
                    pg, pgk = nextF()
                    pu, puk = nextF()
                    for k in range(8):
                        mm(pg[0:cm, 0:tn], wg[:, k, 0, c0:c0 + cm], xnT[:, k, c0t:c0t + tn], k == 0, k == 7, [wk] + xk(st), [pgk])
                    for k in range(8):
                        mm(pu[0:cm, 0:tn], wg[:, k, 1, c0:c0 + cm], xnT[:, k, c0t:c0t + tn], k == 0, k == 7, [wk] + xk(st), [puk])
                    act(sg[0:cm, 0:tn], pg[0:cm, 0:tn], AF.Silu, [pgk], ["sg"])
                    vtt(hT[0:cm, ci, 0:tn], sg[0:cm, 0:tn], pu[0:cm, 0:tn], ALU.mult, ["sg", puk], [("hT", ci)])
                for bi, b in enumerate(st["blks"]):
                    n = b["n"]
                    pd = [nextF(), nextF()]
                    for half in range(2):
                        for ci, (c0, cm) in enumerate(chunks):
                            mm(pd[half][0][0:n, :], hT[0:cm, ci, bi * 128:bi * 128 + n], wd[0:cm, ci, half * 512:(half + 1) * 512],
                               ci == 0, ci == len(chunks) - 1, [("hT", ci), wdk], [pd[half][1]])
                    for half in range(2):
                        dst = yacc[0:n, b["yi"], half * 512:(half + 1) * 512]
                        if s == 0:
                            vcopy(dst, pd[half][0][0:n, :], [pd[half][1]], [("yacc", b["yi"], half)])
                        else:
                            vtt(dst, dst, pd[half][0][0:n, :], ALU.add, [("yacc", b["yi"], half), pd[half][1]],
                                [("yacc", b["yi"], half)])

    def post_res(b, gname, scale, out_ap, outk):
        n = b["n"]
        ya = yacc[0:n, b["yi"], :]
        yk = [("yacc", b["yi"], 0), ("yacc", b["yi"], 1)]
        rstd(ya, n, yk, 1)
        vstt(tokC[0:n, :], ya, ss[0:n, 1:2], gains[gname][0:n, :], ALU.mult, ALU.mult, yk + [("ss", 1), "G_" + gname], ["tokC"])
        vstt(out_ap, tokC[0:n, :], scale, hres[0:n, b["yi"], :], ALU.mult, ALU.add, ["tokC", ("hres", b["yi"])], [outk])

    WIN = W["w_in"]
    for gi in range(NG):
        sts = subtiles(gi)
        T = gi
        nkb = 16 * T + 16
        for b in groups[gi]:
            n = b["n"]
            hk = ("hres", b["yi"])
            dma("sync", hres[0:n, b["yi"], :], src_rows(b, xp, xs), (), [hk], ("hres_ld", b["yi"]))
            norm_to_T(hres[0:n, b["yi"], :], n, [hk], "g1pre", b["col0"])
        ffn(gi, "w_gu1", "w_dn1")
        for b in groups[gi]:
            n = b["n"]
            hk = ("hres", b["yi"])
            post_res(b, "g1post", 0.5, hres[0:n, b["yi"], :], hk)
            norm_to_T(hres[0:n, b["yi"], :], n, [hk], "gmpre", b["col0"])
        if STAGE < 1:
            for b in groups[gi]:
                n = b["n"]
                dma("sync", src_rows(b, y_p, y_s), hres[0:n, b["yi"], :], [("hres", b["yi"])], (), ("y_st", b["yi"]))
            continue
        st = sts[0]
        w1, w1k = wload_cols("w_in", 0, 512)
        w1b, w1bk = wload_cols("w_in", 512, 1024)
        for h in range(8):
            pq, pqk = nextF()
            for k in range(8):
                mm(pq[0:64, :], w1[:, k, h * 64:(h + 1) * 64], xnT[:, k, 0:512], k == 0, k == 7, [w1k] + xk(st), [pqk])
            vts(Qext[0:64, h, :], pq[0:64, :], 0.125, None, ALU.mult, ALU.bypass, [pqk], [("Qext", h)])
            pk_, pkk = nextF()
            for k in range(8):
                mm(pk_[0:64, :], w1b[:, k, h * 64:(h + 1) * 64], xnT[:, k, 0:512], k == 0, k == 7, [w1bk] + xk(st), [pkk])
            vcopy(KTst[0:64, :], pk_[0:64, :], [pkk], ["KTst"])
            dma("sync", kt_in[T][h * 68:(h + 1) * 68, :], KTst[:, :], ["KTst"], [("kt_in", T)], "KTst_st")
        for b in st["blks"]:
            pk_, pkk = nextF()
            for k in range(8):
                mm(pk_[:, :], xnT[:, k, b["col0"]:b["col0"] + 128], w1b[:, k, :], k == 0, k == 7, [w1bk, ("xnT", b["col0"])], [pkk])
            vcopy(otmp[:, :], pk_[:, :], [pkk], ["otmp"])
            dma("sync", nk_p[b["row0"]:b["row0"] + 128, :], otmp[:, :], ["otmp"], (), "otmp_st")
        if len(sts) > 1:
            pk_, pkk = nextF()
            for k in range(8):
                mm(pk_[0:32, :], xnT[:, k, 512:544], w1b[:, k, :], k == 0, k == 7, [w1bk, ("xnT", 512)], [pkk])
            vcopy(otmp[0:32, :], pk_[0:32, :], [pkk], ["otmp"])
            dma("sync", nk_s[:, :], otmp[0:32, :], ["otmp"], (), "otmp_st")
        w2, w2k = wload_cols("w_in", 1024, 1536)
        w2f, w2fk = wload_cols("w_in", 1536, 1544)
        for b in st["blks"]:
            j = b["kb"]
            pv, pvk = nextF()
            for k in range(8):
                mm(pv[:, :], xnT[:, k, b["col0"]:b["col0"] + 128], w2[:, k, 0:512], k == 0, k == 7, [w2k, ("xnT", b["col0"])], [pvk])
            pf, pfk = nextF()
            for k in range(8):
                mm(pf[:, 0:8], xnT[:, k, b["col0"]:b["col0"] + 128], w2f[:, k, 0:8], k == 0, k == 7, [w2fk, ("xnT", b["col0"])], [pfk])
            vcopy(otmp[:, :], pv[:, :], [pvk], ["otmp"])
            dma("sync", nv_p[b["row0"]:b["row0"] + 128, :], otmp[:, :], ["otmp"], (), "otmp_st")
            vcopy(Vst[:, j, :, 0:64], pv[:, :].rearrange("p (h d) -> p h d", h=8), [pvk], [("Vst", j)])
            vtt(lft[:, :], pf[:, 0:8], bfor[:, :], ALU.add, [pfk, "bfor"], ["lft"])
            act(lft[:, :], lft[:, :], AF.Exp, ["lft"], ["lft"], scale=-1.0)
            act(lft[:, :], lft[:, :], AF.Ln, ["lft"], ["lft"], bias=1.0)
            vts(lf[:, j, :], lft[:, :], -1.0, None, ALU.mult, ALU.bypass, ["lft"], [("lf", j)])
            dma("sync", nlf_p[b["row0"]:b["row0"] + 128, :], lf[:, j, :], [("lf", j)], (), ("lf_st", j))
        if len(sts) > 1:
            pv, pvk = nextF()
            for k in range(8):
                mm(pv[0:32, :], xnT[:, k, 512:544], w2[:, k, 0:512], k == 0, k == 7, [w2k, ("xnT", 512)], [pvk])
            pf, pfk = nextF()
            for k in range(8):
                mm(pf[0:32, 0:8], xnT[:, k, 512:544], w2f[:, k, 0:8], k == 0, k == 7, [w2fk, ("xnT", 512)], [pfk])
            vcopy(otmp[0:32, :], pv[0:32, :], [pvk], ["otmp"])
            dma("sync", nv_s[:, :], otmp[0:32, :], ["otmp"], (), "otmp_st")
            vtt(lft[0:32, :], pf[0:32, 0:8], bfor[0:32, :], ALU.add, [pfk, "bfor"], ["lft"])
            act(lft[0:32, :], lft[0:32, :], AF.Exp, ["lft"], ["lft"], scale=-1.0)
            act(lft[0:32, :], lft[0:32, :], AF.Ln, ["lft"], ["lft"], bias=1.0)
            vts(lft[0:32, :], lft[0:32, :], -1.0, None, ALU.mult, ALU.bypass, ["lft"], ["lft"])
            dma("sync", nlf_s[:, :], lft[0:32, :], ["lft"], (), "lft_st")
        v_in_v = v_in[T].ap().rearrange("(h p b) e -> h p b e", h=8, p=128)
        for h in range(8):
            dma("sync", v_in_v[h], Vst[:, :, h, :], [("Vst", j) for j in range(4)], [("v_in", T)], ("Vst_st", h))
        lfk = [("lf", j) for j in range(4)]
        Cloc_t = yacc[:, 1, 512:544].rearrange("p (j h) -> p j h", h=8)
        for j in range(4):
            pc, pck = nextF()
            mm(pc[:, 0:8], tri_f[:, :], lf[:, j, :], True, j == 0, ["tri_f"] + lfk, [pck])
            for jj in range(j):
                mm(pc[:, 0:8], ones_f[:, :], lf[:, jj, :], False, jj == j - 1, ["ones_f"] + lfk, [pck])
            vcopy(Cloc_t[:, j, :], pc[:, 0:8], [pck], [("yacc", 1, 1)])
        dma("sync", f_in[T][128:256, 0:32], yacc[:, 1, 512:544], [("yacc", 1, 1)], [("f_in", T)], "cloc_st")
        pr, prk = nextF()
        for j in range(4):
            mm(pr[0:8, j * 128:(j + 1) * 128], lf[:, j, :], tri_f[:, :], True, j == 0, ["tri_f"] + lfk, [prk])
            for jj in range(j):
                mm(pr[0:8, j * 128:(j + 1) * 128], lf[:, jj, :], ones_f[:, :], False, jj == j - 1, ["ones_f"] + lfk, [prk])
        crl = hT[0:8, 0:3, :]
        CRK = [("hT", 0), ("hT", 1), ("hT", 2)]
        crf = otmp[0:8, :]
        crg = tokC[0:8, 0:512]
        vcopy(crl[:, 0, :], pr[0:8, :], [prk], CRK)
        vtt(crf, pr[0:8, :], crl[:, 0, :], ALU.subtract, [prk] + CRK, ["otmp"])
        vcopy(crl[:, 1, :], crf, ["otmp"], CRK)
        vtt(crg, crf, crl[:, 1, :], ALU.subtract, ["otmp"] + CRK, ["tokC"])
        vcopy(crl[:, 2, :], crg, ["tokC"], CRK)
        for h in range(8):
            for r in range(3):
                dma("sync", Qext[64 + r:65 + r, h, :], crl[h:h + 1, r, :], CRK, [("Qext", h)], ("Qx", h))
        w5, w5k = wload_cols("w_in", 2568, 3080)
        w5a, w5ak = wload_cols("w_in", 3080, 3096)
        for b in st["blks"]:
            j = b["kb"]
            pg, pgk = nextF()
            for k in range(8):
                mm(pg[:, :], xnT[:, k, b["col0"]:b["col0"] + 128], w5[:, k, 0:512], k == 0, k == 7, [w5k, ("xnT", b["col0"])], [pgk])
            act(gate_tok[:, j, :], pg[:, :], AF.Silu, [pgk], [("gate", j)])
        pa, pak = nextF()
        for k in range(8):
            mm(pa[0:16, :], w5a[:, k, 0:16], xnT[:, k, 0:512], k == 0, k == 7, [w5ak] + xk(st), [pak])
        vcopy(gaT[:, :], pa[0:16, :], [pak], ["gaT"])
        w3, w3k = wload_cols("w_in", 1544, 2056)
        for h in range(4):
            px, pxk = nextF()
            mm(px[0:64, :], wa2[0:16, h * 64:(h + 1) * 64], gaT[0:16, :], True, True, ["wa2", "gaT"], [pxk])
            act(sp[:, :], px[0:64, :], AF.Exp, [pxk, "nbgate"], ["sp"], scale=-1.0, bias=nbgate[:, h:h + 1])
            act(sp[:, :], sp[:, :], AF.Ln, ["sp"], ["sp"], bias=1.0)
            S.add("dve", (lambda e: e.tensor_tensor_scan(out=cs[:, :], data0=rmask[:, 0:512], data1=sp[:, :], initial=0.0,
                                                          op0=ALU.mult, op1=ALU.add)), ["rmask", "sp"], ["cs"])
            act(eb[:, :], cs[:, :], AF.Exp, ["cs"], ["eb"], scale=-1.0 / 16)
            act(enb[:, :], cs[:, :], AF.Exp, ["cs"], ["enb"], scale=1.0 / 16)
            vcopy(ebl[:, h, :], eb[:, :].rearrange("p (c t) -> p c t", c=4)[:, :, 127], ["eb"], [("ebl", h)])
            pq, pqk = nextF()
            for k in range(8):
                mm(pq[0:64, :], w3[:, k, h * 64:(h + 1) * 64], xnT[:, k, 0:512], k == 0, k == 7, [w3k] + xk(st), [pqk])
            vstt(gqT[:, h, :], pq[0:64, :], 0.125, eb[:, :], ALU.mult, ALU.mult, [pqk, "eb"], [("gqT", h)])
            pk_, pkk = nextF()
            for k in range(8):
                mm(pk_[0:64, :], w3[:, k, 256 + h * 64:256 + (h + 1) * 64], xnT[:, k, 0:512], k == 0, k == 7, [w3k] + xk(st), [pkk])
            vtt(gkT[:, h, :], pk_[0:64, :], enb[:, :], ALU.mult, [pkk, "enb"], [("gkT", h)])
            for c in range(4):
                vts(kdT[:, h, c * 128:(c + 1) * 128], gkT[:, h, c * 128:(c + 1) * 128], ebl[:, h, c:c + 1], None,
                    ALU.mult, ALU.bypass, [("gkT", h), ("ebl", h)], [("kdT", h)])
        for c in range(4):
            pb, pbk = nextB()
            for h in range(4):
                tp(pb[:, h * 64:(h + 1) * 64], kdT[:, h, c * 128:(c + 1) * 128], ident_b[0:64, 0:64], [("kdT", h), "ident_b"], [pbk])
            vcopy(kd_tok[:, c, :, :], pb[:, 0:256].rearrange("p (h k) -> p h k", h=4), [pbk], [("kd_tok", c)])
        w4, w4k = wload_cols("w_in", 2056, 2568)
        for b in st["blks"]:
            j = b["kb"]
            pv, pvk = nextF()
            for k in range(8):
                mm(pv[:, :], xnT[:, k, b["col0"]:b["col0"] + 128], w4[:, k, :], k == 0, k == 7, [w4k, ("xnT", b["col0"])], [pvk])
            vcopy(vg_tok[:, j, :], pv[:, :], [pvk], [("vg_tok", j)])
        if STAGE < 2:
            continue
        Sloc = yacc[0:64, 0, 512:1024].rearrange("p (h v) -> p h v", h=4)
        SLK = [("yacc", 0, 1)]
        for c in range(4):
            pS, pSk = nextF()
            for h in range(4):
                mm(pS[0:64, h * 128:(h + 1) * 128], kd_tok[:, c, h, :], vg_tok[:, c, h * 128:(h + 1) * 128], True, True,
                   [("kd_tok", c), ("vg_tok", c)], [pSk])
            if c == 0:
                vcopy(yacc[0:64, 0, 512:1024], pS[0:64, :], [pSk], SLK)
            else:
                for h in range(4):
                    vstt(Sloc[:, h, :], Sloc[:, h, :], ebl[:, h, c:c + 1], pS[0:64, h * 128:(h + 1) * 128], ALU.mult, ALU.add,
                         SLK + [("ebl", h), pSk], SLK)
        dma("sync", f_in[T][0:64, :], yacc[0:64, 0, 512:1024], SLK, [("f_in", T)], "sloc_st")
        eBt = yacc[0:64, 2, 512:516]
        EK = [("ebl", h) for h in range(4)]
        vtt(eBt, ebl[:, :, 0], ebl[:, :, 1], ALU.mult, EK, [("yacc", 2, 1)])
        vtt(eBt, eBt, ebl[:, :, 2], ALU.mult, EK + [("yacc", 2, 1)], [("yacc", 2, 1)])
        vtt(eBt, eBt, ebl[:, :, 3], ALU.mult, EK + [("yacc", 2, 1)], [("yacc", 2, 1)])
        dma("sync", f_in[T][64:128, 0:4], eBt, [("yacc", 2, 1)], [("f_in", T)], "ebt_st")
        RG = [[0, 1, 2, 3], [4, 5, 6, 7]]
        for nm, src, dst in (("kt", kt_in[T], ktg[T]), ("v", v_in[T], vgt[T]), ("f", f_in[T], fgt[T])):
            S.add("pool", (lambda e, src=src, dst=dst: e.collective_compute("AllGather", ALU.bypass, replica_groups=RG,
                                                                             ins=[src.ap().opt()], outs=[dst.ap().opt()])),
                  [(nm + "_in" if nm != "f" else "f_in", T)], [(nm + "g", T)], dma=("cc", nm), inc=1)
        Clg = yacc[:, 1, 0:128].rearrange("p (q h) -> p q h", h=8)
        CLK = [("yacc", 1, 0)]
        for gq in range(4):
            dma("sync", yacc[:, 1, gq * 32:(gq + 1) * 32], fgt[T][gq * 256 + 128:gq * 256 + 256, 0:32], [("fg", T)], CLK, "clg_ld")
        Coffs = yacc[:, 2, 0:40].rearrange("p (q h) -> p q h", h=8)
        CFK = [("yacc", 2, 0)]
        vcopy(Coffs[:, 0, :], Crun[:, :], ["Crun"], CFK)
        for gq in range(4):
            vts(otmp[:, 0:8], Clg[:, gq * 4 + 3, :], pcol[:, 1:2], None, ALU.mult, ALU.bypass, CLK + ["pcol"], ["otmp"])
            pt_, ptk = nextF()
            mm(pt_[:, 0:8], ones_f[:, :], otmp[:, 0:8], True, True, ["ones_f", "otmp"], [ptk])
            vtt(Coffs[:, gq + 1, :], Coffs[:, gq, :], pt_[:, 0:8], ALU.add, CFK + [ptk], CFK)
        for gq in range(4):
            for bl in range(4):
                qi = 16 * T + 4 * gq + bl
                vtt(Cglob[:, qi, :], Clg[:, gq * 4 + bl, :], Coffs[:, gq, :], ALU.add, CLK + CFK, [("Cglob", qi)])
        vts(Cst[:, T, :], Coffs[:, 0, :], wsel[:, 0:1], None, ALU.mult, ALU.bypass, CFK + ["wsel"], [("Cst", T)])
        for gq in range(1, 4):
            vstt(Cst[:, T, :], Coffs[:, gq, :], wsel[:, gq:gq + 1], Cst[:, T, :], ALU.mult, ALU.add, CFK + ["wsel", ("Cst", T)], [("Cst", T)])
        vcopy(Crun[:, :], Coffs[:, 4, :], CFK, ["Crun"])
        stg = yacc[0:64, 0, 0:512].rearrange("p (h v) -> p h v", h=4)
        STK = [("yacc", 0, 0)]
        ebs = yacc[0:64, 2, 516:520]
        for gq in range(4):
            dma("sync", yacc[0:64, 0, 0:512], fgt[T][gq * 256:gq * 256 + 64, :], [("fg", T)], STK, "stg_ld")
            dma("sync", ebs, fgt[T][gq * 256 + 64:gq * 256 + 128, 0:4], [("fg", T)], [("yacc", 2, 1)], "ebs_ld")
            if gq == 0:
                vts(Sg[:, :, :].rearrange("p h v -> p (h v)"), Srun[:, :, :].rearrange("p h v -> p (h v)"), wsel[0:64, 0:1], None,
                    ALU.mult, ALU.bypass, ["Srun", "wsel"], ["Sg"])
            else:
                vstt(Sg[:, :, :].rearrange("p h v -> p (h v)"), Srun[:, :, :].rearrange("p h v -> p (h v)"), wsel[0:64, gq:gq + 1],
                     Sg[:, :, :].rearrange("p h v -> p (h v)"), ALU.mult, ALU.add, ["Srun", "wsel", "Sg"], ["Sg"])
            for h in range(4):
                vstt(Srun[:, h, :], Srun[:, h, :], ebs[:, h:h + 1], stg[:, h, :], ALU.mult, ALU.add,
                     ["Srun", ("yacc", 2, 1)] + STK, ["Srun"])
        vcopy(Sgb[:, :, :], Sg[:, :, :], ["Sg"], ["Sgb"])
        for c in range(4):
            pA, pAk = nextF()
            for h in range(4):
                mm(pA[:, h * 128:(h + 1) * 128], gkT[:, h, c * 128:(c + 1) * 128], gqT[:, h, c * 128:(c + 1) * 128], True, True,
                   [("gkT", h), ("gqT", h)], [pAk])
            vtt(Am[:, :], pA[:, :], gmask[:, :], ALU.mult, [pAk, "gmask"], ["Am"])
            po, pok = nextF()
            for h in range(4):
                mm(po[:, h * 128:(h + 1) * 128], Am[:, h * 128:(h + 1) * 128], vg_tok[:, c, h * 128:(h + 1) * 128], True, False,
                   ["Am", ("vg_tok", c)], [pok])
                mm(po[:, h * 128:(h + 1) * 128], gqT[:, h, c * 128:(c + 1) * 128], Sgb[:, h, :], False, True,
                   [("gqT", h), "Sgb"], [pok])
            for h in range(4):
                act(tokC[:, h * 128:(h + 1) * 128], po[:, h * 128:(h + 1) * 128], AF.Square, [pok], ["tokC", ("ss", 4 + h)],
                    accum_out=ss[:, 4 + h:5 + h])
            act(ss[:, 4:8], ss[:, 4:8], AF.Sqrt, [("ss", 4 + h) for h in range(4)], [("ss", 4 + h) for h in range(4)],
                scale=1.0 / 128, bias=1e-6)
            vrecip(ss[:, 4:8], ss[:, 4:8], [("ss", 4 + h) for h in range(4)], [("ss", 4 + h) for h in range(4)])
            for h in range(4):
                vstt(otmp[:, h * 128:(h + 1) * 128], po[:, h * 128:(h + 1) * 128], ss[:, 4 + h:5 + h], ggla[:, h * 128:(h + 1) * 128],
                     ALU.mult, ALU.mult, [pok, ("ss", 4 + h), "ggla"], ["otmp"])
            vtt(mix_tok[:, :], otmp[:, :], gate_tok[:, c, :], ALU.mult, ["otmp", ("gate", c)], ["mix_tok"])
            pb, pbk = nextB()
            for q in range(4):
                tp(pb[:, q * 128:(q + 1) * 128], mix_tok[:, q * 128:(q + 1) * 128], ident_b[:, :], ["mix_tok", "ident_b"], [pbk])
            vcopy(mgT[:, :, c * 128:(c + 1) * 128], pb[:, 0:512].rearrange("p (q t) -> p q t", q=4), [pbk], [("mgT", c)])
            pS, pSk = nextF()
            for h in range(4):
                mm(pS[0:64, h * 128:(h + 1) * 128], kd_tok[:, c, h, :], vg_tok[:, c, h * 128:(h + 1) * 128], True, True,
                   [("kd_tok", c), ("vg_tok", c)], [pSk])
            for h in range(4):
                vstt(Sg[:, h, :], Sg[:, h, :], ebl[:, h, c:c + 1], pS[0:64, h * 128:(h + 1) * 128], ALU.mult, ALU.add,
                     ["Sg", ("ebl", h), pSk], ["Sg"])
            vcopy(Sgb[:, :, :], Sg[:, :, :], ["Sg"], ["Sgb"])
        if gi == NG - 1:
            dma("sync", gla_p, Srun[:, :, :].rearrange("p h v -> p (h v)"), ["Srun"], (), "gla_p_st")
        for h in range(8):
            vts(biasT[:, 0:nkb], Cglob[:, 0:nkb, h], Cst[:, T, h:h + 1], -1.0, ALU.subtract, ALU.mult,
                [("Cglob", q) for q in range(nkb)] + [("Cst", T)], ["biasT"])
            vts(biasO[:, :], yacc[:, 1, 512:544].rearrange("p (j h) -> p j h", h=8)[:, :, h], -1.0, None, ALU.mult, ALU.bypass,
                [("yacc", 1, 1)], ["biasO"])
            kts, ktk = nextW()
            kts2, ktk2 = nextW()
            for j in range(T + 1):
                dst_s, dk = (kts, ktk) if j < 2 else (kts2, ktk2)
                dcol = (j % 2) * 2048
                dma("sync", dst_s[0:68, dcol:dcol + 2048].rearrange("p (g t) -> p g t", g=4),
                    ktg[j].ap().rearrange("(g h r) t -> g h r t", g=4, h=8)[:, h, :, :].rearrange("g r t -> r g t"),
                    [("ktg", j)], [dk], dk)
            cur_s, ck_ = (kts, ktk) if T < 2 else (kts2, ktk2)
            ccol = (T % 2) * 2048
            dma("pool", cur_s[67:68, ccol:ccol + 2048], mrow_d, (), [ck_], ck_)
            if T < 2:
                dma("sync", kts2[0:1, 0:8], ktg[0][0:1, 0:8], [("ktg", 0)], [ktk2], ktk2)
            vs_, vk = nextW()
            vv = vs_[:, 0:nkb * 65].rearrange("p (b e) -> p b e", e=65)
            for j in range(T + 1):
                vsrc = vgt[j].ap().rearrange("(g h p b) e -> g h p b e", g=4, h=8, p=128)
                for gq in range(4):
                    dma("sync", vv[:, 16 * j + 4 * gq:16 * j + 4 * gq + 4, :], vsrc[gq, h], [("vg", j)], [vk], vk)
            os_, ok_ = nextW()
            ko = os_[0:68, 0:512]
            vo = os_[:, 512:772].rearrange("p (b e) -> p b e", e=65)
            dma("sync", ko, kt_in[T][h * 68:(h + 1) * 68, :], [("kt_in", T)], [ok_], ok_)
            dma("sync", vo, v_in[T].ap().rearrange("(h p b) e -> h p b e", h=8, p=128)[h], [("v_in", T)], [ok_], ok_)
            po, pok = nextA()
            blks_ = []
            for kb in range(nkb):
                if kb < 32:
                    ksrc, kk_ = kts[0:68, kb * 128:(kb + 1) * 128], ktk
                else:
                    ksrc, kk_ = kts2[0:68, (kb - 32) * 128:(kb - 31) * 128], ktk2
                blks_.append((ksrc, kk_, None, biasT[:, kb:kb + 1], "biasT", vv[:, kb, :], vk))
            for kb in range(4):
                blks_.append((ko[:, kb * 128:(kb + 1) * 128], ok_, kb, biasO[:, kb:kb + 1], "biasO", vo[:, kb, :], ok_))
            NB_ = len(blks_)
            pS_l, pt_l = {}, {}
            for it in range(NB_ + 2):
                if it < NB_:
                    ksrc, kk_, mk, bia, bk, vsrc, vkk = blks_[it]
                    pS, pSk = nextF()
                    mm(pS[:, :], ksrc, Qext[0:68, h, :], True, mk is None, [kk_, ("Qext", h)], [pSk])
                    if mk is not None:
                        mm(pS[:, :], ident_b[:, :], cmask[:, mk, :], False, True, ["ident_b", "cmask"], [pSk])
                    pS_l[it] = (pS, pSk)
                i1 = it - 1
                if 0 <= i1 < NB_:
                    ksrc, kk_, mk, bia, bk, vsrc, vkk = blks_[i1]
                    pS, pSk = pS_l.pop(i1)
                    pt, ptk = nextP()
                    act(pt[:, :], pS[:, :], AF.Exp, [pSk, bk], [ptk], bias=bia)
                    pt_l[i1] = (pt, ptk)
                i2 = it - 2
                if 0 <= i2 < NB_:
                    ksrc, kk_, mk, bia, bk, vsrc, vkk = blks_[i2]
                    pt, ptk = pt_l.pop(i2)
                    mm(po[0:65, :], vsrc, pt[:, :], i2 == 0, i2 == NB_ - 1, [vkk, ptk], [pok])
            vcopy(Osb[:, :], po[0:65, :], [pok], ["Osb"])
            vrecip(Osb[64:65, :], Osb[64:65, :], ["Osb"], ["Osb"])
            pbc, pbck = nextF()
            mm(pbc[0:65, :], ones_f[64:65, 0:65], Osb[64:65, :], True, True, ["ones_f", "Osb"], [pbck])
            vtt(OnT[:, h, :], Osb[:, :], pbc[0:65, :], ALU.mult, ["Osb", pbck], [("OnT", h)])
        if STAGE < 3:
            continue
        if len(sts) > 1 and not os.environ.get("MK_NOSAMPLE"):
            XS = [("xnT", 512)]
            xsT = lambda k: xnT[:, k, 512:544]
            w1, w1k = wload_cols("w_in", 0, 512)
            w1b, w1bk = wload_cols("w_in", 512, 1024)
            for j in range(4):
                pq, pqk = nextF()
                for k in range(8):
                    mm(pq[:, 0:32], w1[:, k, j * 128:(j + 1) * 128], xsT(k), k == 0, k == 7, [w1k] + XS, [pqk])
                vts(qTs[:, j, :], pq[:, 0:32], 0.125, None, ALU.mult, ALU.bypass, [pqk], ["qTs"])
                pk_, pkk = nextF()
                for k in range(8):
                    mm(pk_[:, 0:32], w1b[:, k, j * 128:(j + 1) * 128], xsT(k), k == 0, k == 7, [w1bk] + XS, [pkk])
                vcopy(kTs[:, j, :], pk_[:, 0:32], [pkk], ["kTs"])
            w2, w2k = wload_cols("w_in", 1024, 1536)
            w2f, w2fk = wload_cols("w_in", 1536, 1544)
            pv, pvk = nextF()
            for k in range(8):
                mm(pv[0:32, :], xsT(k), w2[:, k, 0:512], k == 0, k == 7, [w2k] + XS, [pvk])
            vcopy(Vs_bf[:, :], pv[0:32, :], [pvk], ["Vs_bf"])
            pf, pfk = nextF()
            for k in range(8):
                mm(pf[0:32, 0:8], xsT(k), w2f[:, k, 0:8], k == 0, k == 7, [w2fk] + XS, [pfk])
            vtt(lfs[:, :], pf[0:32, 0:8], bfor[0:32, :], ALU.add, [pfk, "bfor"], ["lfs"])
            act(lfs[:, :], lfs[:, :], AF.Exp, ["lfs"], ["lfs"], scale=-1.0)
            act(lfs[:, :], lfs[:, :], AF.Ln, ["lfs"], ["lfs"], bias=1.0)
            pc, pck = nextF()
            mm(pc[0:8, 0:32], lfs[:, :], gmask_s[0:32, 0:32], True, True, ["lfs", "gmask_s"], [pck])
            vcopy(ncnT[:, :], pc[0:8, 0:32], [pck], ["ncnT"])
            w5, w5k = wload_cols("w_in", 2568, 3080)
            w5a, w5ak = wload_cols("w_in", 3080, 3096)
            pg, pgk = nextF()
            for k in range(8):
                mm(pg[0:32, :], xsT(k), w5[:, k, 0:512], k == 0, k == 7, [w5k] + XS, [pgk])
            act(gate_tok[0:32, 0, :], pg[0:32, :], AF.Silu, [pgk], [("gate", 0)])
            pa, pak = nextF()
            for k in range(8):
                mm(pa[0:16, 0:32], w5a[:, k, 0:16], xsT(k), k == 0, k == 7, [w5ak] + XS, [pak])
            vcopy(gaT[:, 0:32], pa[0:16, 0:32], [pak], ["gaT"])
            w3, w3k = wload_cols("w_in", 1544, 2056)
            for h in range(4):
                px, pxk = nextF()
                mm(px[0:64, 0:32], wa2[0:16, h * 64:(h + 1) * 64], gaT[0:16, 0:32], True, True, ["wa2", "gaT"], [pxk])
                act(sp[:, 0:32], px[0:64, 0:32], AF.Exp, [pxk, "nbgate"], ["sp"], scale=-1.0, bias=nbgate[:, h:h + 1])
                act(sp[:, 0:32], sp[:, 0:32], AF.Ln, ["sp"], ["sp"], bias=1.0)
                S.add("dve", (lambda e: e.tensor_tensor_scan(out=cs[:, 0:32], data0=rmask[:, 512:544], data1=sp[:, 0:32], initial=0.0,
                                                              op0=ALU.mult, op1=ALU.add)), ["rmask", "sp"], ["cs"])
                act(eb[:, 0:32], cs[:, 0:32], AF.Exp, ["cs"], ["eb"], scale=-1.0 / 16)
                act(enb[:, 0:32], cs[:, 0:32], AF.Exp, ["cs"], ["enb"], scale=1.0 / 16)
                vcopy(ebl[:, h, :], eb[:, 0:32].rearrange("p (c t) -> p c t", c=4)[:, :, 7], ["eb"], [("ebl", h)])
                pq, pqk = nextF()
                for k in range(8):
                    mm(pq[0:64, 0:32], w3[:, k, h * 64:(h + 1) * 64], xsT(k), k == 0, k == 7, [w3k] + XS, [pqk])
                vstt(gqT[:, h, 0:32], pq[0:64, 0:32], 0.125, eb[:, 0:32], ALU.mult, ALU.mult, [pqk, "eb"], [("gqT", h)])
                pk_, pkk = nextF()
                for k in range(8):
                    mm(pk_[0:64, 0:32], w3[:, k, 256 + h * 64:256 + (h + 1) * 64], xsT(k), k == 0, k == 7, [w3k] + XS, [pkk])
                vtt(gkT[:, h, 0:32], pk_[0:64, 0:32], enb[:, 0:32], ALU.mult, [pkk, "enb"], [("gkT", h)])
                for c in range(4):
                    vts(kdT[:, h, c * 8:(c + 1) * 8], gkT[:, h, c * 8:(c + 1) * 8], ebl[:, h, c:c + 1], None,
                        ALU.mult, ALU.bypass, [("gkT", h), ("ebl", h)], [("kdT", h)])
            pb, pbk = nextB()
            for h in range(4):
                tp(pb[0:32, h * 64:(h + 1) * 64], kdT[:, h, 0:32], ident_b[0:64, 0:64], [("kdT", h), "ident_b"], [pbk])
            vcopy(kd_tok[0:32, 0, :, :], pb[0:32, 0:256].rearrange("p (h k) -> p h k", h=4), [pbk], [("kd_tok", 0)])
            w4, w4k = wload_cols("w_in", 2056, 2568)
            pv, pvk = nextF()
            for k in range(8):
                mm(pv[0:32, :], xsT(k), w4[:, k, :], k == 0, k == 7, [w4k] + XS, [pvk])
            vcopy(vg_tok[0:32, 0, :], pv[0:32, :], [pvk], [("vg_tok", 0)])

            idx_i = otmp[:, :].bitcast(I32)
            triS = sg[:, 0:128]
            bmask = tokC[0:64, 0:512]
            KTg = hT
            HK = [("hT", c) for c in range(4)]
            dma("sync", idx_i, ptab_d, (), ["otmp"], "idx_ld")
            vts(idx_i, idx_i, 128.0, pcol[:, 0:1], ALU.mult, ALU.add, ["otmp", "pcol"], ["otmp"])
            vtt(triS, ones_f[:, :], tri_f[:, :], ALU.subtract, ["ones_f", "tri_f"], ["sg"])
            dma("sync", bmask, bmask_d, (), ["tokC"], "bmask_ld")
            idx4 = sg[:, 256:384].bitcast(I32)
            triS4 = sg[:, 384:512].rearrange("p (r n) -> p r n", r=4)
            dma("sync", idx4, ptab4_d, (), ["sg"], "idx4_ld")
            dma("sync", sg[:, 384:512], triS4_d, (), ["sg"], "tri4_ld")
            vts(idx4, idx4, 32.0, pcol[:, 2:3], ALU.mult, ALU.add, ["sg", "pcol"], ["sg"])
            ck4 = cache_k.rearrange("(r f) n -> r (f n)", f=4)
            cv4 = cache_v.rearrange("(r f) n -> r (f n)", f=4)

            def gather4(out_ap, src, col, writes, key):
                S.add("pool", lambda e: e.indirect_dma_start(out=out_ap, out_offset=None, in_=src,
                                                             in_offset=bass.IndirectOffsetOnAxis(ap=idx4[:, col:col + 1], axis=0)),
                      ["sg"], writes, dma=key)
            ck_rows, cv_rows, clf_rows = cache_k, cache_v, cache_lf

            def gather(out_ap, src, col, reads, writes, key):
                S.add("pool", lambda e: e.indirect_dma_start(out=out_ap, out_offset=None, in_=src,
                                                             in_offset=bass.IndirectOffsetOnAxis(ap=idx_i[:, col:col + 1], axis=0)),
                      ["otmp"] + list(reads), writes, dma=key)

            for bb in range(4):
                memset(Qbd[:], 0.0, ["Qbd"])
                for j in range(4):
                    vcopy(Qbd[0:64, j, (2 * j) * 8:(2 * j) * 8 + 8], qTs[0:64, j, bb * 8:(bb + 1) * 8], ["qTs"], ["Qbd"])
                    vcopy(Qbd[64:128, j, (2 * j + 1) * 8:(2 * j + 1) * 8 + 8], qTs[64:128, j, bb * 8:(bb + 1) * 8], ["qTs"], ["Qbd"])
                lb = 2 * (bb % 2)
                lfp = yacc[:, lb, :].rearrange("p (g h) -> p g h", h=8)
                prf = yacc[:, lb + 1, :].rearrange("p (g h) -> p g h", h=8)
                YK = lambda j, lb=lb: [("yacc", lb + j, 0), ("yacc", lb + j, 1)]
                nlb = 2 * ((bb + 1) % 2)
                lfp_n = yacc[:, nlb, :].rearrange("p (g h) -> p g h", h=8)
                YKn = [("yacc", nlb, 0), ("yacc", nlb, 1)]
                if bb == 0:
                    for pg_ in range(128):
                        gather(lfp[:, pg_, :], clf_rows, pg_, (), YK(0), ("lfp", pg_ % 8))
                for h in range(8):
                    S.add("dve", (lambda e, h=h, prf=prf, lfp=lfp: e.tensor_tensor_scan(out=prf[:, :, h], data0=ones_f[:, :], data1=lfp[:, :, h], initial=0.0,
                                                                       op0=ALU.mult, op1=ALU.add)), ["ones_f"] + YK(0), YK(1))
                for h in range(8):
                    vts(prf[:, :, h], prf[:, :, h], prf[:, 127, h:h + 1], -1.0, ALU.subtract, ALU.mult, YK(1), YK(1))
                pacc, pacck = nextA()
                for g in range(32):
                    ks_, kk = nextW()
                    kp = ks_[:, 0:2048].rearrange("p (a n) -> p a n", a=4)
                    vs2, vk2 = nextW()
                    vp = vs2[:, 0:2048].rearrange("p (a n) -> p a n", a=4)
                    gather4(ks_[:, 0:2048], ck4, bb * 32 + g, [kk], (kk, 0))
                    gather4(vs2[:, 0:2048], cv4, bb * 32 + g, [vk2], (vk2, 0))
                    if bb < 3:
                        for a in range(4):
                            gather(lfp_n[:, g * 4 + a, :], clf_rows, (bb + 1) * 128 + g * 4 + a, (), YKn, ("lfp", (g * 4 + a) % 8))
                    par = g % 2
                    KTg_, HK_ = (KTg, HK) if par == 0 else (xnT[:, 0:4, 0:512], [("xnT", 0), ("xnT", 128), ("xnT", 256), ("xnT", 384)] * 1)
                    afT_, afk = (gaT[0:8, :], "gaT") if par == 0 else (gate_tok[0:8, 1, :], ("gate", 1))
                    Pg_, pgk_ = (Am[0:64, :], "Am") if par == 0 else (tokb[0:64, 0:512], "tokb")
                    PT_, ptk_ = (PT[:, :, :], "PT") if par == 0 else (tokb[:, 512:768].rearrange("p (a n) -> p a n", a=4), "tokb")
                    pa, pak = nextF()
                    for r4 in range(4):
                        for pq in range(4):
                            pg_ = g * 4 + pq
                            oc = r4 * 128 + pq * 32
                            mm(pa[0:8, oc:oc + 32], lfp[:, pg_, :], triS4[:, r4, :], True, False, YK(0) + ["sg"], [pak])
                            mm(pa[0:8, oc:oc + 32], prf[:, pg_, :], ones_f[:, 0:32], False, True, YK(1) + ["ones_f"], [pak])
                    vcopy(afT_, pa[0:8, :], [pak], [afk])
                    for jj in range(2):
                        pb, pbk = nextB()
                        for j2 in range(2):
                            j = jj * 2 + j2
                            for a in range(4):
                                tp(pb[:, j2 * 512 + a * 128:j2 * 512 + (a + 1) * 128], kp[:, a, j * 128:(j + 1) * 128], ident_b[:, :],
                                   [kk, "ident_b"], [pbk])
                        vcopy(KTg_[:, jj * 2:jj * 2 + 2, :], pb[:, :].rearrange("p (j n) -> p j n", j=2), [pbk],
                              HK_[jj * 2:jj * 2 + 2] if par == 0 else HK_)
                    pS, pSk = nextF()
                    for j in range(4):
                        mm(pS[0:64, :], Qbd[:, j, :], KTg_[:, j, :], j == 0, False, ["Qbd"] + ([HK_[j]] if par == 0 else HK_), [pSk])
                    mm(pS[0:64, :], esel[0:8, :], afT_, False, True, ["esel", afk], [pSk])
                    act(Pg_, pS[0:64, :], AF.Exp, [pSk], [pgk_, ("dsum", g)], accum_out=dsum[:, g:g + 1])
                    pb, pbk = nextB()
                    for a in range(4):
                        tp(pb[:, a * 64:(a + 1) * 64], Pg_[:, a * 128:(a + 1) * 128], ident_b[0:64, 0:64], [pgk_, "ident_b"], [pbk])
                    vcopy(PT_, pb[:, 0:256].rearrange("p (a n) -> p a n", a=4), [pbk], [ptk_])
                    for a in range(4):
                        mm(pacc[0:64, :], PT_[:, a, :], vp[:, a, :], g == 0 and a == 0, False, [ptk_, vk2], [pacck])
                pS, pSk = nextF()
                for j in range(4):
                    mm(pS[0:64, 0:32], Qbd[:, j, :], kTs[:, j, :], j == 0, False, ["Qbd", "kTs"], [pSk])
                mm(pS[0:64, 0:32], esel[0:8, :], ncnT[0:8, :], False, False, ["esel", "ncnT"], [pSk])
                mm(pS[0:64, 0:32], ident_b[0:64, 0:64], smask[:, bb * 32:(bb + 1) * 32], False, True, ["ident_b", "smask"], [pSk])
                act(Am[0:64, 0:32], pS[0:64, 0:32], AF.Exp, [pSk], ["Am", ("dsum", 32)], accum_out=dsum[:, 32:33])
                pb, pbk = nextB()
                tp(pb[0:32, 0:64], Am[0:64, 0:32], ident_b[0:64, 0:64], ["Am", "ident_b"], [pbk])
                vcopy(PT[0:32, 0, :], pb[0:32, 0:64], [pbk], ["PT"])
                mm(pacc[0:64, :], PT[0:32, 0, :], Vs_bf[0:32, :], False, True, ["PT", "Vs_bf"], [pacck])
                DK = [("dsum", g) for g in range(33)]
                S.add("dve", lambda e: e.reduce_sum(out=dsum[:, 39:40], in_=dsum[:, 0:33], axis=mybir.AxisListType.X), DK, [("dsum", 39)])
                vrecip(dsum[:, 39:40], dsum[:, 39:40], [("dsum", 39)], [("dsum", 39)])
                vstt(mix_tok[0:64, :], pacc[0:64, :], dsum[:, 39:40], bmask, ALU.mult, ALU.mult, [pacck, ("dsum", 39), "tokC"], ["mix_tok"])
                po2, po2k = nextF()
                for h in range(8):
                    mm(po2[0:64, h * 8:(h + 1) * 8], mix_tok[0:64, h * 64:(h + 1) * 64], qsel[:, :], True, True, ["mix_tok", "qsel"], [po2k])
                vcopy(OnT_s[0:64, :, bb * 8:(bb + 1) * 8], po2[0:64, 0:64].rearrange("p (h q) -> p h q", h=8), [po2k], [("OnT_s", bb)])

            Vflat = Vst[:, :, :, :].rearrange("p a b c -> p (a b c)")
            Ssb = Vflat[0:64, 0:2048].rearrange("p (b h v) -> p b h v", b=4, h=4)
            VK = [("Vst", j) for j in range(4)]
            Ss1 = yacc[0:64, 2, 512:1024].rearrange("p (h v) -> p h v", h=4)
            for bb in range(4):
                dma("sync", Ss1, sgla_d[bb].rearrange("h k v -> k h v"), (), [("yacc", 2, 1)], "Ss1_ld")
                vcopy(Ssb[:, bb, :, :], Ss1, [("yacc", 2, 1)], VK)
                for h in range(4):
                    vtt(gq_m[:, bb * 4 + h, :], gqT[:, h, 0:32], colm[:, bb * 32:(bb + 1) * 32], ALU.mult, [("gqT", h), "colm"], ["gq_m"])
            pA, pAk = nextF()
            for h in range(4):
                mm(pA[0:32, h * 32:(h + 1) * 32], gkT[:, h, 0:32], gqT[:, h, 0:32], True, True, [("gkT", h), ("gqT", h)], [pAk])
            vtt(Am[0:32, 0:128], pA[0:32, 0:128], gmask_s[:, :], ALU.mult, [pAk, "gmask_s"], ["Am"])
            po, pok = nextF()
            for h in range(4):
                mm(po[0:32, h * 128:(h + 1) * 128], Am[0:32, h * 32:(h + 1) * 32], vg_tok[0:32, 0, h * 128:(h + 1) * 128], True, False,
                   ["Am", ("vg_tok", 0)], [pok])
                for bb in range(4):
                    mm(po[0:32, h * 128:(h + 1) * 128], gq_m[:, bb * 4 + h, :], Ssb[:, bb, h, :], False, bb == 3, ["gq_m"] + VK, [pok])
            for h in range(4):
                act(tokC[0:32, 512 + h * 128:512 + (h + 1) * 128], po[0:32, h * 128:(h + 1) * 128], AF.Square, [pok], ["tokC", ("ss", 4 + h)],
                    accum_out=ss[0:32, 4 + h:5 + h])
            SK = [("ss", 4 + h) for h in range(4)]
            act(ss[0:32, 4:8], ss[0:32, 4:8], AF.Sqrt, SK, SK, scale=1.0 / 128, bias=1e-6)
            vrecip(ss[0:32, 4:8], ss[0:32, 4:8], SK, SK)
            for h in range(4):
                vstt(otmp[0:32, h * 128:(h + 1) * 128], po[0:32, h * 128:(h + 1) * 128], ss[0:32, 4 + h:5 + h], ggla[0:32, h * 128:(h + 1) * 128],
                     ALU.mult, ALU.mult, [pok, ("ss", 4 + h), "ggla"], ["otmp"])
            vtt(mix_tok[0:32, :], otmp[0:32, :], gate_tok[0:32, 0, :], ALU.mult, ["otmp", ("gate", 0)], ["mix_tok"])
            pb, pbk = nextB()
            for q in range(4):
                tp(pb[:, q * 32:(q + 1) * 32], mix_tok[0:32, q * 128:(q + 1) * 128], ident_b[0:32, 0:32], ["mix_tok", "ident_b"], [pbk])
            vcopy(mgT_s[:, :, :], pb[:, 0:128].rearrange("p (q t) -> p q t", q=4), [pbk], ["mgT_s"])
            for bb in range(4):
                vts(kd_tok[0:32, 1, :, :], kd_tok[0:32, 0, :, :], rowm[:, bb:bb + 1], None, ALU.mult, ALU.bypass,
                    [("kd_tok", 0), "rowm"], [("kd_tok", 1)])
                pS, pSk = nextF()
                for h in range(4):
                    mm(pS[0:64, h * 128:(h + 1) * 128], kd_tok[0:32, 1, h, :], vg_tok[0:32, 0, h * 128:(h + 1) * 128], True, True,
                       [("kd_tok", 1), ("vg_tok", 0)], [pSk])
                dma("sync", Ss1, sgla_d[bb].rearrange("h k v -> k h v"), (), [("yacc", 2, 1)], "Ss1_ld")
                for h in range(4):
                    vstt(Ss1[:, h, :], Ss1[:, h, :], ebl[:, h, bb:bb + 1], pS[0:64, h * 128:(h + 1) * 128], ALU.mult, ALU.add,
                         [("yacc", 2, 1), ("ebl", h), pSk], [("yacc", 2, 1)])
                dma("sync", gla_s[bb].rearrange("k (h v) -> k h v", h=4), Ss1, [("yacc", 2, 1)], (), "Ss1_st")
        wo = []
        for half in range(2):
            ws, wk = nextW()
            v = ws[:, 0:4096].rearrange("p (h n) -> p h n", h=8)
            memset(v[64:65, :, :], 0.0, [wk])
            dma("sync", v[0:64, :, :], WB["w_out"][0:512, half * 512:(half + 1) * 512].rearrange("(h d) n -> d h n", d=64), [("Wb", "w_out")], [wk], wk)
            ws2, wk2 = nextW()
            v2 = ws2[:, 0:2048].rearrange("p (c n) -> p c n", c=4)
            dma("sync", v2, WB["w_out"][512:1024, half * 512:(half + 1) * 512].rearrange("(c p) n -> p c n", p=128), [("Wb", "w_out")], [wk2], wk2)
            wo.append((v, wk, v2, wk2))
        for b in st["blks"]:
            j = b["kb"]
            for half in range(2):
                v, wk, v2, wk2 = wo[half]
                pd, pdk = nextF()
                for h in range(8):
                    mm(pd[:, :], OnT[0:65, h, j * 128:(j + 1) * 128], v[0:65, h, :], h == 0, False, [("OnT", h), wk], [pdk])
                for c in range(4):
                    mm(pd[:, :], mgT[:, c, j * 128:(j + 1) * 128], v2[:, c, :], False, c == 3, [("mgT", j), wk2], [pdk])
                vcopy(yacc[:, j, half * 512:(half + 1) * 512], pd[:, :], [pdk], [("yacc", j, half)])
            post_res(b, "gmpost", 1.0, hres[:, j, :], ("hres", j))
            norm_to_T(hres[:, j, :], 128, [("hres", j)], "g2pre", b["col0"])
        for b in groups[gi][4:]:
            for half in range(2):
                v, wk, v2, wk2 = wo[half]
                pd, pdk = nextF()
                for h in range(8):
                    mm(pd[0:32, :], OnT_s[0:65, h, :], v[0:65, h, :], h == 0, False, [("OnT_s", q) for q in range(4)] + [wk], [pdk])
                for c in range(4):
                    mm(pd[0:32, :], mgT_s[:, c, :], v2[:, c, :], False, c == 3, ["mgT_s", wk2], [pdk])
                vcopy(yacc[0:32, 4, half * 512:(half + 1) * 512], pd[0:32, :], [pdk], [("yacc", 4, half)])
            post_res(b, "gmpost", 1.0, hres[0:32, 4, :], ("hres", 4))
            norm_to_T(hres[0:b["n"], b["yi"], :], b["n"], [("hres", b["yi"])], "g2pre", b["col0"])
        ffn(gi, "w_gu2", "w_dn2")
        for b in groups[gi]:
            n = b["n"]
            post_res(b, "g2post", 0.5, hres[0:n, b["yi"], :], ("hres", b["yi"]))
            dma("sync", src_rows(b, y_p, y_s), hres[0:n, b["yi"], :], [("hres", b["yi"])], (), ("y_st", b["yi"]))
    S.emit(nc)
    es.close()
    return nc


_NC = None


def _consts():
    c = {}
    c["ident"] = np.eye(128, dtype=np.float32)
    s = np.arange(128)
    c["tri"] = (s[:, None] <= s[None, :]).astype(np.float32)
    c["ones"] = np.ones((128, 128), np.float32)
    rm = np.ones((64, 544), np.float32)
    rm[:, 0:512:128] = 0
    rm[:, 512::8] = 0
    c["rmask"] = rm
    c["gmask"] = np.tile((s[:, None] <= s[None, :]).astype(np.float32), (1, 4))
    t = np.arange(32)
    c["gmask_s"] = np.tile(((t[:, None] <= t[None, :]) & (t[:, None] // 8 == t[None, :] // 8)).astype(np.float32), (1, 4))
    q = np.arange(512)
    cm = np.zeros((128, 4, 512), np.float32)
    for kb in range(4):
        cm[:, kb, :] = np.where(q[None, :] >= kb * 128 + s[:, None], 0.0, NEG)
    c["cmask"] = cm
    c["wsel"] = np.zeros((128, 4), np.float32)
    c["lmask"] = np.zeros((16, 16), np.float32)
    c["pcol"] = np.stack([np.arange(128, dtype=np.float32), (np.arange(128) == 127).astype(np.float32),
                          (np.arange(128) % 32).astype(np.float32)], 1)
    kk_ = np.arange(128)[:, None, None]
    c["triS4"] = (kk_ > 4 * np.arange(32)[None, None, :] + np.arange(4)[None, :, None]).astype(np.float32).reshape(128, 128)
    hq = np.arange(64)
    c["esel"] = (np.arange(8)[:, None] == hq[None, :] // 8).astype(np.float32)
    c["qsel"] = (hq[:, None] % 8 == np.arange(8)[None, :]).astype(np.float32)
    c["bmask"] = (hq[:, None] // 8 == np.arange(512)[None, :] // 64).astype(np.float32)
    c["colm"] = np.broadcast_to((np.arange(4)[:, None] == t[None, :] // 8).astype(np.float32)[None], (64, 4, 32)).copy()
    c["rowm"] = (t[:, None] // 8 == np.arange(4)[None, :]).astype(np.float32)
    sm = np.full((64, 4, 32), NEG, np.float32)
    for b in range(4):
        for qq in range(8):
            sm[qq::8, b, b * 8:b * 8 + qq + 1] = 0.0
    c["smask_new"] = sm
    return c


def kernel(**inp):
    global _NC
    if _NC is None:
        _NC = build()
    f = lambda a: np.ascontiguousarray(np.asarray(a), dtype=np.float32)
    bc = lambda v, n=128: np.ascontiguousarray(np.broadcast_to(f(v).reshape(1, -1), (n, f(v).size)))
    xp_full = f(inp["x_prompt"])
    xs_full = f(inp["x_sample"]).reshape(256, D)
    ck = f(inp["cache_k"]).reshape(-1, 512)
    cv = f(inp["cache_v"]).reshape(-1, 512)
    clf = f(inp["cache_logf"]).reshape(-1, 8)
    pt = np.asarray(inp["page_table"]).astype(np.int32)
    sg = f(inp["state_gla"])[0]
    shared = dict(
        w_gu1=f(inp["ffn1_w_gu"])[0], w_dn1=f(inp["ffn1_w_down"])[0], w_in=f(inp["w_in"])[0], w_out=f(inp["w_out"])[0],
        w_gu2=f(inp["ffn2_w_gu"])[0], w_dn2=f(inp["ffn2_w_down"])[0], w_a2=f(inp["w_gate_up"])[0],
        g1pre=bc(inp["ffn1_norm_pre"]), g1post=bc(inp["ffn1_norm_post"]), gmpre=bc(inp["mix_norm_pre"]),
        gmpost=bc(inp["mix_norm_post"]), g2pre=bc(inp["ffn2_norm_pre"]), g2post=bc(inp["ffn2_norm_post"]),
        bfor=bc(inp["b_forget"]), bgate=np.ascontiguousarray(f(inp["b_gate"]).reshape(4, 64).T),
        ggla=np.ascontiguousarray(np.tile(bc(inp["gla_norm"]), (1, 4))),
        cache_k=ck, cache_v=cv, cache_lf=clf)
    shared.update(_consts())
    in_maps = []
    for c in range(8):
        m = dict(shared)
        g = c % 4
        m["xp"] = np.ascontiguousarray(xp_full[c // 4].reshape(4, 4, 512, D)[:, g].reshape(2048, D))
        ws = np.zeros((128, 4), np.float32); ws[:, g] = 1.0
        m["wsel"] = ws
        m["mrow"] = np.repeat((np.arange(4) >= g).astype(np.float32), 512)[None, :].copy()
        m["xs"] = np.ascontiguousarray(xs_full[c * 32:(c + 1) * 32])
        m["ptab"] = np.ascontiguousarray(np.broadcast_to(pt[c * 4:(c + 1) * 4].reshape(1, 512), (128, 512)))
        m["sgla"] = np.ascontiguousarray(sg[c * 4:(c + 1) * 4])
        p4 = pt[c * 4:(c + 1) * 4].reshape(4, 32, 4)
        m["ptab4"] = np.ascontiguousarray(np.transpose(p4, (2, 0, 1))[np.arange(128) // 32].reshape(128, 128))
        in_maps.append(m)
    res = run_bass_kernel_spmd(_NC, in_maps, core_ids=list(range(8))).results
    r = lambda c, k: np.asarray(res[c][k], dtype=np.float32)
    def gath(key, w):
        o = np.zeros((2, 4, 4, 512, w), np.float32)
        for c in range(8):
            o[c // 4, :, c % 4] = r(c, key).reshape(4, 512, w)
        return o.reshape(2, 8192, w)
    y_prompt = gath("y_p", D)
    y_sample = np.concatenate([r(c, "y_s") for c in range(8)]).reshape(32, 8, D)
    nk_p = gath("nk_p", 512).reshape(1, 2, 8192, 8, 64)
    nv_p = gath("nv_p", 512).reshape(1, 2, 8192, 8, 64)
    nlf_p = gath("nlf_p", 8).reshape(1, 2, 8192, 8)
    gla_p = np.stack([r(0, "gla_p"), r(4, "gla_p")]).reshape(2, 64, 4, 128).transpose(0, 2, 1, 3).reshape(1, 2, 4, 64, 128)
    nk_s = np.concatenate([r(c, "nk_s") for c in range(8)]).reshape(1, 32, 8, 8, 64)
    nv_s = np.concatenate([r(c, "nv_s") for c in range(8)]).reshape(1, 32, 8, 8, 64)
    nlf_s = np.concatenate([r(c, "nlf_s") for c in range(8)]).reshape(1, 32, 8, 8)
    gla_s = np.concatenate([r(c, "gla_s") for c in range(8)]).reshape(32, 64, 4, 128).transpose(0, 2, 1, 3).reshape(1, 32, 4, 64, 128)
    return (y_prompt, y_sample, nk_p, nv_p, nlf_p, np.ascontiguousarray(gla_p), nk_s, nv_s, nlf_s, np.ascontiguousarray(gla_s))
```
